# Optimizing a Trainium2 kernel written in Bass

```python
import math
import jax, jax.numpy as jnp
from jax import lax
import numpy as np

D_MODEL = 2048
BATCH = 2
SEQ = 4096
DEPTH = 4

HEAD_V = 128
N_HEADS_TOTAL = D_MODEL // HEAD_V
DIFF_HEADS = N_HEADS_TOTAL // 2
GLA_HEADS = N_HEADS_TOTAL // 4
RET_HEADS = N_HEADS_TOTAL - DIFF_HEADS - GLA_HEADS
DIFF_QK = HEAD_V // 2
GLA_DK = HEAD_V // 2
GLA_DV = HEAD_V
GLA_GATE_RANK = 16
GLA_TAU = 16.0
GLA_CHUNK = 32
RET_DK = HEAD_V // 2
RET_DV = HEAD_V
RET_CHUNK = 64
ROPE_BASE = 10000.0
N_BUCKETS = 32
MAX_DISTANCE = 128
Q_BLOCK = 128
D_FF = ((8 * D_MODEL // 3 + 255) // 256) * 256
CONV_W = 3
EPS = 1e-6

DIFF_W = DIFF_HEADS * HEAD_V
GLA_W = GLA_HEADS * GLA_DV
RET_W = RET_HEADS * RET_DV
MIX_W = DIFF_W + GLA_W + RET_W
IN_SPLITS = (DIFF_HEADS * 2 * DIFF_QK, DIFF_HEADS * 2 * DIFF_QK, DIFF_W,
             GLA_HEADS * GLA_DK, GLA_HEADS * GLA_DK, GLA_W, GLA_W, GLA_GATE_RANK, GLA_GATE_RANK,
             RET_HEADS * RET_DK, RET_HEADS * RET_DK, RET_W, RET_W)
IN_W = sum(IN_SPLITS)

kernel_name = 'hybrid_parallel_head_encoder'

F32 = jnp.float32


def rmsnorm(x, g):
    xf = x.astype(F32)
    y = xf * lax.rsqrt(jnp.mean(xf * xf, axis=-1, keepdims=True) + EPS)
    return (y * g.astype(F32)).astype(x.dtype)


def split_heads(t, n):
    b, s, _ = t.shape
    return t.reshape(b, s, n, -1).transpose(0, 2, 1, 3)


def head_rmsnorm(o, g):
    of = o.astype(F32)
    of = of * lax.rsqrt(jnp.mean(of * of, axis=-1, keepdims=True) + EPS)
    b, h, t, d = o.shape
    return (of.transpose(0, 2, 1, 3).reshape(b, t, h * d) * g.astype(F32)).astype(o.dtype)


def t5_bucket(rel):
    half = N_BUCKETS // 2
    max_exact = half // 2
    ret = jnp.where(rel > 0, half, 0)
    n = jnp.abs(rel)
    nf = jnp.maximum(n, 1).astype(F32)
    large = max_exact + (jnp.log(nf / max_exact) / math.log(MAX_DISTANCE / max_exact)
                         * (half - max_exact)).astype(jnp.int32)
    large = jnp.minimum(large, half - 1)
    return ret + jnp.where(n < max_exact, n, large)


def rotary(t, positions):
    half = t.shape[-1] // 2
    inv = 1.0 / (ROPE_BASE ** jnp.linspace(0.0, 1.0, half, dtype=F32))
    ang = positions.astype(F32)[:, None] * inv[None, :]
    cos, sin = jnp.cos(ang), jnp.sin(ang)
    t1, t2 = t[..., :half].astype(F32), t[..., half:].astype(F32)
    return jnp.concatenate([t1 * cos - t2 * sin, t1 * sin + t2 * cos], axis=-1).astype(t.dtype)


def diff_attention(q, k, v, lam, bias_rel):
    b, h, t, _ = q.shape
    nb = t // Q_BLOCK
    scale = DIFF_QK ** -0.5
    k1, k2 = k[..., :DIFF_QK], k[..., DIFF_QK:]
    q_blocks = q.reshape(b, h, nb, Q_BLOCK, -1).transpose(2, 0, 1, 3, 4)
    starts = jnp.arange(nb, dtype=jnp.int32) * Q_BLOCK
    key_pos = jnp.arange(t, dtype=jnp.int32)

    def block(args):
        qb, s = args
        q_pos = s + jnp.arange(Q_BLOCK, dtype=jnp.int32)
        bias = bias_rel[:, key_pos[None, :] - q_pos[:, None] + t - 1]
        s1 = jnp.einsum('bhqd,bhkd->bhqk', qb[..., :DIFF_QK], k1).astype(F32) * scale + bias
        s2 = jnp.einsum('bhqd,bhkd->bhqk', qb[..., DIFF_QK:], k2).astype(F32) * scale + bias
        p = jax.nn.softmax(s1, axis=-1) - lam * jax.nn.softmax(s2, axis=-1)
        return jnp.einsum('bhqk,bhkd->bhqd', p.astype(v.dtype), v)

    out = lax.map(block, (q_blocks, starts))
    return out.transpose(1, 2, 0, 3, 4).reshape(b, h, t, -1)


def chunk_states(k_end, v, chunk_decay):
    kv = jnp.einsum('bhnck,bhncv->nbhkv', k_end, v)
    dec = jnp.moveaxis(chunk_decay, 2, 0)[..., None]

    def step(state, inp):
        kv_n, d_n = inp
        return d_n * state + kv_n, state

    init = jnp.zeros(kv.shape[1:], kv.dtype)
    _, prev = lax.scan(step, init, (kv, dec))
    return jnp.moveaxis(prev, 0, 2)


def gla_direction(q, k, v, log_a):
    b, h, t, dk = q.shape
    dv = v.shape[-1]
    c = GLA_CHUNK
    n = t // c
    q, k = q.reshape(b, h, n, c, dk), k.reshape(b, h, n, c, dk)
    v = v.reshape(b, h, n, c, dv)
    cum = jnp.cumsum(log_a.astype(F32).reshape(b, h, n, c, dk), axis=3)
    last = cum[..., -1:, :]
    lower = jnp.tril(jnp.ones((c, c), dtype=bool))
    pair = jnp.where(lower[:, :, None],
                     jnp.exp(jnp.minimum(cum[..., :, None, :] - cum[..., None, :, :], 0.0)), 0.0)
    scores = jnp.einsum('bhnik,bhnjk,bhnijk->bhnij', q, k, pair.astype(q.dtype))
    intra = jnp.einsum('bhnij,bhnjv->bhniv', scores, v)
    states = chunk_states(k * jnp.exp(last - cum).astype(k.dtype), v,
                          jnp.exp(last[..., 0, :]).astype(k.dtype))
    inter = jnp.einsum('bhnik,bhnkv->bhniv', q * jnp.exp(cum).astype(q.dtype), states)
    return (intra + inter).reshape(b, h, t, dv)


def retention_direction(q, k, v, log_gamma):
    b, h, t, dk = q.shape
    dv = v.shape[-1]
    c = RET_CHUNK
    n = t // c
    q, k = q.reshape(b, h, n, c, dk), k.reshape(b, h, n, c, dk)
    v = v.reshape(b, h, n, c, dv)
    pos = jnp.arange(c, dtype=F32)
    rel = pos[:, None] - pos[None, :]
    lg = log_gamma.astype(F32)
    decay = jnp.where(rel >= 0, jnp.exp(lg[:, None, None] * jnp.maximum(rel, 0.0)), 0.0)
    scores = jnp.einsum('bhnid,bhnjd->bhnij', q, k) * decay[None, :, None].astype(q.dtype)
    intra = jnp.einsum('bhnij,bhnjv->bhniv', scores, v)
    xi = jnp.exp(lg[:, None] * (pos + 1.0))
    zeta = jnp.exp(lg[:, None] * (c - 1.0 - pos))
    chunk_dec = jnp.broadcast_to(jnp.exp(lg * c)[None, :, None, None], (b, h, n, dk))
    states = chunk_states(k * zeta[None, :, None, :, None].astype(k.dtype), v, chunk_dec.astype(k.dtype))
    inter = jnp.einsum('bhnid,bhndv->bhniv', q, states) * xi[None, :, None, :, None].astype(q.dtype)
    return (intra + inter).reshape(b, h, t, dv)


def flip_t(z):
    return jnp.flip(z, axis=2)


def hybrid_mixer(h, w_in, lam_params, lam_init, bias_rel, gate_w, gate_b,
                 decay_logit, head_gain, w_o):
    t = h.shape[1]
    proj = h @ w_in
    split_idx = [int(i) for i in np.cumsum(IN_SPLITS)[:-1]]
    (dq, dk, dv, gq, gk, gv, gr, ga_f, ga_b, rq, rk, rv, rg) = jnp.split(proj, split_idx, axis=-1)
    g_diff = head_gain[:DIFF_W]
    g_gla = head_gain[DIFF_W:DIFF_W + GLA_W]
    g_ret = head_gain[DIFF_W + GLA_W:]

    lp = lam_params.astype(F32)
    lam = jnp.exp(jnp.sum(lp[0] * lp[1])) - jnp.exp(jnp.sum(lp[2] * lp[3])) + lam_init
    a_out = diff_attention(split_heads(dq, DIFF_HEADS), split_heads(dk, DIFF_HEADS),
                           split_heads(dv, DIFF_HEADS), lam, bias_rel)
    a_out = head_rmsnorm(a_out, g_diff) * (1.0 - lam_init)

    q = split_heads(gq, GLA_HEADS) * (GLA_DK ** -0.5)
    k = split_heads(gk, GLA_HEADS)
    v = split_heads(gv, GLA_HEADS)
    la_f = split_heads(jax.nn.log_sigmoid((ga_f @ gate_w[0] + gate_b[0]).astype(F32)) / GLA_TAU, GLA_HEADS)
    la_b = split_heads(jax.nn.log_sigmoid((ga_b @ gate_w[1] + gate_b[1]).astype(F32)) / GLA_TAU, GLA_HEADS)
    o = gla_direction(q, k, v, la_f) + flip_t(gla_direction(flip_t(q), flip_t(k), flip_t(v), flip_t(la_b)))
    b_out = head_rmsnorm(o, g_gla) * jax.nn.silu(gr)

    pos = jnp.arange(t, dtype=jnp.int32)
    q = rotary(split_heads(rq, RET_HEADS), pos)
    k = rotary(split_heads(rk, RET_HEADS), pos) * (RET_DK ** -0.5)
    v = split_heads(rv, RET_HEADS)
    log_gamma = jax.nn.log_sigmoid(decay_logit.astype(F32))
    o = retention_direction(q, k, v, log_gamma[0]) + flip_t(
        retention_direction(flip_t(q), flip_t(k), flip_t(v), log_gamma[1]))
    c_out = head_rmsnorm(o, g_ret) * jax.nn.silu(rg)

    merged = jnp.concatenate([a_out, b_out, c_out], axis=-1).astype(h.dtype)
    return merged @ w_o


def conv_ffn(h, w_up, conv_w, conv_b, w_down):
    t = h.shape[1]
    u = h @ w_up
    pad = CONV_W // 2
    up = jnp.pad(u, ((0, 0), (pad, pad), (0, 0)))
    u = sum(up[:, i:i + t] * conv_w[i] for i in range(CONV_W)) + conv_b
    a, g = jnp.split(u, 2, axis=-1)
    return (a * jax.nn.silu(g)) @ w_down


def setup_inputs(seed: int = 0) -> dict:
    key = jax.random.key(seed)
    ks = jax.random.split(key, 16)
    nrm = jax.random.normal
    base_logit = jnp.log(2.0 ** (5.0 + jnp.arange(RET_HEADS, dtype=F32)) - 1.0)
    return {
        'x': nrm(ks[0], (BATCH, SEQ, D_MODEL), F32),
        'ln1_g': 1.0 + 0.02 * nrm(ks[1], (DEPTH, D_MODEL), F32),
        'w_in': nrm(ks[2], (DEPTH, D_MODEL, IN_W), F32) * D_MODEL ** -0.5,
        'diff_lambda': 0.1 * nrm(ks[3], (DEPTH, 4, DIFF_QK), F32),
        'rel_bias': 0.5 * nrm(ks[4], (N_BUCKETS, DIFF_HEADS), F32),
        'gla_gate_w': nrm(ks[5], (DEPTH, 2, GLA_GATE_RANK, GLA_HEADS * GLA_DK), F32) * GLA_GATE_RANK ** -0.5,
        'gla_gate_b': 0.1 * nrm(ks[6], (DEPTH, 2, GLA_HEADS * GLA_DK), F32),
        'ret_decay_logit': base_logit + 0.05 * nrm(ks[7], (DEPTH, 2, RET_HEADS), F32),
        'head_gain': 1.0 + 0.02 * nrm(ks[8], (DEPTH, MIX_W), F32),
        'w_o': nrm(ks[9], (DEPTH, MIX_W, D_MODEL), F32) * MIX_W ** -0.5,
        'ln2_g': 1.0 + 0.02 * nrm(ks[10], (DEPTH, D_MODEL), F32),
        'w_up': nrm(ks[11], (DEPTH, D_MODEL, 2 * D_FF), F32) * D_MODEL ** -0.5,
        'conv_w': nrm(ks[12], (DEPTH, CONV_W, 2 * D_FF), F32) * CONV_W ** -0.5,
        'conv_b': 0.02 * nrm(ks[13], (DEPTH, 2 * D_FF), F32),
        'w_down': nrm(ks[14], (DEPTH, D_FF, D_MODEL), F32) * D_FF ** -0.5,
        'final_g': 1.0 + 0.02 * nrm(ks[15], (D_MODEL,), F32),
    }


def reference(x, ln1_g, w_in, diff_lambda, rel_bias, gla_gate_w, gla_gate_b, ret_decay_logit,
              head_gain, w_o, ln2_g, w_up, conv_w, conv_b, w_down, final_g):
    t = x.shape[1]
    rel = jnp.arange(-(t - 1), t, dtype=jnp.int32)
    bias_rel = rel_bias.astype(F32)[t5_bucket(rel)].T
    for layer in range(DEPTH):
        lam_init = 0.8 - 0.6 * math.exp(-0.3 * layer)
        h = rmsnorm(x, ln1_g[layer])
        x = x + hybrid_mixer(h, w_in[layer], diff_lambda[layer], lam_init, bias_rel,
                             gla_gate_w[layer], gla_gate_b[layer], ret_decay_logit[layer],
                             head_gain[layer], w_o[layer])
        h = rmsnorm(x, ln2_g[layer])
        x = x + conv_ffn(h, w_up[layer], conv_w[layer], conv_b[layer], w_down[layer])
    return rmsnorm(x, final_g)
```

```python
import contextlib
import math
import numpy as np
import ml_dtypes
import concourse.bass as bass
import concourse.mybir as mybir
from concourse.bass_utils import run_bass_kernel_spmd

F32 = mybir.dt.float32
BF16 = mybir.dt.bfloat16
AF = mybir.ActivationFunctionType
ALU = mybir.AluOpType
NPBF = ml_dtypes.bfloat16

D = 2048
T = 4096
NB = 2
NCORE = 8
NT = 1024
NTH = NT + 2
DEPTH = 4
IN_W = 6176
DFF = 5632
EPS = 1e-6
PT_DQ, PT_DK, PT_GQ, PT_GK, PT_RQ, PT_RK, PT_GA = 0, 1024, 2048, 2304, 2560, 2816, 3072
NFM = 3104
PV_DV, PV_GK, PV_GV, PV_RK, PV_RV = 0, 1024, 1280, 1792, 2048
NTM = 2560
W_EXT = IN_W + 512


class Buf:
    __slots__ = ("name", "last_w", "readers", "dma_sem", "dma_cnt")

    def __init__(self, name=""):
        self.name = name
        self.last_w = None
        self.readers = []
        self.dma_sem = None
        self.dma_cnt = 0


class Sched:
    def __init__(self, nc, stack):
        self.nc = nc
        self.stack = stack
        self.eng = {"pe": nc.tensor, "act": nc.scalar, "dve": nc.vector, "pool": nc.gpsimd, "sp": nc.sync}
        self.sem, self.cnt, self.seen = {}, {}, {}
        for k in self.eng:
            self.sem[k] = stack.enter_context(nc.semaphore("s_" + k))
            self.cnt[k] = 0
            self.seen[k] = {}
        self.nsem = len(self.eng)
        self.all_dma = []
        self.n_inst = 0

    def sbuf(self, name, shape, dt):
        return self.stack.enter_context(self.nc.sbuf_tensor(name, shape, dt))

    def psum(self, name, shape, dt=F32):
        return self.stack.enter_context(self.nc.psum_tensor(name, shape, dt))

    def _wait(self, e, deps):
        best = {}
        for d in deps:
            if d is None:
                continue
            sem, val, en = d
            if e == "pe" and en == "pe":
                continue
            key = id(sem)
            if key not in best or best[key][1] < val:
                best[key] = (sem, val)
        for key, (sem, val) in best.items():
            if self.seen[e].get(key, 0) >= val:
                continue
            self.eng[e].wait_ge(sem, val)
            self.seen[e][key] = val

    @staticmethod
    def _deps(reads, writes):
        deps = []
        for b in reads:
            deps.append(b.last_w)
        for b in writes:
            deps.append(b.last_w)
            deps.extend(b.readers)
        return deps

    @staticmethod
    def _mark(tok, reads, writes):
        for b in writes:
            b.last_w = tok
            b.readers = []
        for b in reads:
            if b not in writes:
                b.readers = [r for r in b.readers if r[2] != tok[2] or tok[2] == "dma"] + [tok]

    def op(self, e, fn, reads=(), writes=()):
        self._wait(e, self._deps(reads, writes))
        ins = fn(self.eng[e])
        self.cnt[e] += 1
        ins.then_inc(self.sem[e], 1)
        self._mark((self.sem[e], self.cnt[e], e), reads, writes)
        self.n_inst += 1
        return ins

    def dma(self, e, out, in_, reads=(), writes=(), owner=None):
        self._wait(e, self._deps(reads, writes))
        if owner is None:
            owner = (list(writes) + list(reads))[0]
        if owner.dma_sem is None:
            owner.dma_sem = self.stack.enter_context(self.nc.semaphore("d_%d" % self.nsem))
            self.nsem += 1
            self.all_dma.append(owner)
        owner.dma_cnt += 16
        ins = self.eng[e].dma_start(out=out, in_=in_)
        ins.then_inc(owner.dma_sem, 16)
        self._mark((owner.dma_sem, owner.dma_cnt, "dma"), reads, writes)
        self.n_inst += 1
        return ins

    def finish(self):
        deps = [(b.dma_sem, b.dma_cnt, "dma") for b in self.all_dma]
        for k in self.eng:
            if k != "sp" and self.cnt[k] > 0:
                deps.append((self.sem[k], self.cnt[k], k))
        self._wait("sp", deps)


def _chunked(ap2d):
    return ap2d.rearrange("(c p) n -> p c n", p=128)


def _rms_rstd(S, nc, x_sb, bx, ncols, pieces, ones, bones, sq, bsq, pss, bpss, rstd, brstd, epst, beps, dscale):
    for ch in range(16):
        s = ch % 2
        S.op("act", lambda e: e.activation(sq[s][:, 0:ncols], x_sb[:, ch, 0:ncols], AF.Square),
             reads=[bx[ch]], writes=[bsq[s]])
        for pi, (c0, c1) in enumerate(pieces):
            S.op("pe", lambda e: e.matmul(pss[pi][:, 0:c1 - c0], ones[:], sq[s][:, c0:c1],
                                          start=(ch == 0), stop=(ch == 15)),
                 reads=[bsq[s], bones], writes=[bpss[pi]])
    for pi, (c0, c1) in enumerate(pieces):
        S.op("act", lambda e: e.activation(rstd[:, c0:c1], pss[pi][:, 0:c1 - c0], AF.Ln, bias=epst[:, 0:1], scale=dscale),
             reads=[bpss[pi], beps], writes=[brstd])
    S.op("act", lambda e: e.activation(rstd[:, 0:ncols], rstd[:, 0:ncols], AF.Exp, scale=-0.5),
         reads=[brstd], writes=[brstd])


def build_A():
    nc = bass.Bass("TRN2", target_bir_lowering=False)
    xT = nc.dram_tensor("xT", [D, NT], F32, kind="ExternalInput").ap()
    g1 = nc.dram_tensor("g1", [128, 16], F32, kind="ExternalInput").ap()
    w = nc.dram_tensor("w", [D, W_EXT], F32, kind="ExternalInput").ap()
    rotf = nc.dram_tensor("rotf", [128, 2, NT], F32, kind="ExternalInput").ap()
    rott = nc.dram_tensor("rott", [128, 2, 8, 256], F32, kind="ExternalInput").ap()
    PT = nc.dram_tensor("PT", [NFM, NT], BF16, kind="ExternalOutput").ap()
    PG = nc.dram_tensor("PG", [1024, NT], F32, kind="ExternalOutput").ap()
    PV = nc.dram_tensor("PV", [NT, NTM], BF16, kind="ExternalOutput").ap()
    with contextlib.ExitStack() as st:
        S = Sched(nc, st)
        x_sb = S.sbuf("x_sb", [128, 16, NT], F32); bx = [Buf() for _ in range(16)]
        h_sb = S.sbuf("h_sb", [128, 16, NT], BF16); bh = [Buf() for _ in range(16)]
        g_sb = S.sbuf("g_sb", [128, 16], F32); bg = Buf()
        rf_sb = S.sbuf("rf_sb", [128, 2, NT], F32); brf = Buf()
        rt_sb = S.sbuf("rt_sb", [128, 2, 8, 256], F32); brt = Buf()
        ones = S.sbuf("ones", [128, 128], BF16); bones = Buf()
        epst = S.sbuf("epst", [128, 1], F32); beps = Buf()
        sq = [S.sbuf("sq%d" % i, [128, NT], BF16) for i in range(2)]; bsq = [Buf(), Buf()]
        rstd = S.sbuf("rstd", [128, NT], F32); brstd = Buf()
        pan = [S.sbuf("pan%d" % i, [128, 16, 512], BF16) for i in range(2)]; bpan = [Buf(), Buf()]
        stg32 = [S.sbuf("stg32_%d" % i, [128, NT], F32) for i in range(2)]; bstg32 = [Buf(), Buf()]
        stg16 = [S.sbuf("stg16_%d" % i, [128, NT], BF16) for i in range(2)]; bstg16 = [Buf(), Buf()]
        stgT = [S.sbuf("stgT%d" % i, [128, 512], BF16) for i in range(3)]; bstgT = [Buf() for _ in range(3)]
        tmpA = S.sbuf("tmpA", [128, 512], F32); btA = Buf()
        tmpB = S.sbuf("tmpB", [128, 512], F32); btB = Buf()
        P = [S.psum("P%d" % i, [128, 512]) for i in range(8)]; bP = [Buf() for _ in range(8)]

        S.op("pool", lambda e: e.memset(ones[:], 1.0), writes=[bones])
        S.op("pool", lambda e: e.memset(epst[:], EPS), writes=[beps])
        S.dma("sp", g_sb[:], g1, writes=[bg])
        for ch in range(16):
            S.dma("sp", x_sb[:, ch, :], xT[ch * 128:(ch + 1) * 128, :], writes=[bx[ch]])
        S.dma("sp", rf_sb[:], rotf, writes=[brf])
        S.dma("sp", rt_sb[:], rott, writes=[brt])

        XQ, XK = IN_W, IN_W + 256
        panels = [
            (0, 512, [(j * 128, 128, PT, PT_DQ + j * 128, 0.125, 'bf') for j in range(4)], []),
            (512, 512, [(j * 128, 128, PT, PT_DQ + 512 + j * 128, 0.125, 'bf') for j in range(4)], []),
            (1024, 512, [(j * 128, 128, PT, PT_DK + j * 128, 1.0, 'bf') for j in range(4)], []),
            (1536, 512, [(j * 128, 128, PT, PT_DK + 512 + j * 128, 1.0, 'bf') for j in range(4)], []),
            (2048, 512, [], [(0, 512, PV_DV, 1.0, None)]),
            (2560, 512, [], [(0, 512, PV_DV + 512, 1.0, None)]),
            (3072, 512, [(0, 128, PT, PT_GQ, 0.125, 'bf'), (128, 128, PT, PT_GQ + 128, 0.125, 'bf'),
                         (256, 128, PT, PT_GK, 1.0, 'bf'), (384, 128, PT, PT_GK + 128, 1.0, 'bf')],
             [(256, 256, PV_GK, 1.0, None)]),
            (3584, 512, [], [(0, 512, PV_GV, 1.0, None)]),
            (4096, 512, [(j * 128, 128, PG, j * 128, 1.0, 'f32') for j in range(4)], []),
            (4608, 32, [(0, 32, PT, PT_GA, 1.0, 'bf')], []),
            ((4640, XQ), 512, [(0, 128, PT, PT_RQ, 1.0, ('rot', 256)), (128, 128, PT, PT_RQ + 128, 1.0, ('rot', 384))], []),
            ((4896, XK), 512, [(0, 128, PT, PT_RK, 0.125, ('rot', 256)), (128, 128, PT, PT_RK + 128, 0.125, ('rot', 384))],
             [(0, 256, PV_RK, 0.125, 256)]),
            (5152, 512, [], [(0, 512, PV_RV, 1.0, None)]),
            (5664, 512, [(j * 128, 128, PG, 512 + j * 128, 1.0, 'f32') for j in range(4)], []),
        ]

        def load_panel(pi):
            c0, ncols, _, _ = panels[pi]
            s = pi % 2
            if isinstance(c0, tuple):
                S.dma("pool", pan[s][:, :, 0:256], _chunked(w[:, c0[0]:c0[0] + 256]), writes=[bpan[s]])
                S.dma("pool", pan[s][:, :, 256:512], _chunked(w[:, c0[1]:c0[1] + 256]), writes=[bpan[s]])
            else:
                S.dma("pool", pan[s][:, :, 0:ncols], _chunked(w[:, c0:c0 + ncols]), writes=[bpan[s]])

        load_panel(0)
        _rms_rstd(S, nc, x_sb, bx, NT, [(0, 512), (512, 1024)], ones, bones, sq, bsq, [P[6], P[7]], [bP[6], bP[7]],
                  rstd, brstd, epst, beps, 1.0 / D)
        for ch in range(16):
            S.op("dve", lambda e: e.scalar_tensor_tensor(out=h_sb[:, ch, :], in0=x_sb[:, ch, :], scalar=g_sb[:, ch:ch + 1],
                                                         in1=rstd[:], op0=ALU.mult, op1=ALU.mult),
                 reads=[bx[ch], bg, brstd], writes=[bh[ch]])

        rot = [0]
        evc = [0]

        def next_ps():
            i = rot[0] % 6
            rot[0] += 1
            return P[i], bP[i]

        def evac(dst_ap, src_ap, scale, reads, writes):
            if evc[0] % 2 == 0:
                S.op("act", lambda e: e.activation(dst_ap, src_ap, AF.Copy, scale=float(scale)), reads=reads, writes=writes)
            else:
                S.op("dve", lambda e: e.tensor_scalar(out=dst_ap, in0=src_ap, scalar1=float(scale), scalar2=None, op0=ALU.mult),
                     reads=reads, writes=writes)
            evc[0] += 1

        n32 = n16 = nT = 0
        for pi, (c0, ncols, fm_jobs, tm_jobs) in enumerate(panels):
            s = pi % 2
            if pi + 1 < len(panels):
                load_panel(pi + 1)
            pn = pan[s]
            for (off, M, dst, row0, scale, kind) in fm_jobs:
                if kind == 'f32':
                    stg, bstg = stg32[n32 % 2], bstg32[n32 % 2]; n32 += 1
                else:
                    stg, bstg = stg16[n16 % 2], bstg16[n16 % 2]; n16 += 1
                for th in range(2):
                    ps, bps = next_ps()
                    for ch in range(16):
                        S.op("pe", lambda e: e.matmul(ps[0:M, :], pn[:, ch, off:off + M], h_sb[:, ch, th * 512:(th + 1) * 512],
                                                      start=(ch == 0), stop=(ch == 15)),
                             reads=[bpan[s], bh[ch]], writes=[bps])
                    if isinstance(kind, tuple):
                        soff = kind[1]
                        ps2, bps2 = next_ps()
                        for ch in range(16):
                            S.op("pe", lambda e: e.matmul(ps2[0:M, :], pn[:, ch, soff:soff + M], h_sb[:, ch, th * 512:(th + 1) * 512],
                                                          start=(ch == 0), stop=(ch == 15)),
                                 reads=[bpan[s], bh[ch]], writes=[bps2])
                        S.op("dve", lambda e: e.tensor_tensor(out=tmpA[:], in0=ps[:], in1=rf_sb[:, 0, th * 512:(th + 1) * 512], op=ALU.mult),
                             reads=[bps, brf], writes=[btA])
                        S.op("dve", lambda e: e.tensor_tensor(out=tmpB[:], in0=ps2[:], in1=rf_sb[:, 1, th * 512:(th + 1) * 512], op=ALU.mult),
                             reads=[bps2, brf], writes=[btB])
                        S.op("dve", lambda e: e.tensor_tensor(out=tmpA[:], in0=tmpA[:], in1=tmpB[:], op=ALU.add),
                             reads=[btA, btB], writes=[btA])
                        S.op("dve", lambda e: e.tensor_scalar(out=stg[:, th * 512:(th + 1) * 512], in0=tmpA[:], scalar1=float(scale),
                                                              scalar2=None, op0=ALU.mult),
                             reads=[btA], writes=[bstg])
                    else:
                        evac(stg[0:M, th * 512:(th + 1) * 512], ps[0:M, :], scale, [bps], [bstg])
                S.dma("sp", dst[row0:row0 + M, :], stg[0:M, :], reads=[bstg], owner=bstg)
            for (off, ncl, pv0, scale, soff) in tm_jobs:
                for tt in range(8):
                    ps, bps = next_ps()
                    for ch in range(16):
                        S.op("pe", lambda e: e.matmul(ps[:, 0:ncl], h_sb[:, ch, tt * 128:(tt + 1) * 128], pn[:, ch, off:off + ncl],
                                                      start=(ch == 0), stop=(ch == 15)),
                             reads=[bpan[s], bh[ch]], writes=[bps])
                    stt, bstt = stgT[nT % 3], bstgT[nT % 3]; nT += 1
                    if soff is not None:
                        ps2, bps2 = next_ps()
                        for ch in range(16):
                            S.op("pe", lambda e: e.matmul(ps2[:, 0:ncl], h_sb[:, ch, tt * 128:(tt + 1) * 128], pn[:, ch, soff:soff + ncl],
                                                          start=(ch == 0), stop=(ch == 15)),
                                 reads=[bpan[s], bh[ch]], writes=[bps2])
                        S.op("dve", lambda e: e.tensor_tensor(out=tmpA[:, 0:ncl], in0=ps[:, 0:ncl], in1=rt_sb[:, 0, tt, :], op=ALU.mult),
                             reads=[bps, brt], writes=[btA])
                        S.op("dve", lambda e: e.tensor_tensor(out=tmpB[:, 0:ncl], in0=ps2[:, 0:ncl], in1=rt_sb[:, 1, tt, :], op=ALU.mult),
                             reads=[bps2, brt], writes=[btB])
                        S.op("dve", lambda e: e.tensor_tensor(out=tmpA[:, 0:ncl], in0=tmpA[:, 0:ncl], in1=tmpB[:, 0:ncl], op=ALU.add),
                             reads=[btA, btB], writes=[btA])
                        S.op("dve", lambda e: e.tensor_scalar(out=stt[:, 0:ncl], in0=tmpA[:, 0:ncl], scalar1=float(scale), scalar2=None, op0=ALU.mult),
                             reads=[btA], writes=[bstt])
                    else:
                        evac(stt[:, 0:ncl], ps[:, 0:ncl], scale, [bps], [bstt])
                    S.dma("sp", PV[tt * 128:(tt + 1) * 128, pv0:pv0 + ncl], stt[:, 0:ncl], reads=[bstt], owner=bstt)
        S.finish()
    return nc


def build_C(final):
    nc = bass.Bass("TRN2", target_bir_lowering=False)
    xh = nc.dram_tensor("xh", [D, NTH], F32, kind="ExternalInput").ap()
    mh = nc.dram_tensor("mh", [D, NTH], BF16, kind="ExternalInput").ap()
    wo = nc.dram_tensor("wo", [D, D], F32, kind="ExternalInput").ap()
    g2 = nc.dram_tensor("g2", [128, 16], F32, kind="ExternalInput").ap()
    wup = nc.dram_tensor("wup", [D, 2 * DFF], F32, kind="ExternalInput").ap()
    cw = nc.dram_tensor("cw", [128, 88, 4], F32, kind="ExternalInput").ap()
    wdn = nc.dram_tensor("wdn", [DFF, D], F32, kind="ExternalInput").ap()
    gF = nc.dram_tensor("gF", [128, 16], F32, kind="ExternalInput").ap()
    xo = nc.dram_tensor("xo", [D, NT], F32, kind="ExternalOutput").ap()
    PIECES = [(0, 512), (512, 1024), (1024, NTH)]
    with contextlib.ExitStack() as st:
        S = Sched(nc, st)
        x_sb = S.sbuf("x_sb", [128, 16, NTH], F32); bx = [Buf() for _ in range(16)]
        m_sb = S.sbuf("m_sb", [128, 16, NTH], BF16); bm = [Buf() for _ in range(16)]
        act_sb = S.sbuf("act_sb", [128, 22, NT], BF16); bact = [Buf() for _ in range(22)]
        g_sb = S.sbuf("g_sb", [128, 16], F32); bg = Buf()
        gF_sb = S.sbuf("gF_sb", [128, 16], F32); bgF = Buf()
        cw_sb = S.sbuf("cw_sb", [128, 88, 4], F32); bcw = Buf()
        ones = S.sbuf("ones", [128, 128], BF16); bones = Buf()
        epst = S.sbuf("epst", [128, 1], F32); beps = Buf()
        sq = [S.sbuf("sq%d" % i, [128, NTH], BF16) for i in range(2)]; bsq = [Buf(), Buf()]
        rstd = S.sbuf("rstd", [128, NTH], F32); brstd = Buf()
        pool_ = [S.sbuf("pool%d" % i, [128, 8192], BF16) for i in range(2)]; bpool = [Buf(), Buf()]
        ua = S.sbuf("ua", [128, NTH], F32); bua = Buf()
        ug = S.sbuf("ug", [128, NTH], F32); bug = Buf()
        ya = S.sbuf("ya", [128, NT], F32); bya = Buf()
        yg = S.sbuf("yg", [128, NT], F32); byg = Buf()
        P = [S.psum("P%d" % i, [128, 512]) for i in range(8)]; bP = [Buf() for _ in range(8)]

        S.op("pool", lambda e: e.memset(ones[:], 1.0), writes=[bones])
        S.op("pool", lambda e: e.memset(epst[:], EPS), writes=[beps])
        S.dma("sp", g_sb[:], g2, writes=[bg])
        S.dma("sp", gF_sb[:], gF, writes=[bgF])
        S.dma("sp", cw_sb[:], cw, writes=[bcw])
        for ch in range(16):
            S.dma("sp", m_sb[:, ch, :], mh[ch * 128:(ch + 1) * 128, :], writes=[bm[ch]])
        for ch in range(16):
            S.dma("sp", x_sb[:, ch, :], xh[ch * 128:(ch + 1) * 128, :], writes=[bx[ch]])

        jobs = []
        rot = [0]

        def next_ps(n=7):
            i = rot[0] % n
            rot[0] += 1
            return P[i], bP[i]

        def mk_wo(pi):
            def load(s):
                v = pool_[s][:].rearrange("p (c n) -> p c n", c=16)
                S.dma("pool", v, _chunked(wo[:, pi * 512:(pi + 1) * 512]), writes=[bpool[s]])

            def comp(s):
                v = pool_[s][:].rearrange("p (c n) -> p c n", c=16)
                for j in range(4):
                    fb = pi * 4 + j
                    for (c0, c1) in PIECES:
                        ps, bps = next_ps()
                        for ch in range(16):
                            S.op("pe", lambda e: e.matmul(ps[:, 0:c1 - c0], v[:, ch, j * 128:(j + 1) * 128], m_sb[:, ch, c0:c1],
                                                          start=(ch == 0), stop=(ch == 15)),
                                 reads=[bpool[s], bm[ch]], writes=[bps])
                        S.op("dve", lambda e: e.tensor_tensor(out=x_sb[:, fb, c0:c1], in0=ps[:, 0:c1 - c0], in1=x_sb[:, fb, c0:c1], op=ALU.add),
                             reads=[bps, bx[fb]], writes=[bx[fb]])
            return load, comp

        for pi in range(4):
            jobs.append(mk_wo(pi))

        def norm2(s_unused=None):
            _rms_rstd(S, nc, x_sb, bx, NTH, PIECES, ones, bones, sq, bsq, [P[5], P[6], P[7]], [bP[5], bP[6], bP[7]],
                      rstd, brstd, epst, beps, 1.0 / D)
            for ch in range(16):
                S.op("dve", lambda e: e.scalar_tensor_tensor(out=m_sb[:, ch, :], in0=x_sb[:, ch, :], scalar=g_sb[:, ch:ch + 1],
                                                             in1=rstd[:], op0=ALU.mult, op1=ALU.mult),
                     reads=[bx[ch], bg, brstd], writes=[bm[ch]])

        def mk_up(hf, pp):
            a0 = (hf * 22 + pp * 2) * 128

            def load(s):
                v = pool_[s][:].rearrange("p (c n) -> p c n", c=16)
                S.dma("pool", v[:, :, 0:256], _chunked(wup[:, a0:a0 + 256]), writes=[bpool[s]])
                S.dma("pool", v[:, :, 256:512], _chunked(wup[:, DFF + a0:DFF + a0 + 256]), writes=[bpool[s]])

            def comp(s):
                v = pool_[s][:].rearrange("p (c n) -> p c n", c=16)
                for j in range(2):
                    li = pp * 2 + j
                    blk = hf * 22 + li
                    for (coff, u, bu, y, by, wi) in ((j * 128, ua, bua, ya, bya, blk), (256 + j * 128, ug, bug, yg, byg, 44 + blk)):
                        pss = []
                        for (c0, c1) in PIECES:
                            if c0 == 1024:
                                ps, bps = P[7], bP[7]
                            else:
                                ps, bps = next_ps(7)
                            for ch in range(16):
                                S.op("pe", lambda e: e.matmul(ps[:, 0:c1 - c0], v[:, ch, coff:coff + 128], m_sb[:, ch, c0:c1],
                                                              start=(ch == 0), stop=(ch == 15)),
                                     reads=[bpool[s], bm[ch]], writes=[bps])
                            pss.append((ps, bps))
                        S.op("act", lambda e: e.activation(u[:, 1:513], pss[0][0][:, 0:512], AF.Copy), reads=[pss[0][1]], writes=[bu])
                        S.op("act", lambda e: e.activation(u[:, 513:1025], pss[1][0][:, 0:512], AF.Copy), reads=[pss[1][1]], writes=[bu])
                        S.op("act", lambda e: e.activation(u[:, 0:1], pss[2][0][:, 0:1], AF.Copy), reads=[pss[2][1]], writes=[bu])
                        S.op("act", lambda e: e.activation(u[:, 1025:1026], pss[2][0][:, 1:2], AF.Copy), reads=[pss[2][1]], writes=[bu])
                        S.op("dve", lambda e: e.tensor_scalar(out=y[:], in0=u[:, 1:1025], scalar1=cw_sb[:, wi, 1:2], scalar2=cw_sb[:, wi, 3:4],
                                                              op0=ALU.mult, op1=ALU.add), reads=[bu, bcw], writes=[by])
                        S.op("dve", lambda e: e.scalar_tensor_tensor(out=y[:], in0=u[:, 0:1024], scalar=cw_sb[:, wi, 0:1], in1=y[:],
                                                                     op0=ALU.mult, op1=ALU.add), reads=[bu, bcw, by], writes=[by])
                        S.op("dve", lambda e: e.scalar_tensor_tensor(out=y[:], in0=u[:, 2:1026], scalar=cw_sb[:, wi, 2:3], in1=y[:],
                                                                     op0=ALU.mult, op1=ALU.add), reads=[bu, bcw, by], writes=[by])
                    S.op("act", lambda e: e.activation(yg[:], yg[:], AF.Silu), reads=[byg], writes=[byg])
                    S.op("dve", lambda e: e.tensor_tensor(out=act_sb[:, li, :], in0=ya[:], in1=yg[:], op=ALU.mult),
                         reads=[bya, byg], writes=[bact[li]])
            return load, comp

        def mk_dn(hf, pj):
            def load(s):
                v = pool_[s][:, 0:22 * 256].rearrange("p (c n) -> p c n", c=22)
                src = wdn[hf * 22 * 128:(hf + 1) * 22 * 128, pj * 256:(pj + 1) * 256].rearrange("(c p) n -> p c n", p=128)
                S.dma("pool", v, src, writes=[bpool[s]])

            def comp(s):
                v = pool_[s][:, 0:22 * 256].rearrange("p (c n) -> p c n", c=22)
                for j in range(2):
                    fb = pj * 2 + j
                    for th in range(2):
                        ps, bps = next_ps()
                        for li in range(22):
                            S.op("pe", lambda e: e.matmul(ps[:], v[:, li, j * 128:(j + 1) * 128], act_sb[:, li, th * 512:(th + 1) * 512],
                                                          start=(li == 0), stop=(li == 21)),
                                 reads=[bpool[s], bact[li]], writes=[bps])
                        S.op("dve", lambda e: e.tensor_tensor(out=x_sb[:, fb, th * 512:(th + 1) * 512], in0=ps[:],
                                                              in1=x_sb[:, fb, th * 512:(th + 1) * 512], op=ALU.add),
                             reads=[bps, bx[fb]], writes=[bx[fb]])
            return load, comp

        jobs.append((None, norm2))
        for hf in range(2):
            for pp in range(11):
                jobs.append(mk_up(hf, pp))
            for pj in range(8):
                jobs.append(mk_dn(hf, pj))

        loads = [(i, j[0]) for i, j in enumerate(jobs) if j[0] is not None]
        slot_of = {}
        for k, (i, _) in enumerate(loads):
            slot_of[i] = k % 2
        nxt = 0
        loads[0][1](slot_of[loads[0][0]]); nxt = 1
        for i, (ld, comp) in enumerate(jobs):
            if ld is not None:
                if nxt < len(loads):
                    loads[nxt][1](slot_of[loads[nxt][0]]); nxt += 1
                comp(slot_of[i])
            else:
                comp()

        if final:
            _rms_rstd(S, nc, x_sb, bx, NT, [(0, 512), (512, 1024)], ones, bones, sq, bsq, [P[5], P[6]], [bP[5], bP[6]],
                      rstd, brstd, epst, beps, 1.0 / D)
            for ch in range(16):
                S.op("dve", lambda e: e.scalar_tensor_tensor(out=x_sb[:, ch, 0:NT], in0=x_sb[:, ch, 0:NT], scalar=gF_sb[:, ch:ch + 1],
                                                             in1=rstd[:, 0:NT], op0=ALU.mult, op1=ALU.mult),
                     reads=[bx[ch], bgF, brstd], writes=[bx[ch]])
        for ch in range(16):
            S.dma("sp", xo[ch * 128:(ch + 1) * 128, :], x_sb[:, ch, 0:NT], reads=[bx[ch]], owner=bx[ch])
        S.finish()
    return nc


def _vec16(v):
    return np.ascontiguousarray(np.asarray(v, np.float32).reshape(16, 128).T)


def _rot_tables():
    half = 32
    inv = (1.0 / (10000.0 ** np.linspace(0.0, 1.0, half, dtype=np.float32))).astype(np.float32)
    ang = np.arange(T, dtype=np.float32)[:, None] * inv[None, :]
    return np.cos(ang).astype(np.float32), np.sin(ang).astype(np.float32)


def _rot_core_tables(cos, sin, c):
    pos = (c % 4) * NT + np.arange(NT)
    r = np.arange(128)
    sgn_r = np.where(r % 64 < 32, -1.0, 1.0).astype(np.float32)
    rotf = np.empty((128, 2, NT), np.float32)
    rotf[:, 0, :] = cos[pos][:, r % 32].T
    rotf[:, 1, :] = (sin[pos][:, r % 32] * sgn_r[None, :]).T
    col = np.arange(256)
    sgn_c = np.where(col % 64 < 32, -1.0, 1.0).astype(np.float32)
    ct = cos[pos][:, col % 32].reshape(8, 128, 256).transpose(1, 0, 2)
    stb = (sin[pos][:, col % 32] * sgn_c[None, :]).reshape(8, 128, 256).transpose(1, 0, 2)
    rott = np.ascontiguousarray(np.stack([ct, stb], axis=1))
    return rotf, rott


def _w_ext(w_in_l):
    def swapped(c0):
        idx = np.arange(256)
        src = c0 + (idx // 64) * 64 + ((idx % 64) + 32) % 64
        return w_in_l[:, src]
    return np.ascontiguousarray(np.concatenate([w_in_l, swapped(4640), swapped(4896)], axis=1))


_PROG = {}


def _prog(name, fn, *a):
    key = (name,) + a
    if key not in _PROG:
        _PROG[key] = fn(*a)
    return _PROG[key]


def run_A(xT_cores, g1, w_ext, rot_cores):
    nc = _prog("A", build_A)
    g = _vec16(g1)
    in_maps = [{"xT": xT_cores[c], "g1": g, "w": w_ext, "rotf": rot_cores[c][0], "rott": rot_cores[c][1]} for c in range(NCORE)]
    res = run_bass_kernel_spmd(nc, in_maps, core_ids=list(range(NCORE)))
    return [res.results[c] for c in range(NCORE)]


def run_C(xh_cores, mh_cores, wo, g2, wup, conv_w, conv_b, wdn, gF, final):
    nc = _prog("C", build_C, bool(final))
    cw = np.empty((128, 88, 4), np.float32)
    for i in range(3):
        cw[:, :, i] = conv_w[i].reshape(88, 128).T
    cw[:, :, 3] = conv_b.reshape(88, 128).T
    in_maps = [{"xh": xh_cores[c], "mh": mh_cores[c], "wo": wo, "g2": _vec16(g2), "wup": wup, "cw": cw, "wdn": wdn,
                "gF": _vec16(gF)} for c in range(NCORE)]
    res = run_bass_kernel_spmd(nc, in_maps, core_ids=list(range(NCORE)))
    return [res.results[c]["xo"] for c in range(NCORE)]


def _halo(full_T, c, dtype):
    b, q = c // 4, c % 4
    out = np.zeros((D, NTH), dtype)
    out[:, 0:NT] = full_T[b][:, q * NT:(q + 1) * NT]
    if q > 0:
        out[:, NT] = full_T[b][:, q * NT - 1]
    if q < 3:
        out[:, NT + 1] = full_T[b][:, (q + 1) * NT]
    return out


NEAR_D = (-128, 0, 128, 256, 384, 512)


def build_B():
    nc = bass.Bass("TRN2", target_bir_lowering=False)
    dqk = nc.dram_tensor("dqk", [2, 4, 64, T], BF16, kind="ExternalInput").ap()
    dv = nc.dram_tensor("dv", [2, T, 128], BF16, kind="ExternalInput").ap()
    btile = nc.dram_tensor("btile", [6, 128, 512], F32, kind="ExternalInput").ap()
    cb = nc.dram_tensor("cb", [128, 2], F32, kind="ExternalInput").ap()
    lamp = nc.dram_tensor("lamp", [128, 256], F32, kind="ExternalInput").ap()
    lcst = nc.dram_tensor("lcst", [128, 2], F32, kind="ExternalInput").ap()
    gains = nc.dram_tensor("gains", [128, 3], F32, kind="ExternalInput").ap()
    lqk = nc.dram_tensor("lqk", [2, 2, 64, T], BF16, kind="ExternalInput").ap()
    lkv = nc.dram_tensor("lkv", [2, T, 192], BF16, kind="ExternalInput").ap()
    ga = nc.dram_tensor("ga", [32, T], BF16, kind="ExternalInput").ap()
    gw = nc.dram_tensor("gw", [33, 128], F32, kind="ExternalInput").ap()
    gate = nc.dram_tensor("gate", [2, 128, T], F32, kind="ExternalInput").ap()
    rdec = nc.dram_tensor("rdec", [128, 2], F32, kind="ExternalInput").ap()
    cmask = nc.dram_tensor("cmask", [128, 4, 128], F32, kind="ExternalInput").ap()
    MO = nc.dram_tensor("MO", [4, 128, T], BF16, kind="ExternalOutput").ap()
    NCH = T // 128
    with contextlib.ExitStack() as st:
        S = Sched(nc, st)
        P = [S.psum("P%d" % i, [128, 512]) for i in range(8)]; bP = [Buf() for _ in range(8)]
        ones = S.sbuf("ones", [128, 128], BF16); bones = Buf()
        onesF = S.sbuf("onesF", [128, 512], F32); bonesF = Buf()
        epst = S.sbuf("epst", [128, 1], F32); beps = Buf()
        one1 = S.sbuf("one1", [128, 1], F32); bone1 = Buf()
        cm = S.sbuf("cm", [128, 4, 128], F32); bcm = Buf()
        M4 = S.sbuf("M4", [128, 2, 512], F32); bM4 = Buf()
        cb_sb = S.sbuf("cb_sb", [128, 2], F32); bcb = Buf()
        lamp_sb = S.sbuf("lamp_sb", [128, 256], F32); blamp = Buf()
        lc_sb = S.sbuf("lc_sb", [128, 2], F32); blc = Buf()
        gn_sb = S.sbuf("gn_sb", [128, 3], F32); bgn = Buf()
        rd_sb = S.sbuf("rd_sb", [128, 2], F32); brd = Buf()
        sm = S.sbuf("sm", [128, 16], F32); bsm = Buf()
        tmp64 = S.sbuf("tmp64", [128, 128], F32); bt64 = Buf()
        S.op("pool", lambda e: e.memset(ones[:], 1.0), writes=[bones])
        S.op("pool", lambda e: e.memset(onesF[:], 1.0), writes=[bonesF])
        S.op("pool", lambda e: e.memset(epst[:], EPS), writes=[beps])
        S.op("pool", lambda e: e.memset(one1[:], 1.0), writes=[bone1])
        S.dma("sp", cm[:], cmask, writes=[bcm])
        S.dma("sp", cb_sb[:], cb, writes=[bcb])
        S.dma("sp", lamp_sb[:], lamp, writes=[blamp])
        S.dma("sp", lc_sb[:], lcst, writes=[blc])
        S.dma("sp", gn_sb[:], gains, writes=[bgn])
        S.dma("sp", rd_sb[:], rdec, writes=[brd])
        for k in range(4):
            S.op("dve", lambda e: e.tensor_copy(M4[:, 0, k * 128:(k + 1) * 128], cm[:, 0, :]), reads=[bcm], writes=[bM4])
            S.op("dve", lambda e: e.tensor_copy(M4[:, 1, k * 128:(k + 1) * 128], cm[:, 1, :]), reads=[bcm], writes=[bM4])

        o_sb = S.sbuf("o_sb", [128, 512], F32); bo = Buf()
        sq_sb = S.sbuf("sq_sb", [128, 512], BF16); bsq = Buf()
        rs_sb = S.sbuf("rs_sb", [128, 512], F32); brs = Buf()
        stg = [S.sbuf("stg%d" % i, [128, 512], BF16) for i in range(2)]; bstg = [Buf(), Buf()]
        nst = [0]

        def finalize(src_ap, src_bufs, pss, bpss, scal_ap, scal_bufs, gate_ap, gate_bufs, dst_ap):
            S.op("act", lambda e: e.activation(o_sb[:], src_ap, AF.Copy), reads=src_bufs, writes=[bo])
            S.op("act", lambda e: e.activation(sq_sb[:], src_ap, AF.Square), reads=src_bufs, writes=[bsq])
            S.op("pe", lambda e: e.matmul(pss[:], ones[:], sq_sb[:], start=True, stop=True), reads=[bones, bsq], writes=[bpss])
            S.op("act", lambda e: e.activation(rs_sb[:], pss[:], AF.Ln, bias=epst[:, 0:1], scale=1.0 / 128), reads=[bpss, beps], writes=[brs])
            S.op("act", lambda e: e.activation(rs_sb[:], rs_sb[:], AF.Exp, scale=-0.5), reads=[brs], writes=[brs])
            S.op("dve", lambda e: e.tensor_tensor(out=o_sb[:], in0=o_sb[:], in1=rs_sb[:], op=ALU.mult), reads=[bo, brs], writes=[bo])
            sg, bsg = stg[nst[0] % 2], bstg[nst[0] % 2]; nst[0] += 1
            if gate_ap is not None:
                S.op("dve", lambda e: e.scalar_tensor_tensor(out=sg[:], in0=o_sb[:], scalar=scal_ap, in1=gate_ap, op0=ALU.mult, op1=ALU.mult),
                     reads=[bo] + scal_bufs + gate_bufs, writes=[bsg])
            else:
                S.op("dve", lambda e: e.tensor_scalar(out=sg[:], in0=o_sb[:], scalar1=scal_ap, scalar2=None, op0=ALU.mult),
                     reads=[bo] + scal_bufs, writes=[bsg])
            S.dma("sp", dst_ap, sg[:], reads=[bsg], owner=bsg)

        qd = S.sbuf("qd", [128, T], BF16); bqd = Buf()
        kd = S.sbuf("kd", [128, T], BF16); bkd = Buf()
        Qf0 = S.sbuf("Qf0", [128, T], BF16); bQf = Buf()
        Qb0 = S.sbuf("Qb0", [128, T], BF16); bQb = Buf()
        Kp = S.sbuf("Kp", [128, T], BF16); bKp = Buf()
        ktv = S.sbuf("ktv", [128, NCH, 192], BF16); bktv = Buf()
        kend = S.sbuf("kend", [128, NCH, 128], BF16); bkend = Buf()
        KVa = S.sbuf("KVa", [128, NCH, 128], F32); bKVf = Buf(); bKVb = Buf()
        Sbf = S.sbuf("Sbf", [128, NCH, 128], BF16); bSbf = Buf()
        gt_sb = S.sbuf("gt_sb", [128, T], F32); bgt = Buf()
        ga_sb = S.sbuf("ga_sb", [33, T], BF16); bga = Buf()
        gw_sb = S.sbuf("gw_sb", [33, 128], BF16); bgw = Buf()
        la4 = [S.sbuf("la4_%d" % i, [128, 4, 128], F32) for i in range(2)]; bla4 = [Buf(), Buf()]
        la_ret = S.sbuf("la_ret", [128, 4, 128], F32); blar = Buf()
        Epos = S.sbuf("Epos", [128, 512], F32); bEpos = Buf()
        Eneg = S.sbuf("Eneg", [128, 512], F32); bEneg = Buf()
        Eend = S.sbuf("Eend", [128, 4, 128], F32); bEend = Buf()
        Dall = S.sbuf("Dall", [128, NCH], F32); bD = Buf()
        Pf = S.sbuf("Pf", [128, 512], BF16); bPf = Buf()
        Pb = S.sbuf("Pb", [128, 512], BF16); bPb = Buf()

        S.op("pool", lambda e: e.memset(Qf0[:], 0.0), writes=[bQf])
        S.op("pool", lambda e: e.memset(Qb0[:], 0.0), writes=[bQb])
        S.op("pool", lambda e: e.memset(ga_sb[:], 1.0), writes=[bga])
        S.dma("sp", ga_sb[0:32, :], ga, writes=[bga])
        S.dma("pool", gw_sb[:], gw, writes=[bgw])
        S.op("act", lambda e: e.activation(sm[:, 0:2], rd_sb[:], AF.Exp, scale=-1.0), reads=[brd], writes=[bsm])
        S.op("act", lambda e: e.activation(sm[:, 0:2], sm[:, 0:2], AF.Ln, bias=one1[:, 0:1]), reads=[bsm, bone1], writes=[bsm])
        S.op("dve", lambda e: e.tensor_scalar(out=sm[:, 0:2], in0=sm[:, 0:2], scalar1=-1.0, scalar2=None, op0=ALU.mult), reads=[bsm], writes=[bsm])
        for k in range(4):
            S.op("dve", lambda e: e.tensor_scalar(out=la_ret[:, k, 0:64], in0=onesF[:, 0:64], scalar1=sm[:, 0:1], scalar2=None, op0=ALU.mult),
                 reads=[bonesF, bsm], writes=[blar])
            S.op("dve", lambda e: e.tensor_scalar(out=la_ret[:, k, 64:128], in0=onesF[:, 0:64], scalar1=sm[:, 1:2], scalar2=None, op0=ALU.mult),
                 reads=[bonesF, bsm], writes=[blar])

        for kind in range(2):
            for half in range(2):
                S.dma("sp", qd[half * 64:(half + 1) * 64, :], lqk[kind, 0], writes=[bqd])
                S.dma("sp", kd[half * 64:(half + 1) * 64, :], lqk[kind, 1], writes=[bkd])
            S.dma("sp", ktv[:], lkv[kind].rearrange("(n p) c -> p n c", p=128), writes=[bktv])
            S.dma("sp", gt_sb[:], gate[kind], writes=[bgt])
            S.op("act", lambda e: e.activation(gt_sb[:], gt_sb[:], AF.Silu), reads=[bgt], writes=[bgt])
            for g in range(NCH // 4):
                cols = slice(g * 512, (g + 1) * 512)
                if kind == 0:
                    la, bla = la4[g % 2], bla4[g % 2]
                    zps, bz = P[0], bP[0]
                    for k in range(4):
                        n = g * 4 + k
                        S.op("pe", lambda e: e.matmul(zps[:, k * 128:(k + 1) * 128], ga_sb[:, n * 128:(n + 1) * 128], gw_sb[:], start=True, stop=True),
                             reads=[bga, bgw], writes=[bz])
                    lav = la[:].rearrange("p k c -> p (k c)")
                    S.op("act", lambda e: e.activation(lav, zps[:], AF.Exp, scale=-1.0), reads=[bz], writes=[bla])
                    S.op("act", lambda e: e.activation(lav, lav, AF.Ln, bias=one1[:, 0:1]), reads=[bla, bone1], writes=[bla])
                    S.op("dve", lambda e: e.tensor_scalar(out=lav, in0=lav, scalar1=-1.0 / 16.0, scalar2=None, op0=ALU.mult), reads=[bla], writes=[bla])
                else:
                    la, bla = la_ret, blar
                psA, bA, psB, bB, psC, bC, psKV, bKV = P[1], bP[1], P[2], bP[2], P[3], bP[3], P[4], bP[4]
                for k in range(4):
                    S.op("pe", lambda e: e.matmul(psA[:, k * 128:(k + 1) * 128], la[:, k, :], cm[:, 0, :], start=True, stop=True),
                         reads=[bla, bcm], writes=[bA])
                    S.op("pe", lambda e: e.matmul(psB[:, k * 128:(k + 1) * 128], la[:, k, :], cm[:, 1, :], start=True, stop=True),
                         reads=[bla, bcm], writes=[bB])
                    S.op("pe", lambda e: e.matmul(psC[:, k * 128:k * 128 + 64], cm[:, 2, :], la[:, k, 0:64], start=True, stop=True),
                         reads=[bla, bcm], writes=[bC])
                    S.op("pe", lambda e: e.matmul(psC[:, k * 128 + 64:(k + 1) * 128], cm[:, 3, :], la[:, k, 64:128], start=True, stop=True),
                         reads=[bla, bcm], writes=[bC])
                S.op("act", lambda e: e.activation(Epos[0:64, :], psA[0:64, :], AF.Exp), reads=[bA], writes=[bEpos])
                S.op("act", lambda e: e.activation(Epos[64:128, :], psB[64:128, :], AF.Exp), reads=[bB], writes=[bEpos])
                S.op("act", lambda e: e.activation(Eneg[0:64, :], psA[0:64, :], AF.Exp, scale=-1.0), reads=[bA], writes=[bEneg])
                S.op("act", lambda e: e.activation(Eneg[64:128, :], psB[64:128, :], AF.Exp, scale=-1.0), reads=[bB], writes=[bEneg])
                S.op("act", lambda e: e.activation(Eend[:].rearrange("p k c -> p (k c)"), psC[:], AF.Exp), reads=[bC], writes=[bEend])
                S.op("dve", lambda e: e.tensor_tensor(out=Qf0[0:64, cols], in0=qd[0:64, cols], in1=Epos[0:64, :], op=ALU.mult),
                     reads=[bqd, bEpos], writes=[bQf])
                S.op("dve", lambda e: e.tensor_tensor(out=Qb0[64:128, cols], in0=qd[64:128, cols], in1=Epos[64:128, :], op=ALU.mult),
                     reads=[bqd, bEpos], writes=[bQb])
                S.op("dve", lambda e: e.tensor_tensor(out=Kp[:, cols], in0=kd[:, cols], in1=Eneg[:], op=ALU.mult),
                     reads=[bkd, bEneg], writes=[bKp])
                S.op("dve", lambda e: e.tensor_tensor(out=kend[:, g * 4:(g + 1) * 4, 0:64], in0=ktv[:, g * 4:(g + 1) * 4, 0:64], in1=Eend[:, :, 0:64], op=ALU.mult),
                     reads=[bktv, bEend], writes=[bkend])
                S.op("dve", lambda e: e.tensor_tensor(out=kend[:, g * 4:(g + 1) * 4, 64:128], in0=ktv[:, g * 4:(g + 1) * 4, 0:64], in1=Eend[:, :, 64:128], op=ALU.mult),
                     reads=[bktv, bEend], writes=[bkend])
                Ev = Epos[:].rearrange("p (k c) -> p k c", c=128)
                S.op("dve", lambda e: e.tensor_copy(Dall[0:64, g * 4:(g + 1) * 4], Ev[0:64, :, 127]), reads=[bEpos], writes=[bD])
                S.op("dve", lambda e: e.tensor_copy(Dall[64:128, g * 4:(g + 1) * 4], Ev[64:128, :, 0]), reads=[bEpos], writes=[bD])
                for k in range(4):
                    n = g * 4 + k
                    S.op("pe", lambda e: e.matmul(psKV[:, k * 128:(k + 1) * 128], kend[:, n, :], ktv[:, n, 64:192], start=True, stop=True),
                         reads=[bkend, bktv], writes=[bKV])
                S.op("act", lambda e: e.activation(KVa[:, g * 4:(g + 1) * 4, :].rearrange("p k c -> p (k c)"), psKV[:], AF.Copy),
                     reads=[bKV], writes=[bKVf, bKVb])
            for i in range(1, NCH):
                n = i
                S.op("dve", lambda e: e.scalar_tensor_tensor(out=KVa[0:64, n, :], in0=KVa[0:64, n - 1, :], scalar=Dall[0:64, n:n + 1],
                                                             in1=KVa[0:64, n, :], op0=ALU.mult, op1=ALU.add),
                     reads=[bD, bKVf], writes=[bKVf])
                m = NCH - 1 - i
                S.op("dve", lambda e: e.scalar_tensor_tensor(out=KVa[64:128, m, :], in0=KVa[64:128, m + 1, :], scalar=Dall[64:128, m:m + 1],
                                                             in1=KVa[64:128, m, :], op0=ALU.mult, op1=ALU.add),
                     reads=[bD, bKVb], writes=[bKVb])
            S.op("pool", lambda e: e.memset(Sbf[0:64, 0, :], 0.0), writes=[bSbf])
            S.op("pool", lambda e: e.memset(Sbf[64:128, NCH - 1, :], 0.0), writes=[bSbf])
            S.op("dve", lambda e: e.tensor_copy(Sbf[0:64, 1:NCH, :], KVa[0:64, 0:NCH - 1, :]), reads=[bKVf], writes=[bSbf])
            S.op("dve", lambda e: e.tensor_copy(Sbf[64:128, 0:NCH - 1, :], KVa[64:128, 1:NCH, :]), reads=[bKVb], writes=[bSbf])
            for g in range(NCH // 4):
                psSf, bSf, psSb, bSb = P[0], bP[0], P[1], bP[1]
                psO, bO = (P[2], bP[2]) if g % 2 == 0 else (P[3], bP[3])
                for k in range(4):
                    n = g * 4 + k
                    cs = slice(n * 128, (n + 1) * 128)
                    S.op("pe", lambda e: e.matmul(psSf[:, k * 128:(k + 1) * 128], Kp[:, cs], Qf0[:, cs], start=True, stop=True),
                         reads=[bKp, bQf], writes=[bSf])
                    S.op("pe", lambda e: e.matmul(psSb[:, k * 128:(k + 1) * 128], Kp[:, cs], Qb0[:, cs], start=True, stop=True),
                         reads=[bKp, bQb], writes=[bSb])
                S.op("dve", lambda e: e.tensor_tensor(out=Pf[:], in0=psSf[:], in1=M4[:, 0, :], op=ALU.mult), reads=[bSf, bM4], writes=[bPf])
                S.op("dve", lambda e: e.tensor_tensor(out=Pb[:], in0=psSb[:], in1=M4[:, 1, :], op=ALU.mult), reads=[bSb, bM4], writes=[bPb])
                for k in range(4):
                    n = g * 4 + k
                    cs = slice(n * 128, (n + 1) * 128)
                    ko = slice(k * 128, (k + 1) * 128)
                    S.op("pe", lambda e: e.matmul(psO[:, ko], ktv[:, n, 64:192], Pf[:, ko], start=True, stop=False), reads=[bktv, bPf], writes=[bO])
                    S.op("pe", lambda e: e.matmul(psO[:, ko], ktv[:, n, 64:192], Pb[:, ko], start=False, stop=False), reads=[bktv, bPb], writes=[bO])
                    S.op("pe", lambda e: e.matmul(psO[:, ko], Sbf[:, n, :], Qf0[:, cs], start=False, stop=False), reads=[bSbf, bQf], writes=[bO])
                    S.op("pe", lambda e: e.matmul(psO[:, ko], Sbf[:, n, :], Qb0[:, cs], start=False, stop=True), reads=[bSbf, bQb], writes=[bO])
                finalize(psO[:], [bO], P[4], bP[4], gn_sb[:, 1 + kind:2 + kind], [bgn], gt_sb[:, g * 512:(g + 1) * 512], [bgt],
                         MO[2 + kind, :, g * 512:(g + 1) * 512])

        S.op("dve", lambda e: e.tensor_tensor(out=tmp64[:, 0:64], in0=lamp_sb[:, 0:64], in1=lamp_sb[:, 64:128], op=ALU.mult), reads=[blamp], writes=[bt64])
        S.op("dve", lambda e: e.tensor_tensor(out=tmp64[:, 64:128], in0=lamp_sb[:, 128:192], in1=lamp_sb[:, 192:256], op=ALU.mult), reads=[blamp], writes=[bt64])
        S.op("dve", lambda e: e.reduce_sum(sm[:, 2:3], tmp64[:, 0:64], axis=mybir.AxisListType.X), reads=[bt64], writes=[bsm])
        S.op("dve", lambda e: e.reduce_sum(sm[:, 3:4], tmp64[:, 64:128], axis=mybir.AxisListType.X), reads=[bt64], writes=[bsm])
        S.op("act", lambda e: e.activation(sm[:, 2:4], sm[:, 2:4], AF.Exp), reads=[bsm], writes=[bsm])
        S.op("dve", lambda e: e.tensor_tensor(out=sm[:, 4:5], in0=sm[:, 3:4], in1=sm[:, 2:3], op=ALU.subtract), reads=[bsm], writes=[bsm])
        S.op("dve", lambda e: e.tensor_tensor(out=sm[:, 4:5], in0=sm[:, 4:5], in1=lc_sb[:, 0:1], op=ALU.subtract), reads=[bsm, blc], writes=[bsm])
        S.op("dve", lambda e: e.tensor_tensor(out=sm[:, 5:6], in0=gn_sb[:, 0:1], in1=lc_sb[:, 1:2], op=ALU.mult), reads=[bgn, blc], writes=[bsm])

        EB = S.sbuf("EB", [128, 6, 512], BF16); bEB = Buf()
        bst = S.sbuf("bst", [128, 512], F32); bbst = Buf()
        for i in range(6):
            S.dma("sp", bst[:], btile[i], writes=[bbst])
            S.op("act", lambda e: e.activation(EB[:, i, :], bst[:], AF.Exp), reads=[bbst], writes=[bEB])
        q1 = S.sbuf("q1", [64, T], BF16); q2 = S.sbuf("q2", [64, T], BF16)
        k1 = S.sbuf("k1", [64, T], BF16); k2 = S.sbuf("k2", [64, T], BF16)
        bq = [Buf() for _ in range(4)]
        vd = S.sbuf("vd", [128, NCH, 128], BF16); bvd = Buf()
        Et = [S.sbuf("Et%d" % i, [128, 512], BF16) for i in range(2)]; bEt = [Buf(), Buf()]
        E = [S.sbuf("E%d" % i, [128, 512], BF16) for i in range(4)]; bE = [Buf() for _ in range(4)]
        r1 = S.sbuf("r1", [128, 512], F32); br1 = Buf()
        r2 = S.sbuf("r2", [128, 512], F32); br2 = Buf()
        od = S.sbuf("od", [128, 512], F32); bod = Buf()
        nE = 0
        nS = 0
        for b in range(2):
            for i, tl in enumerate((q1, q2, k1, k2)):
                S.dma("sp", tl[:], dqk[b, i], writes=[bq[i]])
            S.dma("sp", vd[:], dv[b].rearrange("(n p) c -> p n c", p=128), writes=[bvd])
            for qb in range(T // 512):
                qs = slice(qb * 512, (qb + 1) * 512)
                pO = [(P[4], bP[4]), (P[5], bP[5])]
                pR = [(P[6], bP[6]), (P[7], bP[7])]
                for kt_ in range(NCH):
                    ks = slice(kt_ * 128, (kt_ + 1) * 128)
                    d = kt_ * 128 - qb * 512
                    for si, (kk, qq) in enumerate(((k1, q1), (k2, q2))):
                        ps, bps = P[(nS % 2) * 2 + si], bP[(nS % 2) * 2 + si]
                        S.op("pe", lambda e: e.matmul(ps[:], kk[:, ks], qq[:, qs], start=True, stop=True),
                             reads=[bq[2 + si], bq[si]], writes=[bps])
                        Ecur, bEcur = E[nE % 4], bE[nE % 4]; nE += 1
                        if d in NEAR_D:
                            et, bet = Et[si], bEt[si]
                            S.op("act", lambda e: e.activation(et[:], ps[:], AF.Exp), reads=[bps], writes=[bet])
                            S.op("dve", lambda e: e.tensor_tensor(out=Ecur[:], in0=et[:], in1=EB[:, NEAR_D.index(d), :], op=ALU.mult),
                                 reads=[bet, bEB], writes=[bEcur])
                        else:
                            col = 0 if d < 0 else 1
                            S.op("act", lambda e: e.activation(Ecur[:], ps[:], AF.Exp, bias=cb_sb[:, col:col + 1]), reads=[bps, bcb], writes=[bEcur])
                        S.op("pe", lambda e: e.matmul(pO[si][0][:], vd[:, kt_, :], Ecur[:], start=(kt_ == 0), stop=(kt_ == NCH - 1)),
                             reads=[bvd, bEcur], writes=[pO[si][1]])
                        S.op("pe", lambda e: e.matmul(pR[si][0][:], ones[:], Ecur[:], start=(kt_ == 0), stop=(kt_ == NCH - 1)),
                             reads=[bones, bEcur], writes=[pR[si][1]])
                    nS += 1
                S.op("dve", lambda e: e.reciprocal(r1[:], pR[0][0][:]), reads=[pR[0][1]], writes=[br1])
                S.op("dve", lambda e: e.reciprocal(r2[:], pR[1][0][:]), reads=[pR[1][1]], writes=[br2])
                S.op("dve", lambda e: e.tensor_tensor(out=r1[:], in0=pO[0][0][:], in1=r1[:], op=ALU.mult), reads=[pO[0][1], br1], writes=[br1])
                S.op("dve", lambda e: e.tensor_tensor(out=r2[:], in0=pO[1][0][:], in1=r2[:], op=ALU.mult), reads=[pO[1][1], br2], writes=[br2])
                S.op("dve", lambda e: e.scalar_tensor_tensor(out=od[:], in0=r2[:], scalar=sm[:, 4:5], in1=r1[:], op0=ALU.mult, op1=ALU.add),
                     reads=[br1, br2, bsm], writes=[bod])
                finalize(od[:], [bod], pR[0][0], pR[0][1], sm[:, 5:6], [bsm], None, [], MO[b, :, qs])
        S.finish()
    return nc


def _t5_bucket(rel):
    half, max_exact = 16, 8
    ret = np.where(rel > 0, half, 0)
    n = np.abs(rel)
    nf = np.maximum(n, 1).astype(np.float32)
    large = max_exact + (np.log(nf / max_exact) / math.log(128 / max_exact) * (half - max_exact)).astype(np.int32)
    large = np.minimum(large, half - 1)
    return ret + np.where(n < max_exact, n, large)


def _cmask():
    j = np.arange(128)[:, None]
    i = np.arange(128)[None, :]
    m = np.stack([(j <= i), (j >= i), (j > i), (j < i)], axis=1).astype(np.float32)
    return np.ascontiguousarray(m)


def prep_B(resA, p, L, bucket):
    PT = [np.concatenate([np.asarray(resA[4 * b + q]["PT"]) for q in range(4)], axis=1) for b in range(NB)]
    PG = [np.concatenate([np.asarray(resA[4 * b + q]["PG"]) for q in range(4)], axis=1) for b in range(NB)]
    PV = [np.concatenate([np.asarray(resA[4 * b + q]["PV"]) for q in range(4)], axis=0) for b in range(NB)]
    lam_init = 0.8 - 0.6 * math.exp(-0.3 * L)
    cm = _cmask()
    kl = np.arange(128)[:, None]
    ql = np.arange(512)[None, :]
    in_maps = []
    for c in range(NCORE):
        h, hg, bg = c, c % 4, c // 4
        dqk = np.stack([np.stack([PT[b][PT_DQ + h * 128:PT_DQ + h * 128 + 64], PT[b][PT_DQ + h * 128 + 64:PT_DQ + (h + 1) * 128],
                                  PT[b][PT_DK + h * 128:PT_DK + h * 128 + 64], PT[b][PT_DK + h * 128 + 64:PT_DK + (h + 1) * 128]])
                        for b in range(NB)])
        dv = np.stack([PV[b][:, PV_DV + h * 128:PV_DV + (h + 1) * 128] for b in range(NB)])
        btile = np.stack([p['rel_bias'][bucket[d + kl - ql + T - 1], h] for d in NEAR_D]).astype(np.float32)
        cb = np.empty((128, 2), np.float32)
        cb[:, 0] = p['rel_bias'][15, h]
        cb[:, 1] = p['rel_bias'][31, h]
        lamp = np.ascontiguousarray(np.broadcast_to(p['diff_lambda'][L].reshape(1, 256), (128, 256))).astype(np.float32)
        lcst = np.empty((128, 2), np.float32)
        lcst[:, 0] = lam_init
        lcst[:, 1] = 1.0 - lam_init
        hgain = p['head_gain'][L]
        gains = np.stack([hgain[h * 128:(h + 1) * 128], hgain[1024 + hg * 128:1024 + (hg + 1) * 128],
                          hgain[1536 + hg * 128:1536 + (hg + 1) * 128]], axis=1).astype(np.float32)
        lqk = np.stack([np.stack([PT[bg][PT_GQ + hg * 64:PT_GQ + (hg + 1) * 64], PT[bg][PT_GK + hg * 64:PT_GK + (hg + 1) * 64]]),
                        np.stack([PT[bg][PT_RQ + hg * 64:PT_RQ + (hg + 1) * 64], PT[bg][PT_RK + hg * 64:PT_RK + (hg + 1) * 64]])])
        lkv = np.stack([np.concatenate([PV[bg][:, PV_GK + hg * 64:PV_GK + (hg + 1) * 64], PV[bg][:, PV_GV + hg * 128:PV_GV + (hg + 1) * 128]], axis=1),
                        np.concatenate([PV[bg][:, PV_RK + hg * 64:PV_RK + (hg + 1) * 64], PV[bg][:, PV_RV + hg * 128:PV_RV + (hg + 1) * 128]], axis=1)])
        ga = PT[bg][PT_GA:PT_GA + 32]
        gw = np.zeros((33, 128), np.float32)
        gw[0:16, 0:64] = p['gla_gate_w'][L][0][:, hg * 64:(hg + 1) * 64]
        gw[16:32, 64:128] = p['gla_gate_w'][L][1][:, hg * 64:(hg + 1) * 64]
        gw[32, 0:64] = p['gla_gate_b'][L][0][hg * 64:(hg + 1) * 64]
        gw[32, 64:128] = p['gla_gate_b'][L][1][hg * 64:(hg + 1) * 64]
        gate = np.stack([PG[bg][hg * 128:(hg + 1) * 128], PG[bg][512 + hg * 128:512 + (hg + 1) * 128]])
        rdec = np.empty((128, 2), np.float32)
        rdec[:, 0] = p['ret_decay_logit'][L][0, hg]
        rdec[:, 1] = p['ret_decay_logit'][L][1, hg]
        in_maps.append({"dqk": np.ascontiguousarray(dqk), "dv": np.ascontiguousarray(dv), "btile": np.ascontiguousarray(btile), "cb": cb,
                        "lamp": lamp, "lcst": lcst, "gains": np.ascontiguousarray(gains), "lqk": np.ascontiguousarray(lqk),
                        "lkv": np.ascontiguousarray(lkv), "ga": np.ascontiguousarray(ga), "gw": gw, "gate": np.ascontiguousarray(gate),
                        "rdec": rdec, "cmask": cm})
    return in_maps


def run_B(in_maps):
    nc = _prog("B", build_B)
    res = run_bass_kernel_spmd(nc, in_maps, core_ids=list(range(NCORE)))
    mT = np.empty((NB, D, T), NPBF)
    for c in range(NCORE):
        MO = np.asarray(res.results[c]["MO"])
        h, hg, bg = c, c % 4, c // 4
        for b in range(NB):
            mT[b, h * 128:(h + 1) * 128] = MO[b]
        mT[bg, 1024 + hg * 128:1024 + (hg + 1) * 128] = MO[2]
        mT[bg, 1536 + hg * 128:1536 + (hg + 1) * 128] = MO[3]
    return mT


def kernel(**p):
    p = {k: np.asarray(v) for k, v in p.items()}
    x = p['x'].astype(np.float32)
    xT = np.ascontiguousarray(x.transpose(0, 2, 1))
    cos, sin = _rot_tables()
    rot_cores = [_rot_core_tables(cos, sin, c) for c in range(NCORE)]
    bucket = _t5_bucket(np.arange(-(T - 1), T))
    for L in range(DEPTH):
        xT_cores = [np.ascontiguousarray(xT[c // 4][:, (c % 4) * NT:(c % 4 + 1) * NT]) for c in range(NCORE)]
        resA = run_A(xT_cores, p['ln1_g'][L], _w_ext(p['w_in'][L]), rot_cores)
        mT = run_B(prep_B(resA, p, L, bucket))
        xh = [_halo(xT, c, np.float32) for c in range(NCORE)]
        mh = [_halo(mT, c, NPBF) for c in range(NCORE)]
        xo = run_C(xh, mh, p['w_o'][L], p['ln2_g'][L], p['w_up'][L], p['conv_w'][L], p['conv_b'][L], p['w_down'][L],
                   p['final_g'], L == DEPTH - 1)
        for c in range(NCORE):
            xT[c // 4][:, (c % 4) * NT:(c % 4 + 1) * NT] = xo[c]
    return np.ascontiguousarray(xT.transpose(0, 2, 1)).astype(np.float32)
```

```python
import contextlib
import math
import numpy as np
import ml_dtypes
import concourse.bass as bass
import concourse.mybir as mybir
from concourse.bass_utils import run_bass_kernel_spmd

F32 = mybir.dt.float32
BF16 = mybir.dt.bfloat16
AF = mybir.ActivationFunctionType
ALU = mybir.AluOpType
NPBF = ml_dtypes.bfloat16

D = 2048
T = 4096
NB = 2
NCORE = 8
NT = 1024
NTH = NT + 2
DEPTH = 4
IN_W = 6176
DFF = 5632
EPS = 1e-6
PT_DQ, PT_DK, PT_GQ, PT_GK, PT_RQ, PT_RK, PT_GA = 0, 1024, 2048, 2304, 2560, 2816, 3072
NFM = 3104
PV_DV, PV_GK, PV_GV, PV_RK, PV_RV = 0, 1024, 1280, 1792, 2048
NTM = 2560
W_EXT = IN_W + 512


class Buf:
    __slots__ = ("name", "last_w", "readers", "dma_sem", "dma_cnt")

    def __init__(self, name=""):
        self.name = name
        self.last_w = None
        self.readers = []
        self.dma_sem = None
        self.dma_cnt = 0


class Sched:
    def __init__(self, nc, stack):
        self.nc = nc
        self.stack = stack
        self.eng = {"pe": nc.tensor, "act": nc.scalar, "dve": nc.vector, "pool": nc.gpsimd, "sp": nc.sync}
        self.sem, self.cnt, self.seen = {}, {}, {}
        for k in self.eng:
            self.sem[k] = stack.enter_context(nc.semaphore("s_" + k))
            self.cnt[k] = 0
            self.seen[k] = {}
        self.nsem = len(self.eng)
        self.all_dma = []
        self.n_inst = 0

    def sbuf(self, name, shape, dt):
        return self.stack.enter_context(self.nc.sbuf_tensor(name, shape, dt))

    def psum(self, name, shape, dt=F32):
        return self.stack.enter_context(self.nc.psum_tensor(name, shape, dt))

    def _wait(self, e, deps):
        best = {}
        for d in deps:
            if d is None:
                continue
            sem, val, en = d
            if e == "pe" and en == "pe":
                continue
            key = id(sem)
            if key not in best or best[key][1] < val:
                best[key] = (sem, val)
        for key, (sem, val) in best.items():
            if self.seen[e].get(key, 0) >= val:
                continue
            self.eng[e].wait_ge(sem, val)
            self.seen[e][key] = val

    @staticmethod
    def _deps(reads, writes):
        deps = []
        for b in reads:
            deps.append(b.last_w)
        for b in writes:
            deps.append(b.last_w)
            deps.extend(b.readers)
        return deps

    @staticmethod
    def _mark(tok, reads, writes):
        for b in writes:
            b.last_w = tok
            b.readers = []
        for b in reads:
            if b not in writes:
                b.readers = [r for r in b.readers if r[2] != tok[2] or tok[2] == "dma"] + [tok]

    def op(self, e, fn, reads=(), writes=()):
        self._wait(e, self._deps(reads, writes))
        ins = fn(self.eng[e])
        self.cnt[e] += 1
        ins.then_inc(self.sem[e], 1)
        self._mark((self.sem[e], self.cnt[e], e), reads, writes)
        self.n_inst += 1
        return ins

    def dma(self, e, out, in_, reads=(), writes=(), owner=None):
        self._wait(e, self._deps(reads, writes))
        if owner is None:
            owner = (list(writes) + list(reads))[0]
        if owner.dma_sem is None:
            owner.dma_sem = self.stack.enter_context(self.nc.semaphore("d_%d" % self.nsem))
            self.nsem += 1
            self.all_dma.append(owner)
        owner.dma_cnt += 16
        ins = self.eng[e].dma_start(out=out, in_=in_)
        ins.then_inc(owner.dma_sem, 16)
        self._mark((owner.dma_sem, owner.dma_cnt, "dma"), reads, writes)
        self.n_inst += 1
        return ins

    def finish(self):
        deps = [(b.dma_sem, b.dma_cnt, "dma") for b in self.all_dma]
        for k in self.eng:
            if k != "sp" and self.cnt[k] > 0:
                deps.append((self.sem[k], self.cnt[k], k))
        self._wait("sp", deps)


def _chunked(ap2d):
    return ap2d.rearrange("(c p) n -> p c n", p=128)


def _rms_rstd(S, nc, x_sb, bx, ncols, pieces, ones, bones, sq, bsq, pss, bpss, rstd, brstd, epst, beps, dscale):
    for ch in range(16):
        s = ch % 2
        S.op("act", lambda e: e.activation(sq[s][:, 0:ncols], x_sb[:, ch, 0:ncols], AF.Square),
             reads=[bx[ch]], writes=[bsq[s]])
        for pi, (c0, c1) in enumerate(pieces):
            S.op("pe", lambda e: e.matmul(pss[pi][:, 0:c1 - c0], ones[:], sq[s][:, c0:c1],
                                          start=(ch == 0), stop=(ch == 15)),
                 reads=[bsq[s], bones], writes=[bpss[pi]])
    for pi, (c0, c1) in enumerate(pieces):
        S.op("act", lambda e: e.activation(rstd[:, c0:c1], pss[pi][:, 0:c1 - c0], AF.Ln, bias=epst[:, 0:1], scale=dscale),
             reads=[bpss[pi], beps], writes=[brstd])
    S.op("act", lambda e: e.activation(rstd[:, 0:ncols], rstd[:, 0:ncols], AF.Exp, scale=-0.5),
         reads=[brstd], writes=[brstd])


def build_A():
    nc = bass.Bass("TRN2", target_bir_lowering=False)
    xT = nc.dram_tensor("xT", [D, NT], F32, kind="ExternalInput").ap()
    g1 = nc.dram_tensor("g1", [128, 16], F32, kind="ExternalInput").ap()
    w = nc.dram_tensor("w", [D, W_EXT], F32, kind="ExternalInput").ap()
    rotf = nc.dram_tensor("rotf", [128, 2, NT], F32, kind="ExternalInput").ap()
    rott = nc.dram_tensor("rott", [128, 2, 8, 256], F32, kind="ExternalInput").ap()
    PT = nc.dram_tensor("PT", [NFM, NT], BF16, kind="ExternalOutput").ap()
    PG = nc.dram_tensor("PG", [1024, NT], F32, kind="ExternalOutput").ap()
    PV = nc.dram_tensor("PV", [NT, NTM], BF16, kind="ExternalOutput").ap()
    with contextlib.ExitStack() as st:
        S = Sched(nc, st)
        x_sb = S.sbuf("x_sb", [128, 16, NT], F32); bx = [Buf() for _ in range(16)]
        h_sb = S.sbuf("h_sb", [128, 16, NT], BF16); bh = [Buf() for _ in range(16)]
        g_sb = S.sbuf("g_sb", [128, 16], F32); bg = Buf()
        rf_sb = S.sbuf("rf_sb", [128, 2, NT], F32); brf = Buf()
        rt_sb = S.sbuf("rt_sb", [128, 2, 8, 256], F32); brt = Buf()
        ones = S.sbuf("ones", [128, 128], BF16); bones = Buf()
        epst = S.sbuf("epst", [128, 1], F32); beps = Buf()
        sq = [S.sbuf("sq%d" % i, [128, NT], BF16) for i in range(2)]; bsq = [Buf(), Buf()]
        rstd = S.sbuf("rstd", [128, NT], F32); brstd = Buf()
        pan = [S.sbuf("pan%d" % i, [128, 16, 512], BF16) for i in range(2)]; bpan = [Buf(), Buf()]
        stg32 = [S.sbuf("stg32_%d" % i, [128, NT], F32) for i in range(2)]; bstg32 = [Buf(), Buf()]
        stg16 = [S.sbuf("stg16_%d" % i, [128, NT], BF16) for i in range(2)]; bstg16 = [Buf(), Buf()]
        stgT = [S.sbuf("stgT%d" % i, [128, 512], BF16) for i in range(3)]; bstgT = [Buf() for _ in range(3)]
        tmpA = S.sbuf("tmpA", [128, 512], F32); btA = Buf()
        tmpB = S.sbuf("tmpB", [128, 512], F32); btB = Buf()
        P = [S.psum("P%d" % i, [128, 512]) for i in range(8)]; bP = [Buf() for _ in range(8)]

        S.op("pool", lambda e: e.memset(ones[:], 1.0), writes=[bones])
        S.op("pool", lambda e: e.memset(epst[:], EPS), writes=[beps])
        S.dma("sp", g_sb[:], g1, writes=[bg])
        for ch in range(16):
            S.dma("sp", x_sb[:, ch, :], xT[ch * 128:(ch + 1) * 128, :], writes=[bx[ch]])
        S.dma("sp", rf_sb[:], rotf, writes=[brf])
        S.dma("sp", rt_sb[:], rott, writes=[brt])

        XQ, XK = IN_W, IN_W + 256
        panels = [
            (0, 512, [(j * 128, 128, PT, PT_DQ + j * 128, 0.125, 'bf') for j in range(4)], []),
            (512, 512, [(j * 128, 128, PT, PT_DQ + 512 + j * 128, 0.125, 'bf') for j in range(4)], []),
            (1024, 512, [(j * 128, 128, PT, PT_DK + j * 128, 1.0, 'bf') for j in range(4)], []),
            (1536, 512, [(j * 128, 128, PT, PT_DK + 512 + j * 128, 1.0, 'bf') for j in range(4)], []),
            (2048, 512, [], [(0, 512, PV_DV, 1.0, None)]),
            (2560, 512, [], [(0, 512, PV_DV + 512, 1.0, None)]),
            (3072, 512, [(0, 128, PT, PT_GQ, 0.125, 'bf'), (128, 128, PT, PT_GQ + 128, 0.125, 'bf'),
                         (256, 128, PT, PT_GK, 1.0, 'bf'), (384, 128, PT, PT_GK + 128, 1.0, 'bf')],
             [(256, 256, PV_GK, 1.0, None)]),
            (3584, 512, [], [(0, 512, PV_GV, 1.0, None)]),
            (4096, 512, [(j * 128, 128, PG, j * 128, 1.0, 'f32') for j in range(4)], []),
            (4608, 32, [(0, 32, PT, PT_GA, 1.0, 'bf')], []),
            ((4640, XQ), 512, [(0, 128, PT, PT_RQ, 1.0, ('rot', 256)), (128, 128, PT, PT_RQ + 128, 1.0, ('rot', 384))], []),
            ((4896, XK), 512, [(0, 128, PT, PT_RK, 0.125, ('rot', 256)), (128, 128, PT, PT_RK + 128, 0.125, ('rot', 384))],
             [(0, 256, PV_RK, 0.125, 256)]),
            (5152, 512, [], [(0, 512, PV_RV, 1.0, None)]),
            (5664, 512, [(j * 128, 128, PG, 512 + j * 128, 1.0, 'f32') for j in range(4)], []),
        ]

        def load_panel(pi):
            c0, ncols, _, _ = panels[pi]
            s = pi % 2
            if isinstance(c0, tuple):
                S.dma("pool", pan[s][:, :, 0:256], _chunked(w[:, c0[0]:c0[0] + 256]), writes=[bpan[s]])
                S.dma("pool", pan[s][:, :, 256:512], _chunked(w[:, c0[1]:c0[1] + 256]), writes=[bpan[s]])
            else:
                S.dma("pool", pan[s][:, :, 0:ncols], _chunked(w[:, c0:c0 + ncols]), writes=[bpan[s]])

        load_panel(0)
        _rms_rstd(S, nc, x_sb, bx, NT, [(0, 512), (512, 1024)], ones, bones, sq, bsq, [P[6], P[7]], [bP[6], bP[7]],
                  rstd, brstd, epst, beps, 1.0 / D)
        for ch in range(16):
            S.op("dve", lambda e: e.scalar_tensor_tensor(out=h_sb[:, ch, :], in0=x_sb[:, ch, :], scalar=g_sb[:, ch:ch + 1],
                                                         in1=rstd[:], op0=ALU.mult, op1=ALU.mult),
                 reads=[bx[ch], bg, brstd], writes=[bh[ch]])

        rot = [0]
        evc = [0]

        def next_ps():
            i = rot[0] % 6
            rot[0] += 1
            return P[i], bP[i]

        def evac(dst_ap, src_ap, scale, reads, writes):
            if evc[0] % 2 == 0:
                S.op("act", lambda e: e.activation(dst_ap, src_ap, AF.Copy, scale=float(scale)), reads=reads, writes=writes)
            else:
                S.op("dve", lambda e: e.tensor_scalar(out=dst_ap, in0=src_ap, scalar1=float(scale), scalar2=None, op0=ALU.mult),
                     reads=reads, writes=writes)
            evc[0] += 1

        n32 = n16 = nT = 0
        for pi, (c0, ncols, fm_jobs, tm_jobs) in enumerate(panels):
            s = pi % 2
            if pi + 1 < len(panels):
                load_panel(pi + 1)
            pn = pan[s]
            for (off, M, dst, row0, scale, kind) in fm_jobs:
                if kind == 'f32':
                    stg, bstg = stg32[n32 % 2], bstg32[n32 % 2]; n32 += 1
                else:
                    stg, bstg = stg16[n16 % 2], bstg16[n16 % 2]; n16 += 1
                for th in range(2):
                    ps, bps = next_ps()
                    for ch in range(16):
                        S.op("pe", lambda e: e.matmul(ps[0:M, :], pn[:, ch, off:off + M], h_sb[:, ch, th * 512:(th + 1) * 512],
                                                      start=(ch == 0), stop=(ch == 15)),
                             reads=[bpan[s], bh[ch]], writes=[bps])
                    if isinstance(kind, tuple):
                        soff = kind[1]
                        ps2, bps2 = next_ps()
                        for ch in range(16):
                            S.op("pe", lambda e: e.matmul(ps2[0:M, :], pn[:, ch, soff:soff + M], h_sb[:, ch, th * 512:(th + 1) * 512],
                                                          start=(ch == 0), stop=(ch == 15)),
                                 reads=[bpan[s], bh[ch]], writes=[bps2])
                        S.op("dve", lambda e: e.tensor_tensor(out=tmpA[:], in0=ps[:], in1=rf_sb[:, 0, th * 512:(th + 1) * 512], op=ALU.mult),
                             reads=[bps, brf], writes=[btA])
                        S.op("dve", lambda e: e.tensor_tensor(out=tmpB[:], in0=ps2[:], in1=rf_sb[:, 1, th * 512:(th + 1) * 512], op=ALU.mult),
                             reads=[bps2, brf], writes=[btB])
                        S.op("dve", lambda e: e.tensor_tensor(out=tmpA[:], in0=tmpA[:], in1=tmpB[:], op=ALU.add),
                             reads=[btA, btB], writes=[btA])
                        S.op("dve", lambda e: e.tensor_scalar(out=stg[:, th * 512:(th + 1) * 512], in0=tmpA[:], scalar1=float(scale),
                                                              scalar2=None, op0=ALU.mult),
                             reads=[btA], writes=[bstg])
                    else:
                        evac(stg[0:M, th * 512:(th + 1) * 512], ps[0:M, :], scale, [bps], [bstg])
                S.dma("sp", dst[row0:row0 + M, :], stg[0:M, :], reads=[bstg], owner=bstg)
            for (off, ncl, pv0, scale, soff) in tm_jobs:
                for tt in range(8):
                    ps, bps = next_ps()
                    for ch in range(16):
                        S.op("pe", lambda e: e.matmul(ps[:, 0:ncl], h_sb[:, ch, tt * 128:(tt + 1) * 128], pn[:, ch, off:off + ncl],
                                                      start=(ch == 0), stop=(ch == 15)),
                             reads=[bpan[s], bh[ch]], writes=[bps])
                    stt, bstt = stgT[nT % 3], bstgT[nT % 3]; nT += 1
                    if soff is not None:
                        ps2, bps2 = next_ps()
                        for ch in range(16):
                            S.op("pe", lambda e: e.matmul(ps2[:, 0:ncl], h_sb[:, ch, tt * 128:(tt + 1) * 128], pn[:, ch, soff:soff + ncl],
                                                          start=(ch == 0), stop=(ch == 15)),
                                 reads=[bpan[s], bh[ch]], writes=[bps2])
                        S.op("dve", lambda e: e.tensor_tensor(out=tmpA[:, 0:ncl], in0=ps[:, 0:ncl], in1=rt_sb[:, 0, tt, :], op=ALU.mult),
                             reads=[bps, brt], writes=[btA])
                        S.op("dve", lambda e: e.tensor_tensor(out=tmpB[:, 0:ncl], in0=ps2[:, 0:ncl], in1=rt_sb[:, 1, tt, :], op=ALU.mult),
                             reads=[bps2, brt], writes=[btB])
                        S.op("dve", lambda e: e.tensor_tensor(out=tmpA[:, 0:ncl], in0=tmpA[:, 0:ncl], in1=tmpB[:, 0:ncl], op=ALU.add),
                             reads=[btA, btB], writes=[btA])
                        S.op("dve", lambda e: e.tensor_scalar(out=stt[:, 0:ncl], in0=tmpA[:, 0:ncl], scalar1=float(scale), scalar2=None, op0=ALU.mult),
                             reads=[btA], writes=[bstt])
                    else:
                        evac(stt[:, 0:ncl], ps[:, 0:ncl], scale, [bps], [bstt])
                    S.dma("sp", PV[tt * 128:(tt + 1) * 128, pv0:pv0 + ncl], stt[:, 0:ncl], reads=[bstt], owner=bstt)
        S.finish()
    return nc


def build_C(final):
    nc = bass.Bass("TRN2", target_bir_lowering=False)
    xh = nc.dram_tensor("xh", [D, NTH], F32, kind="ExternalInput").ap()
    mh = nc.dram_tensor("mh", [D, NTH], BF16, kind="ExternalInput").ap()
    wo = nc.dram_tensor("wo", [D, D], F32, kind="ExternalInput").ap()
    g2 = nc.dram_tensor("g2", [128, 16], F32, kind="ExternalInput").ap()
    wup = nc.dram_tensor("wup", [D, 2 * DFF], F32, kind="ExternalInput").ap()
    cw = nc.dram_tensor("cw", [128, 88, 4], F32, kind="ExternalInput").ap()
    wdn = nc.dram_tensor("wdn", [DFF, D], F32, kind="ExternalInput").ap()
    gF = nc.dram_tensor("gF", [128, 16], F32, kind="ExternalInput").ap()
    xo = nc.dram_tensor("xo", [D, NT], F32, kind="ExternalOutput").ap()
    PIECES = [(0, 512), (512, 1024), (1024, NTH)]
    with contextlib.ExitStack() as st:
        S = Sched(nc, st)
        x_sb = S.sbuf("x_sb", [128, 16, NTH], F32); bx = [Buf() for _ in range(16)]
        m_sb = S.sbuf("m_sb", [128, 16, NTH], BF16); bm = [Buf() for _ in range(16)]
        act_sb = S.sbuf("act_sb", [128, 22, NT], BF16); bact = [Buf() for _ in range(22)]
        g_sb = S.sbuf("g_sb", [128, 16], F32); bg = Buf()
        gF_sb = S.sbuf("gF_sb", [128, 16], F32); bgF = Buf()
        cw_sb = S.sbuf("cw_sb", [128, 88, 4], F32); bcw = Buf()
        ones = S.sbuf("ones", [128, 128], BF16); bones = Buf()
        epst = S.sbuf("epst", [128, 1], F32); beps = Buf()
        sq = [S.sbuf("sq%d" % i, [128, NTH], BF16) for i in range(2)]; bsq = [Buf(), Buf()]
        rstd = S.sbuf("rstd", [128, NTH], F32); brstd = Buf()
        pool_ = [S.sbuf("pool%d" % i, [128, 8192], BF16) for i in range(2)]; bpool = [Buf(), Buf()]
        ua = S.sbuf("ua", [128, NTH], F32); bua = Buf()
        ug = S.sbuf("ug", [128, NTH], F32); bug = Buf()
        ya = S.sbuf("ya", [128, NT], F32); bya = Buf()
        yg = S.sbuf("yg", [128, NT], F32); byg = Buf()
        P = [S.psum("P%d" % i, [128, 512]) for i in range(8)]; bP = [Buf() for _ in range(8)]

        S.op("pool", lambda e: e.memset(ones[:], 1.0), writes=[bones])
        S.op("pool", lambda e: e.memset(epst[:], EPS), writes=[beps])
        S.dma("sp", g_sb[:], g2, writes=[bg])
        S.dma("sp", gF_sb[:], gF, writes=[bgF])
        S.dma("sp", cw_sb[:], cw, writes=[bcw])
        for ch in range(16):
            S.dma("sp", m_sb[:, ch, :], mh[ch * 128:(ch + 1) * 128, :], writes=[bm[ch]])
        for ch in range(16):
            S.dma("sp", x_sb[:, ch, :], xh[ch * 128:(ch + 1) * 128, :], writes=[bx[ch]])

        jobs = []
        rot = [0]

        def next_ps(n=7):
            i = rot[0] % n
            rot[0] += 1
            return P[i], bP[i]

        def mk_wo(pi):
            def load(s):
                v = pool_[s][:].rearrange("p (c n) -> p c n", c=16)
                S.dma("pool", v, _chunked(wo[:, pi * 512:(pi + 1) * 512]), writes=[bpool[s]])

            def comp(s):
                v = pool_[s][:].rearrange("p (c n) -> p c n", c=16)
                for j in range(4):
                    fb = pi * 4 + j
                    for (c0, c1) in PIECES:
                        ps, bps = next_ps()
                        for ch in range(16):
                            S.op("pe", lambda e: e.matmul(ps[:, 0:c1 - c0], v[:, ch, j * 128:(j + 1) * 128], m_sb[:, ch, c0:c1],
                                                          start=(ch == 0), stop=(ch == 15)),
                                 reads=[bpool[s], bm[ch]], writes=[bps])
                        S.op("dve", lambda e: e.tensor_tensor(out=x_sb[:, fb, c0:c1], in0=ps[:, 0:c1 - c0], in1=x_sb[:, fb, c0:c1], op=ALU.add),
                             reads=[bps, bx[fb]], writes=[bx[fb]])
            return load, comp

        for pi in range(4):
            jobs.append(mk_wo(pi))

        def norm2(s_unused=None):
            _rms_rstd(S, nc, x_sb, bx, NTH, PIECES, ones, bones, sq, bsq, [P[5], P[6], P[7]], [bP[5], bP[6], bP[7]],
                      rstd, brstd, epst, beps, 1.0 / D)
            for ch in range(16):
                S.op("dve", lambda e: e.scalar_tensor_tensor(out=m_sb[:, ch, :], in0=x_sb[:, ch, :], scalar=g_sb[:, ch:ch + 1],
                                                             in1=rstd[:], op0=ALU.mult, op1=ALU.mult),
                     reads=[bx[ch], bg, brstd], writes=[bm[ch]])

        def mk_up(hf, pp):
            a0 = (hf * 22 + pp * 2) * 128

            def load(s):
                v = pool_[s][:].rearrange("p (c n) -> p c n", c=16)
                S.dma("pool", v[:, :, 0:256], _chunked(wup[:, a0:a0 + 256]), writes=[bpool[s]])
                S.dma("pool", v[:, :, 256:512], _chunked(wup[:, DFF + a0:DFF + a0 + 256]), writes=[bpool[s]])

            def comp(s):
                v = pool_[s][:].rearrange("p (c n) -> p c n", c=16)
                for j in range(2):
                    li = pp * 2 + j
                    blk = hf * 22 + li
                    for (coff, u, bu, y, by, wi) in ((j * 128, ua, bua, ya, bya, blk), (256 + j * 128, ug, bug, yg, byg, 44 + blk)):
                        pss = []
                        for (c0, c1) in PIECES:
                            if c0 == 1024:
                                ps, bps = P[7], bP[7]
                            else:
                                ps, bps = next_ps(7)
                            for ch in range(16):
                                S.op("pe", lambda e: e.matmul(ps[:, 0:c1 - c0], v[:, ch, coff:coff + 128], m_sb[:, ch, c0:c1],
                                                              start=(ch == 0), stop=(ch == 15)),
                                     reads=[bpool[s], bm[ch]], writes=[bps])
                            pss.append((ps, bps))
                        S.op("act", lambda e: e.activation(u[:, 1:513], pss[0][0][:, 0:512], AF.Copy), reads=[pss[0][1]], writes=[bu])
                        S.op("act", lambda e: e.activation(u[:, 513:1025], pss[1][0][:, 0:512], AF.Copy), reads=[pss[1][1]], writes=[bu])
                        S.op("act", lambda e: e.activation(u[:, 0:1], pss[2][0][:, 0:1], AF.Copy), reads=[pss[2][1]], writes=[bu])
                        S.op("act", lambda e: e.activation(u[:, 1025:1026], pss[2][0][:, 1:2], AF.Copy), reads=[pss[2][1]], writes=[bu])
                        S.op("dve", lambda e: e.tensor_scalar(out=y[:], in0=u[:, 1:1025], scalar1=cw_sb[:, wi, 1:2], scalar2=cw_sb[:, wi, 3:4],
                                                              op0=ALU.mult, op1=ALU.add), reads=[bu, bcw], writes=[by])
                        S.op("dve", lambda e: e.scalar_tensor_tensor(out=y[:], in0=u[:, 0:1024], scalar=cw_sb[:, wi, 0:1], in1=y[:],
                                                                     op0=ALU.mult, op1=ALU.add), reads=[bu, bcw, by], writes=[by])
                        S.op("dve", lambda e: e.scalar_tensor_tensor(out=y[:], in0=u[:, 2:1026], scalar=cw_sb[:, wi, 2:3], in1=y[:],
                                                                     op0=ALU.mult, op1=ALU.add), reads=[bu, bcw, by], writes=[by])
                    S.op("act", lambda e: e.activation(yg[:], yg[:], AF.Silu), reads=[byg], writes=[byg])
                    S.op("dve", lambda e: e.tensor_tensor(out=act_sb[:, li, :], in0=ya[:], in1=yg[:], op=ALU.mult),
                         reads=[bya, byg], writes=[bact[li]])
            return load, comp

        def mk_dn(hf, pj):
            def load(s):
                v = pool_[s][:, 0:22 * 256].rearrange("p (c n) -> p c n", c=22)
                src = wdn[hf * 22 * 128:(hf + 1) * 22 * 128, pj * 256:(pj + 1) * 256].rearrange("(c p) n -> p c n", p=128)
                S.dma("pool", v, src, writes=[bpool[s]])

            def comp(s):
                v = pool_[s][:, 0:22 * 256].rearrange("p (c n) -> p c n", c=22)
                for j in range(2):
                    fb = pj * 2 + j
                    for th in range(2):
                        ps, bps = next_ps()
                        for li in range(22):
                            S.op("pe", lambda e: e.matmul(ps[:], v[:, li, j * 128:(j + 1) * 128], act_sb[:, li, th * 512:(th + 1) * 512],
                                                          start=(li == 0), stop=(li == 21)),
                                 reads=[bpool[s], bact[li]], writes=[bps])
                        S.op("dve", lambda e: e.tensor_tensor(out=x_sb[:, fb, th * 512:(th + 1) * 512], in0=ps[:],
                                                              in1=x_sb[:, fb, th * 512:(th + 1) * 512], op=ALU.add),
                             reads=[bps, bx[fb]], writes=[bx[fb]])
            return load, comp

        jobs.append((None, norm2))
        for hf in range(2):
            for pp in range(11):
                jobs.append(mk_up(hf, pp))
            for pj in range(8):
                jobs.append(mk_dn(hf, pj))

        loads = [(i, j[0]) for i, j in enumerate(jobs) if j[0] is not None]
        slot_of = {}
        for k, (i, _) in enumerate(loads):
            slot_of[i] = k % 2
        nxt = 0
        loads[0][1](slot_of[loads[0][0]]); nxt = 1
        for i, (ld, comp) in enumerate(jobs):
            if ld is not None:
                if nxt < len(loads):
                    loads[nxt][1](slot_of[loads[nxt][0]]); nxt += 1
                comp(slot_of[i])
            else:
                comp()

        if final:
            _rms_rstd(S, nc, x_sb, bx, NT, [(0, 512), (512, 1024)], ones, bones, sq, bsq, [P[5], P[6]], [bP[5], bP[6]],
                      rstd, brstd, epst, beps, 1.0 / D)
            for ch in range(16):
                S.op("dve", lambda e: e.scalar_tensor_tensor(out=x_sb[:, ch, 0:NT], in0=x_sb[:, ch, 0:NT], scalar=gF_sb[:, ch:ch + 1],
                                                             in1=rstd[:, 0:NT], op0=ALU.mult, op1=ALU.mult),
                     reads=[bx[ch], bgF, brstd], writes=[bx[ch]])
        for ch in range(16):
            S.dma("sp", xo[ch * 128:(ch + 1) * 128, :], x_sb[:, ch, 0:NT], reads=[bx[ch]], owner=bx[ch])
        S.finish()
    return nc


def _vec16(v):
    return np.ascontiguousarray(np.asarray(v, np.float32).reshape(16, 128).T)


def _rot_tables():
    half = 32
    inv = (1.0 / (10000.0 ** np.linspace(0.0, 1.0, half, dtype=np.float32))).astype(np.float32)
    ang = np.arange(T, dtype=np.float32)[:, None] * inv[None, :]
    return np.cos(ang).astype(np.float32), np.sin(ang).astype(np.float32)


def _rot_core_tables(cos, sin, c):
    pos = (c % 4) * NT + np.arange(NT)
    r = np.arange(128)
    sgn_r = np.where(r % 64 < 32, -1.0, 1.0).astype(np.float32)
    rotf = np.empty((128, 2, NT), np.float32)
    rotf[:, 0, :] = cos[pos][:, r % 32].T
    rotf[:, 1, :] = (sin[pos][:, r % 32] * sgn_r[None, :]).T
    col = np.arange(256)
    sgn_c = np.where(col % 64 < 32, -1.0, 1.0).astype(np.float32)
    ct = cos[pos][:, col % 32].reshape(8, 128, 256).transpose(1, 0, 2)
    stb = (sin[pos][:, col % 32] * sgn_c[None, :]).reshape(8, 128, 256).transpose(1, 0, 2)
    rott = np.ascontiguousarray(np.stack([ct, stb], axis=1))
    return rotf, rott


def _w_ext(w_in_l):
    def swapped(c0):
        idx = np.arange(256)
        src = c0 + (idx // 64) * 64 + ((idx % 64) + 32) % 64
        return w_in_l[:, src]
    return np.ascontiguousarray(np.concatenate([w_in_l, swapped(4640), swapped(4896)], axis=1))


_PROG = {}


def _prog(name, fn, *a):
    key = (name,) + a
    if key not in _PROG:
        _PROG[key] = fn(*a)
    return _PROG[key]


def run_A(xT_cores, g1, w_ext, rot_cores):
    nc = _prog("A", build_A)
    g = _vec16(g1)
    in_maps = [{"xT": xT_cores[c], "g1": g, "w": w_ext, "rotf": rot_cores[c][0], "rott": rot_cores[c][1]} for c in range(NCORE)]
    res = run_bass_kernel_spmd(nc, in_maps, core_ids=list(range(NCORE)))
    return [res.results[c] for c in range(NCORE)]


def run_C(xh_cores, mh_cores, wo, g2, wup, conv_w, conv_b, wdn, gF, final):
    nc = _prog("C", build_C, bool(final))
    cw = np.empty((128, 88, 4), np.float32)
    for i in range(3):
        cw[:, :, i] = conv_w[i].reshape(88, 128).T
    cw[:, :, 3] = conv_b.reshape(88, 128).T
    in_maps = [{"xh": xh_cores[c], "mh": mh_cores[c], "wo": wo, "g2": _vec16(g2), "wup": wup, "cw": cw, "wdn": wdn,
                "gF": _vec16(gF)} for c in range(NCORE)]
    res = run_bass_kernel_spmd(nc, in_maps, core_ids=list(range(NCORE)))
    return [res.results[c]["xo"] for c in range(NCORE)]


def _halo(full_T, c, dtype):
    b, q = c // 4, c % 4
    out = np.zeros((D, NTH), dtype)
    out[:, 0:NT] = full_T[b][:, q * NT:(q + 1) * NT]
    if q > 0:
        out[:, NT] = full_T[b][:, q * NT - 1]
    if q < 3:
        out[:, NT + 1] = full_T[b][:, (q + 1) * NT]
    return out


NEAR_D = (-128, 0, 128, 256, 384, 512)


def build_B():
    nc = bass.Bass("TRN2", target_bir_lowering=False)
    dqk = nc.dram_tensor("dqk", [2, 4, 64, T], BF16, kind="ExternalInput").ap()
    dv = nc.dram_tensor("dv", [2, T, 128], BF16, kind="ExternalInput").ap()
    btile = nc.dram_tensor("btile", [6, 128, 512], F32, kind="ExternalInput").ap()
    cb = nc.dram_tensor("cb", [128, 2], F32, kind="ExternalInput").ap()
    lamp = nc.dram_tensor("lamp", [128, 256], F32, kind="ExternalInput").ap()
    lcst = nc.dram_tensor("lcst", [128, 2], F32, kind="ExternalInput").ap()
    gains = nc.dram_tensor("gains", [128, 3], F32, kind="ExternalInput").ap()
    lqk = nc.dram_tensor("lqk", [2, 2, 64, T], BF16, kind="ExternalInput").ap()
    lkv = nc.dram_tensor("lkv", [2, T, 192], BF16, kind="ExternalInput").ap()
    ga = nc.dram_tensor("ga", [32, T], BF16, kind="ExternalInput").ap()
    gw = nc.dram_tensor("gw", [33, 128], F32, kind="ExternalInput").ap()
    gate = nc.dram_tensor("gate", [2, 128, T], F32, kind="ExternalInput").ap()
    rdec = nc.dram_tensor("rdec", [128, 2], F32, kind="ExternalInput").ap()
    cmask = nc.dram_tensor("cmask", [128, 4, 128], F32, kind="ExternalInput").ap()
    MO = nc.dram_tensor("MO", [4, 128, T], BF16, kind="ExternalOutput").ap()
    NCH = T // 128
    with contextlib.ExitStack() as st:
        S = Sched(nc, st)
        P = [S.psum("P%d" % i, [128, 512]) for i in range(8)]; bP = [Buf() for _ in range(8)]
        ones = S.sbuf("ones", [128, 128], BF16); bones = Buf()
        onesF = S.sbuf("onesF", [128, 512], F32); bonesF = Buf()
        epst = S.sbuf("epst", [128, 1], F32); beps = Buf()
        one1 = S.sbuf("one1", [128, 1], F32); bone1 = Buf()
        cm = S.sbuf("cm", [128, 4, 128], F32); bcm = Buf()
        M4 = S.sbuf("M4", [128, 2, 512], F32); bM4 = Buf()
        cb_sb = S.sbuf("cb_sb", [128, 2], F32); bcb = Buf()
        lamp_sb = S.sbuf("lamp_sb", [128, 256], F32); blamp = Buf()
        lc_sb = S.sbuf("lc_sb", [128, 2], F32); blc = Buf()
        gn_sb = S.sbuf("gn_sb", [128, 3], F32); bgn = Buf()
        rd_sb = S.sbuf("rd_sb", [128, 2], F32); brd = Buf()
        sm = S.sbuf("sm", [128, 16], F32); bsm = Buf()
        tmp64 = S.sbuf("tmp64", [128, 128], F32); bt64 = Buf()
        S.op("pool", lambda e: e.memset(ones[:], 1.0), writes=[bones])
        S.op("pool", lambda e: e.memset(onesF[:], 1.0), writes=[bonesF])
        S.op("pool", lambda e: e.memset(epst[:], EPS), writes=[beps])
        S.op("pool", lambda e: e.memset(one1[:], 1.0), writes=[bone1])
        S.dma("sp", cm[:], cmask, writes=[bcm])
        S.dma("sp", cb_sb[:], cb, writes=[bcb])
        S.dma("sp", lamp_sb[:], lamp, writes=[blamp])
        S.dma("sp", lc_sb[:], lcst, writes=[blc])
        S.dma("sp", gn_sb[:], gains, writes=[bgn])
        S.dma("sp", rd_sb[:], rdec, writes=[brd])
        for k in range(4):
            S.op("dve", lambda e: e.tensor_copy(M4[:, 0, k * 128:(k + 1) * 128], cm[:, 0, :]), reads=[bcm], writes=[bM4])
            S.op("dve", lambda e: e.tensor_copy(M4[:, 1, k * 128:(k + 1) * 128], cm[:, 1, :]), reads=[bcm], writes=[bM4])

        o_sb = S.sbuf("o_sb", [128, 512], F32); bo = Buf()
        sq_sb = S.sbuf("sq_sb", [128, 512], BF16); bsq = Buf()
        rs_sb = S.sbuf("rs_sb", [128, 512], F32); brs = Buf()
        stg = [S.sbuf("stg%d" % i, [128, 512], BF16) for i in range(2)]; bstg = [Buf(), Buf()]
        nst = [0]

        def finalize(src_ap, src_bufs, pss, bpss, scal_ap, scal_bufs, gate_ap, gate_bufs, dst_ap):
            S.op("act", lambda e: e.activation(o_sb[:], src_ap, AF.Copy), reads=src_bufs, writes=[bo])
            S.op("act", lambda e: e.activation(sq_sb[:], src_ap, AF.Square), reads=src_bufs, writes=[bsq])
            S.op("pe", lambda e: e.matmul(pss[:], ones[:], sq_sb[:], start=True, stop=True), reads=[bones, bsq], writes=[bpss])
            S.op("act", lambda e: e.activation(rs_sb[:], pss[:], AF.Ln, bias=epst[:, 0:1], scale=1.0 / 128), reads=[bpss, beps], writes=[brs])
            S.op("act", lambda e: e.activation(rs_sb[:], rs_sb[:], AF.Exp, scale=-0.5), reads=[brs], writes=[brs])
            S.op("dve", lambda e: e.tensor_tensor(out=o_sb[:], in0=o_sb[:], in1=rs_sb[:], op=ALU.mult), reads=[bo, brs], writes=[bo])
            sg, bsg = stg[nst[0] % 2], bstg[nst[0] % 2]; nst[0] += 1
            if gate_ap is not None:
                S.op("dve", lambda e: e.scalar_tensor_tensor(out=sg[:], in0=o_sb[:], scalar=scal_ap, in1=gate_ap, op0=ALU.mult, op1=ALU.mult),
                     reads=[bo] + scal_bufs + gate_bufs, writes=[bsg])
            else:
                S.op("dve", lambda e: e.tensor_scalar(out=sg[:], in0=o_sb[:], scalar1=scal_ap, scalar2=None, op0=ALU.mult),
                     reads=[bo] + scal_bufs, writes=[bsg])
            S.dma("sp", dst_ap, sg[:], reads=[bsg], owner=bsg)

        qd = S.sbuf("qd", [128, T], BF16); bqd = Buf()
        kd = S.sbuf("kd", [128, T], BF16); bkd = Buf()
        Qf0 = S.sbuf("Qf0", [128, T], BF16); bQf = Buf()
        Qb0 = S.sbuf("Qb0", [128, T], BF16); bQb = Buf()
        Kp = S.sbuf("Kp", [128, T], BF16); bKp = Buf()
        ktv = S.sbuf("ktv", [128, NCH, 192], BF16); bktv = Buf()
        kend = S.sbuf("kend", [128, NCH, 128], BF16); bkend = Buf()
        KVa = S.sbuf("KVa", [128, NCH, 128], F32); bKVf = Buf(); bKVb = Buf()
        Sbf = S.sbuf("Sbf", [128, NCH, 128], BF16); bSbf = Buf()
        gt_sb = S.sbuf("gt_sb", [128, T], F32); bgt = Buf()
        ga_sb = S.sbuf("ga_sb", [33, T], BF16); bga = Buf()
        gw_sb = S.sbuf("gw_sb", [33, 128], BF16); bgw = Buf()
        la4 = [S.sbuf("la4_%d" % i, [128, 4, 128], F32) for i in range(2)]; bla4 = [Buf(), Buf()]
        la_ret = S.sbuf("la_ret", [128, 4, 128], F32); blar = Buf()
        Epos = S.sbuf("Epos", [128, 512], F32); bEpos = Buf()
        Eneg = S.sbuf("Eneg", [128, 512], F32); bEneg = Buf()
        Eend = S.sbuf("Eend", [128, 4, 128], F32); bEend = Buf()
        Dall = S.sbuf("Dall", [128, NCH], F32); bD = Buf()
        Pf = S.sbuf("Pf", [128, 512], BF16); bPf = Buf()
        Pb = S.sbuf("Pb", [128, 512], BF16); bPb = Buf()

        S.op("pool", lambda e: e.memset(Qf0[:], 0.0), writes=[bQf])
        S.op("pool", lambda e: e.memset(Qb0[:], 0.0), writes=[bQb])
        S.op("pool", lambda e: e.memset(ga_sb[:], 1.0), writes=[bga])
        S.dma("sp", ga_sb[0:32, :], ga, writes=[bga])
        S.dma("pool", gw_sb[:], gw, writes=[bgw])
        S.op("act", lambda e: e.activation(sm[:, 0:2], rd_sb[:], AF.Exp, scale=-1.0), reads=[brd], writes=[bsm])
        S.op("act", lambda e: e.activation(sm[:, 0:2], sm[:, 0:2], AF.Ln, bias=one1[:, 0:1]), reads=[bsm, bone1], writes=[bsm])
        S.op("dve", lambda e: e.tensor_scalar(out=sm[:, 0:2], in0=sm[:, 0:2], scalar1=-1.0, scalar2=None, op0=ALU.mult), reads=[bsm], writes=[bsm])
        for k in range(4):
            S.op("dve", lambda e: e.tensor_scalar(out=la_ret[:, k, 0:64], in0=onesF[:, 0:64], scalar1=sm[:, 0:1], scalar2=None, op0=ALU.mult),
                 reads=[bonesF, bsm], writes=[blar])
            S.op("dve", lambda e: e.tensor_scalar(out=la_ret[:, k, 64:128], in0=onesF[:, 0:64], scalar1=sm[:, 1:2], scalar2=None, op0=ALU.mult),
                 reads=[bonesF, bsm], writes=[blar])

        for kind in range(2):
            for half in range(2):
                S.dma("sp", qd[half * 64:(half + 1) * 64, :], lqk[kind, 0], writes=[bqd])
                S.dma("sp", kd[half * 64:(half + 1) * 64, :], lqk[kind, 1], writes=[bkd])
            S.dma("sp", ktv[:], lkv[kind].rearrange("(n p) c -> p n c", p=128), writes=[bktv])
            S.dma("sp", gt_sb[:], gate[kind], writes=[bgt])
            S.op("act", lambda e: e.activation(gt_sb[:], gt_sb[:], AF.Silu), reads=[bgt], writes=[bgt])
            for g in range(NCH // 4):
                cols = slice(g * 512, (g + 1) * 512)
                if kind == 0:
                    la, bla = la4[g % 2], bla4[g % 2]
                    zps, bz = P[0], bP[0]
                    for k in range(4):
                        n = g * 4 + k
                        S.op("pe", lambda e: e.matmul(zps[:, k * 128:(k + 1) * 128], ga_sb[:, n * 128:(n + 1) * 128], gw_sb[:], start=True, stop=True),
                             reads=[bga, bgw], writes=[bz])
                    lav = la[:].rearrange("p k c -> p (k c)")
                    S.op("act", lambda e: e.activation(lav, zps[:], AF.Exp, scale=-1.0), reads=[bz], writes=[bla])
                    S.op("act", lambda e: e.activation(lav, lav, AF.Ln, bias=one1[:, 0:1]), reads=[bla, bone1], writes=[bla])
                    S.op("dve", lambda e: e.tensor_scalar(out=lav, in0=lav, scalar1=-1.0 / 16.0, scalar2=None, op0=ALU.mult), reads=[bla], writes=[bla])
                else:
                    la, bla = la_ret, blar
                psA, bA, psB, bB, psC, bC, psKV, bKV = P[1], bP[1], P[2], bP[2], P[3], bP[3], P[4], bP[4]
                for k in range(4):
                    S.op("pe", lambda e: e.matmul(psA[:, k * 128:(k + 1) * 128], la[:, k, :], cm[:, 0, :], start=True, stop=True),
                         reads=[bla, bcm], writes=[bA])
                    S.op("pe", lambda e: e.matmul(psB[:, k * 128:(k + 1) * 128], la[:, k, :], cm[:, 1, :], start=True, stop=True),
                         reads=[bla, bcm], writes=[bB])
                    S.op("pe", lambda e: e.matmul(psC[:, k * 128:k * 128 + 64], cm[:, 2, :], la[:, k, 0:64], start=True, stop=True),
                         reads=[bla, bcm], writes=[bC])
                    S.op("pe", lambda e: e.matmul(psC[:, k * 128 + 64:(k + 1) * 128], cm[:, 3, :], la[:, k, 64:128], start=True, stop=True),
                         reads=[bla, bcm], writes=[bC])
                S.op("act", lambda e: e.activation(Epos[0:64, :], psA[0:64, :], AF.Exp), reads=[bA], writes=[bEpos])
                S.op("act", lambda e: e.activation(Epos[64:128, :], psB[64:128, :], AF.Exp), reads=[bB], writes=[bEpos])
                S.op("act", lambda e: e.activation(Eneg[0:64, :], psA[0:64, :], AF.Exp, scale=-1.0), reads=[bA], writes=[bEneg])
                S.op("act", lambda e: e.activation(Eneg[64:128, :], psB[64:128, :], AF.Exp, scale=-1.0), reads=[bB], writes=[bEneg])
                S.op("act", lambda e: e.activation(Eend[:].rearrange("p k c -> p (k c)"), psC[:], AF.Exp), reads=[bC], writes=[bEend])
                S.op("dve", lambda e: e.tensor_tensor(out=Qf0[0:64, cols], in0=qd[0:64, cols], in1=Epos[0:64, :], op=ALU.mult),
                     reads=[bqd, bEpos], writes=[bQf])
                S.op("dve", lambda e: e.tensor_tensor(out=Qb0[64:128, cols], in0=qd[64:128, cols], in1=Epos[64:128, :], op=ALU.mult),
                     reads=[bqd, bEpos], writes=[bQb])
                S.op("dve", lambda e: e.tensor_tensor(out=Kp[:, cols], in0=kd[:, cols], in1=Eneg[:], op=ALU.mult),
                     reads=[bkd, bEneg], writes=[bKp])
                S.op("dve", lambda e: e.tensor_tensor(out=kend[:, g * 4:(g + 1) * 4, 0:64], in0=ktv[:, g * 4:(g + 1) * 4, 0:64], in1=Eend[:, :, 0:64], op=ALU.mult),
                     reads=[bktv, bEend], writes=[bkend])
                S.op("dve", lambda e: e.tensor_tensor(out=kend[:, g * 4:(g + 1) * 4, 64:128], in0=ktv[:, g * 4:(g + 1) * 4, 0:64], in1=Eend[:, :, 64:128], op=ALU.mult),
                     reads=[bktv, bEend], writes=[bkend])
                Ev = Epos[:].rearrange("p (k c) -> p k c", c=128)
                S.op("dve", lambda e: e.tensor_copy(Dall[0:64, g * 4:(g + 1) * 4], Ev[0:64, :, 127]), reads=[bEpos], writes=[bD])
                S.op("dve", lambda e: e.tensor_copy(Dall[64:128, g * 4:(g + 1) * 4], Ev[64:128, :, 0]), reads=[bEpos], writes=[bD])
                for k in range(4):
                    n = g * 4 + k
                    S.op("pe", lambda e: e.matmul(psKV[:, k * 128:(k + 1) * 128], kend[:, n, :], ktv[:, n, 64:192], start=True, stop=True),
                         reads=[bkend, bktv], writes=[bKV])
                S.op("act", lambda e: e.activation(KVa[:, g * 4:(g + 1) * 4, :].rearrange("p k c -> p (k c)"), psKV[:], AF.Copy),
                     reads=[bKV], writes=[bKVf, bKVb])
            for i in range(1, NCH):
                n = i
                S.op("dve", lambda e: e.scalar_tensor_tensor(out=KVa[0:64, n, :], in0=KVa[0:64, n - 1, :], scalar=Dall[0:64, n:n + 1],
                                                             in1=KVa[0:64, n, :], op0=ALU.mult, op1=ALU.add),
                     reads=[bD, bKVf], writes=[bKVf])
                m = NCH - 1 - i
                S.op("dve", lambda e: e.scalar_tensor_tensor(out=KVa[64:128, m, :], in0=KVa[64:128, m + 1, :], scalar=Dall[64:128, m:m + 1],
                                                             in1=KVa[64:128, m, :], op0=ALU.mult, op1=ALU.add),
                     reads=[bD, bKVb], writes=[bKVb])
            S.op("pool", lambda e: e.memset(Sbf[0:64, 0, :], 0.0), writes=[bSbf])
            S.op("pool", lambda e: e.memset(Sbf[64:128, NCH - 1, :], 0.0), writes=[bSbf])
            S.op("dve", lambda e: e.tensor_copy(Sbf[0:64, 1:NCH, :], KVa[0:64, 0:NCH - 1, :]), reads=[bKVf], writes=[bSbf])
            S.op("dve", lambda e: e.tensor_copy(Sbf[64:128, 0:NCH - 1, :], KVa[64:128, 1:NCH, :]), reads=[bKVb], writes=[bSbf])
            for g in range(NCH // 4):
                psSf, bSf, psSb, bSb = P[0], bP[0], P[1], bP[1]
                psO, bO = (P[2], bP[2]) if g % 2 == 0 else (P[3], bP[3])
                for k in range(4):
                    n = g * 4 + k
                    cs = slice(n * 128, (n + 1) * 128)
                    S.op("pe", lambda e: e.matmul(psSf[:, k * 128:(k + 1) * 128], Kp[:, cs], Qf0[:, cs], start=True, stop=True),
                         reads=[bKp, bQf], writes=[bSf])
                    S.op("pe", lambda e: e.matmul(psSb[:, k * 128:(k + 1) * 128], Kp[:, cs], Qb0[:, cs], start=True, stop=True),
                         reads=[bKp, bQb], writes=[bSb])
                S.op("dve", lambda e: e.tensor_tensor(out=Pf[:], in0=psSf[:], in1=M4[:, 0, :], op=ALU.mult), reads=[bSf, bM4], writes=[bPf])
                S.op("dve", lambda e: e.tensor_tensor(out=Pb[:], in0=psSb[:], in1=M4[:, 1, :], op=ALU.mult), reads=[bSb, bM4], writes=[bPb])
                for k in range(4):
                    n = g * 4 + k
                    cs = slice(n * 128, (n + 1) * 128)
                    ko = slice(k * 128, (k + 1) * 128)
                    S.op("pe", lambda e: e.matmul(psO[:, ko], ktv[:, n, 64:192], Pf[:, ko], start=True, stop=False), reads=[bktv, bPf], writes=[bO])
                    S.op("pe", lambda e: e.matmul(psO[:, ko], ktv[:, n, 64:192], Pb[:, ko], start=False, stop=False), reads=[bktv, bPb], writes=[bO])
                    S.op("pe", lambda e: e.matmul(psO[:, ko], Sbf[:, n, :], Qf0[:, cs], start=False, stop=False), reads=[bSbf, bQf], writes=[bO])
                    S.op("pe", lambda e: e.matmul(psO[:, ko], Sbf[:, n, :], Qb0[:, cs], start=False, stop=True), reads=[bSbf, bQb], writes=[bO])
                finalize(psO[:], [bO], P[4], bP[4], gn_sb[:, 1 + kind:2 + kind], [bgn], gt_sb[:, g * 512:(g + 1) * 512], [bgt],
                         MO[2 + kind, :, g * 512:(g + 1) * 512])

        S.op("dve", lambda e: e.tensor_tensor(out=tmp64[:, 0:64], in0=lamp_sb[:, 0:64], in1=lamp_sb[:, 64:128], op=ALU.mult), reads=[blamp], writes=[bt64])
        S.op("dve", lambda e: e.tensor_tensor(out=tmp64[:, 64:128], in0=lamp_sb[:, 128:192], in1=lamp_sb[:, 192:256], op=ALU.mult), reads=[blamp], writes=[bt64])
        S.op("dve", lambda e: e.reduce_sum(sm[:, 2:3], tmp64[:, 0:64], axis=mybir.AxisListType.X), reads=[bt64], writes=[bsm])
        S.op("dve", lambda e: e.reduce_sum(sm[:, 3:4], tmp64[:, 64:128], axis=mybir.AxisListType.X), reads=[bt64], writes=[bsm])
        S.op("act", lambda e: e.activation(sm[:, 2:4], sm[:, 2:4], AF.Exp), reads=[bsm], writes=[bsm])
        S.op("dve", lambda e: e.tensor_tensor(out=sm[:, 4:5], in0=sm[:, 3:4], in1=sm[:, 2:3], op=ALU.subtract), reads=[bsm], writes=[bsm])
        S.op("dve", lambda e: e.tensor_tensor(out=sm[:, 4:5], in0=sm[:, 4:5], in1=lc_sb[:, 0:1], op=ALU.subtract), reads=[bsm, blc], writes=[bsm])
        S.op("dve", lambda e: e.tensor_tensor(out=sm[:, 5:6], in0=gn_sb[:, 0:1], in1=lc_sb[:, 1:2], op=ALU.mult), reads=[bgn, blc], writes=[bsm])

        EB = S.sbuf("EB", [128, 6, 512], BF16); bEB = Buf()
        bst = S.sbuf("bst", [128, 512], F32); bbst = Buf()
        for i in range(6):
            S.dma("sp", bst[:], btile[i], writes=[bbst])
            S.op("act", lambda e: e.activation(EB[:, i, :], bst[:], AF.Exp), reads=[bbst], writes=[bEB])
        q1 = S.sbuf("q1", [64, T], BF16); q2 = S.sbuf("q2", [64, T], BF16)
        k1 = S.sbuf("k1", [64, T], BF16); k2 = S.sbuf("k2", [64, T], BF16)
        bq = [Buf() for _ in range(4)]
        vd = S.sbuf("vd", [128, NCH, 128], BF16); bvd = Buf()
        Et = [S.sbuf("Et%d" % i, [128, 512], BF16) for i in range(2)]; bEt = [Buf(), Buf()]
        E = [S.sbuf("E%d" % i, [128, 512], BF16) for i in range(4)]; bE = [Buf() for _ in range(4)]
        r1 = S.sbuf("r1", [128, 512], F32); br1 = Buf()
        r2 = S.sbuf("r2", [128, 512], F32); br2 = Buf()
        od = S.sbuf("od", [128, 512], F32); bod = Buf()
        steps = [(b, qb, kt_) for b in range(2) for qb in range(T // 512) for kt_ in range(NCH)]
        loaded = [-1]
        pO = [(P[4], bP[4]), (P[5], bP[5])]
        pR = [(P[6], bP[6]), (P[7], bP[7])]
        Eof = {}

        def load_batch(b):
            for i, tl in enumerate((q1, q2, k1, k2)):
                S.dma("sp", tl[:], dqk[b, i], writes=[bq[i]])
            S.dma("sp", vd[:], dv[b].rearrange("(n p) c -> p n c", p=128), writes=[bvd])
            loaded[0] = b

        def scores(i):
            b, qb, kt_ = steps[i]
            if loaded[0] != b:
                load_batch(b)
            qs = slice(qb * 512, (qb + 1) * 512)
            ks = slice(kt_ * 128, (kt_ + 1) * 128)
            d = kt_ * 128 - qb * 512
            for si, (kk, qq) in enumerate(((k1, q1), (k2, q2))):
                ps, bps = P[(i % 2) * 2 + si], bP[(i % 2) * 2 + si]
                S.op("pe", lambda e: e.matmul(ps[:], kk[:, ks], qq[:, qs], start=True, stop=True),
                     reads=[bq[2 + si], bq[si]], writes=[bps])
                ei = (i % 2) * 2 + si
                Ecur, bEcur = E[ei], bE[ei]
                if d in NEAR_D:
                    et, bet = Et[si], bEt[si]
                    S.op("act", lambda e: e.activation(et[:], ps[:], AF.Exp), reads=[bps], writes=[bet])
                    S.op("dve", lambda e: e.tensor_tensor(out=Ecur[:], in0=et[:], in1=EB[:, NEAR_D.index(d), :], op=ALU.mult),
                         reads=[bet, bEB], writes=[bEcur])
                else:
                    col = 0 if d < 0 else 1
                    S.op("act", lambda e: e.activation(Ecur[:], ps[:], AF.Exp, bias=cb_sb[:, col:col + 1]), reads=[bps, bcb], writes=[bEcur])

        def pv(i):
            b, qb, kt_ = steps[i]
            for si in range(2):
                ei = (i % 2) * 2 + si
                Ecur, bEcur = E[ei], bE[ei]
                S.op("pe", lambda e: e.matmul(pO[si][0][:], vd[:, kt_, :], Ecur[:], start=(kt_ == 0), stop=(kt_ == NCH - 1)),
                     reads=[bvd, bEcur], writes=[pO[si][1]])
                S.op("pe", lambda e: e.matmul(pR[si][0][:], ones[:], Ecur[:], start=(kt_ == 0), stop=(kt_ == NCH - 1)),
                     reads=[bones, bEcur], writes=[pR[si][1]])

        def fin(i):
            b, qb, kt_ = steps[i]
            qs = slice(qb * 512, (qb + 1) * 512)
            S.op("dve", lambda e: e.reciprocal(r1[:], pR[0][0][:]), reads=[pR[0][1]], writes=[br1])
            S.op("dve", lambda e: e.reciprocal(r2[:], pR[1][0][:]), reads=[pR[1][1]], writes=[br2])
            S.op("dve", lambda e: e.tensor_tensor(out=r1[:], in0=pO[0][0][:], in1=r1[:], op=ALU.mult), reads=[pO[0][1], br1], writes=[br1])
            S.op("dve", lambda e: e.tensor_tensor(out=r2[:], in0=pO[1][0][:], in1=r2[:], op=ALU.mult), reads=[pO[1][1], br2], writes=[br2])
            S.op("dve", lambda e: e.scalar_tensor_tensor(out=od[:], in0=r2[:], scalar=sm[:, 4:5], in1=r1[:], op0=ALU.mult, op1=ALU.add),
                 reads=[br1, br2, bsm], writes=[bod])
            finalize(od[:], [bod], pR[0][0], pR[0][1], sm[:, 5:6], [bsm], None, [], MO[b, :, qs])

        scores(0)
        for i in range(len(steps)):
            nxt_same = (i + 1 < len(steps)) and steps[i + 1][0] == steps[i][0]
            if nxt_same:
                scores(i + 1)
            pv(i)
            if steps[i][2] == NCH - 1:
                fin(i)
            if (i + 1 < len(steps)) and not nxt_same:
                scores(i + 1)
        S.finish()
    return nc


def _t5_bucket(rel):
    half, max_exact = 16, 8
    ret = np.where(rel > 0, half, 0)
    n = np.abs(rel)
    nf = np.maximum(n, 1).astype(np.float32)
    large = max_exact + (np.log(nf / max_exact) / math.log(128 / max_exact) * (half - max_exact)).astype(np.int32)
    large = np.minimum(large, half - 1)
    return ret + np.where(n < max_exact, n, large)


def _cmask():
    j = np.arange(128)[:, None]
    i = np.arange(128)[None, :]
    m = np.stack([(j <= i), (j >= i), (j > i), (j < i)], axis=1).astype(np.float32)
    return np.ascontiguousarray(m)


def prep_B(resA, p, L, bucket):
    PT = [np.concatenate([np.asarray(resA[4 * b + q]["PT"]) for q in range(4)], axis=1) for b in range(NB)]
    PG = [np.concatenate([np.asarray(resA[4 * b + q]["PG"]) for q in range(4)], axis=1) for b in range(NB)]
    PV = [np.concatenate([np.asarray(resA[4 * b + q]["PV"]) for q in range(4)], axis=0) for b in range(NB)]
    lam_init = 0.8 - 0.6 * math.exp(-0.3 * L)
    cm = _cmask()
    kl = np.arange(128)[:, None]
    ql = np.arange(512)[None, :]
    in_maps = []
    for c in range(NCORE):
        h, hg, bg = c, c % 4, c // 4
        dqk = np.stack([np.stack([PT[b][PT_DQ + h * 128:PT_DQ + h * 128 + 64], PT[b][PT_DQ + h * 128 + 64:PT_DQ + (h + 1) * 128],
                                  PT[b][PT_DK + h * 128:PT_DK + h * 128 + 64], PT[b][PT_DK + h * 128 + 64:PT_DK + (h + 1) * 128]])
                        for b in range(NB)])
        dv = np.stack([PV[b][:, PV_DV + h * 128:PV_DV + (h + 1) * 128] for b in range(NB)])
        btile = np.stack([p['rel_bias'][bucket[d + kl - ql + T - 1], h] for d in NEAR_D]).astype(np.float32)
        cb = np.empty((128, 2), np.float32)
        cb[:, 0] = p['rel_bias'][15, h]
        cb[:, 1] = p['rel_bias'][31, h]
        lamp = np.ascontiguousarray(np.broadcast_to(p['diff_lambda'][L].reshape(1, 256), (128, 256))).astype(np.float32)
        lcst = np.empty((128, 2), np.float32)
        lcst[:, 0] = lam_init
        lcst[:, 1] = 1.0 - lam_init
        hgain = p['head_gain'][L]
        gains = np.stack([hgain[h * 128:(h + 1) * 128], hgain[1024 + hg * 128:1024 + (hg + 1) * 128],
                          hgain[1536 + hg * 128:1536 + (hg + 1) * 128]], axis=1).astype(np.float32)
        lqk = np.stack([np.stack([PT[bg][PT_GQ + hg * 64:PT_GQ + (hg + 1) * 64], PT[bg][PT_GK + hg * 64:PT_GK + (hg + 1) * 64]]),
                        np.stack([PT[bg][PT_RQ + hg * 64:PT_RQ + (hg + 1) * 64], PT[bg][PT_RK + hg * 64:PT_RK + (hg + 1) * 64]])])
        lkv = np.stack([np.concatenate([PV[bg][:, PV_GK + hg * 64:PV_GK + (hg + 1) * 64], PV[bg][:, PV_GV + hg * 128:PV_GV + (hg + 1) * 128]], axis=1),
                        np.concatenate([PV[bg][:, PV_RK + hg * 64:PV_RK + (hg + 1) * 64], PV[bg][:, PV_RV + hg * 128:PV_RV + (hg + 1) * 128]], axis=1)])
        ga = PT[bg][PT_GA:PT_GA + 32]
        gw = np.zeros((33, 128), np.float32)
        gw[0:16, 0:64] = p['gla_gate_w'][L][0][:, hg * 64:(hg + 1) * 64]
        gw[16:32, 64:128] = p['gla_gate_w'][L][1][:, hg * 64:(hg + 1) * 64]
        gw[32, 0:64] = p['gla_gate_b'][L][0][hg * 64:(hg + 1) * 64]
        gw[32, 64:128] = p['gla_gate_b'][L][1][hg * 64:(hg + 1) * 64]
        gate = np.stack([PG[bg][hg * 128:(hg + 1) * 128], PG[bg][512 + hg * 128:512 + (hg + 1) * 128]])
        rdec = np.empty((128, 2), np.float32)
        rdec[:, 0] = p['ret_decay_logit'][L][0, hg]
        rdec[:, 1] = p['ret_decay_logit'][L][1, hg]
        in_maps.append({"dqk": np.ascontiguousarray(dqk), "dv": np.ascontiguousarray(dv), "btile": np.ascontiguousarray(btile), "cb": cb,
                        "lamp": lamp, "lcst": lcst, "gains": np.ascontiguousarray(gains), "lqk": np.ascontiguousarray(lqk),
                        "lkv": np.ascontiguousarray(lkv), "ga": np.ascontiguousarray(ga), "gw": gw, "gate": np.ascontiguousarray(gate),
                        "rdec": rdec, "cmask": cm})
    return in_maps


def run_B(in_maps):
    nc = _prog("B", build_B)
    res = run_bass_kernel_spmd(nc, in_maps, core_ids=list(range(NCORE)))
    mT = np.empty((NB, D, T), NPBF)
    for c in range(NCORE):
        MO = np.asarray(res.results[c]["MO"])
        h, hg, bg = c, c % 4, c // 4
        for b in range(NB):
            mT[b, h * 128:(h + 1) * 128] = MO[b]
        mT[bg, 1024 + hg * 128:1024 + (hg + 1) * 128] = MO[2]
        mT[bg, 1536 + hg * 128:1536 + (hg + 1) * 128] = MO[3]
    return mT


def kernel(**p):
    p = {k: np.asarray(v) for k, v in p.items()}
    x = p['x'].astype(np.float32)
    xT = np.ascontiguousarray(x.transpose(0, 2, 1))
    cos, sin = _rot_tables()
    rot_cores = [_rot_core_tables(cos, sin, c) for c in range(NCORE)]
    bucket = _t5_bucket(np.arange(-(T - 1), T))
    for L in range(DEPTH):
        xT_cores = [np.ascontiguousarray(xT[c // 4][:, (c % 4) * NT:(c % 4 + 1) * NT]) for c in range(NCORE)]
        resA = run_A(xT_cores, p['ln1_g'][L], _w_ext(p['w_in'][L]), rot_cores)
        mT = run_B(prep_B(resA, p, L, bucket))
        xh = [_halo(xT, c, np.float32) for c in range(NCORE)]
        mh = [_halo(mT, c, NPBF) for c in range(NCORE)]
        xo = run_C(xh, mh, p['w_o'][L], p['ln2_g'][L], p['w_up'][L], p['conv_w'][L], p['conv_b'][L], p['w_down'][L],
                   p['final_g'], L == DEPTH - 1)
        for c in range(NCORE):
            xT[c // 4][:, (c % 4) * NT:(c % 4 + 1) * NT] = xo[c]
    return np.ascontiguousarray(xT.transpose(0, 2, 1)).astype(np.float32)
```

```python
import contextlib
import math
import numpy as np
import ml_dtypes
import concourse.bass as bass
import concourse.mybir as mybir
from concourse.bass_utils import run_bass_kernel_spmd

F32 = mybir.dt.float32
BF16 = mybir.dt.bfloat16
AF = mybir.ActivationFunctionType
ALU = mybir.AluOpType
NPBF = ml_dtypes.bfloat16

D = 2048
T = 4096
NB = 2
NCORE = 8
NT = 1024
NTH = NT + 2
DEPTH = 4
IN_W = 6176
DFF = 5632
EPS = 1e-6
PT_DQ, PT_DK, PT_GQ, PT_GK, PT_RQ, PT_RK, PT_GA = 0, 1024, 2048, 2304, 2560, 2816, 3072
NFM = 3104
PV_DV, PV_GK, PV_GV, PV_RK, PV_RV = 0, 1024, 1280, 1792, 2048
NTM = 2560
W_EXT = IN_W + 512


class Buf:
    __slots__ = ("name", "last_w", "readers", "dma_sem", "dma_cnt")

    def __init__(self, name=""):
        self.name = name
        self.last_w = None
        self.readers = []
        self.dma_sem = None
        self.dma_cnt = 0


class Sched:
    def __init__(self, nc, stack):
        self.nc = nc
        self.stack = stack
        self.eng = {"pe": nc.tensor, "act": nc.scalar, "dve": nc.vector, "pool": nc.gpsimd, "sp": nc.sync}
        self.sem, self.cnt, self.seen = {}, {}, {}
        for k in self.eng:
            self.sem[k] = stack.enter_context(nc.semaphore("s_" + k))
            self.cnt[k] = 0
            self.seen[k] = {}
        self.nsem = len(self.eng)
        self.all_dma = []
        self.sem_pool = []
        self.gstack = stack
        self.n_inst = 0

    def barrier(self):
        deps = [(b.dma_sem, b.dma_cnt, "dma") for b in self.all_dma]
        for k in self.eng:
            if self.cnt[k] > 0:
                deps.append((self.sem[k], self.cnt[k], k))
        for e in self.eng:
            self._wait(e, deps)

    @contextlib.contextmanager
    def phase(self):
        outer = self.stack
        n0 = len(self.all_dma)
        with contextlib.ExitStack() as ps:
            self.stack = ps
            yield
            self.barrier()
            for b in self.all_dma[n0:]:
                self.sem_pool.append((b.dma_sem, b.dma_cnt))
            del self.all_dma[n0:]
        self.stack = outer

    def sbuf(self, name, shape, dt):
        self.n_tiles = getattr(self, "n_tiles", 0) + 1
        return self.stack.enter_context(self.nc.sbuf_tensor("%s_%d" % (name, self.n_tiles), shape, dt))

    def psum(self, name, shape, dt=F32):
        return self.stack.enter_context(self.nc.psum_tensor(name, shape, dt))

    def _wait(self, e, deps):
        best = {}
        for d in deps:
            if d is None:
                continue
            sem, val, en = d
            if e == "pe" and en == "pe":
                continue
            key = id(sem)
            if key not in best or best[key][1] < val:
                best[key] = (sem, val)
        for key, (sem, val) in best.items():
            if self.seen[e].get(key, 0) >= val:
                continue
            self.eng[e].wait_ge(sem, val)
            self.seen[e][key] = val

    @staticmethod
    def _deps(reads, writes):
        deps = []
        for b in reads:
            deps.append(b.last_w)
        for b in writes:
            deps.append(b.last_w)
            deps.extend(b.readers)
        return deps

    @staticmethod
    def _mark(tok, reads, writes):
        for b in writes:
            b.last_w = tok
            b.readers = []
        for b in reads:
            if b not in writes:
                b.readers = [r for r in b.readers if r[2] != tok[2] or tok[2] == "dma"] + [tok]

    def op(self, e, fn, reads=(), writes=()):
        self._wait(e, self._deps(reads, writes))
        ins = fn(self.eng[e])
        self.cnt[e] += 1
        ins.then_inc(self.sem[e], 1)
        self._mark((self.sem[e], self.cnt[e], e), reads, writes)
        self.n_inst += 1
        return ins

    def dma(self, e, out, in_, reads=(), writes=(), owner=None, **kw):
        self._wait(e, self._deps(reads, writes))
        if owner is None:
            owner = (list(writes) + list(reads))[0]
        if owner.dma_sem is None:
            if self.sem_pool:
                owner.dma_sem, owner.dma_cnt = self.sem_pool.pop()
            else:
                owner.dma_sem = self.gstack.enter_context(self.nc.semaphore("d_%d" % self.nsem))
                self.nsem += 1
            self.all_dma.append(owner)
        owner.dma_cnt += 16
        ins = self.eng[e].dma_start(out=out, in_=in_, **kw)
        ins.then_inc(owner.dma_sem, 16)
        self._mark((owner.dma_sem, owner.dma_cnt, "dma"), reads, writes)
        self.n_inst += 1
        return ins

    def finish(self):
        deps = [(b.dma_sem, b.dma_cnt, "dma") for b in self.all_dma]
        for k in self.eng:
            if k != "sp" and self.cnt[k] > 0:
                deps.append((self.sem[k], self.cnt[k], k))
        self._wait("sp", deps)


def _chunked(ap2d):
    return ap2d.rearrange("(c p) n -> p c n", p=128)


def _rms_rstd(S, nc, x_sb, bx, ncols, pieces, ones, bones, sq, bsq, pss, bpss, rstd, brstd, epst, beps, dscale):
    for ch in range(16):
        s = ch % 2
        S.op("act", lambda e: e.activation(sq[s][:, 0:ncols], x_sb[:, ch, 0:ncols], AF.Square),
             reads=[bx[ch]], writes=[bsq[s]])
        for pi, (c0, c1) in enumerate(pieces):
            S.op("pe", lambda e: e.matmul(pss[pi][:, 0:c1 - c0], ones[:], sq[s][:, c0:c1],
                                          start=(ch == 0), stop=(ch == 15)),
                 reads=[bsq[s], bones], writes=[bpss[pi]])
    for pi, (c0, c1) in enumerate(pieces):
        S.op("act", lambda e: e.activation(rstd[:, c0:c1], pss[pi][:, 0:c1 - c0], AF.Ln, bias=epst[:, 0:1], scale=dscale),
             reads=[bpss[pi], beps], writes=[brstd])
    S.op("act", lambda e: e.activation(rstd[:, 0:ncols], rstd[:, 0:ncols], AF.Exp, scale=-0.5),
         reads=[brstd], writes=[brstd])


def emit_A(S, P, bP, xT, g1, w, rotf, rott, PT, PG, PV):
    nc = S.nc
    with S.phase():
        x_sb = S.sbuf("x_sb", [128, 16, NT], F32); bx = [Buf() for _ in range(16)]
        h_sb = S.sbuf("h_sb", [128, 16, NT], BF16); bh = [Buf() for _ in range(16)]
        g_sb = S.sbuf("g_sb", [128, 16], F32); bg = Buf()
        rf_sb = S.sbuf("rf_sb", [128, 2, NT], F32); brf = Buf()
        rt_sb = S.sbuf("rt_sb", [128, 2, 8, 256], F32); brt = Buf()
        ones = S.sbuf("ones", [128, 128], BF16); bones = Buf()
        epst = S.sbuf("epst", [128, 1], F32); beps = Buf()
        sq = [S.sbuf("sq%d" % i, [128, NT], BF16) for i in range(2)]; bsq = [Buf(), Buf()]
        rstd = S.sbuf("rstd", [128, NT], F32); brstd = Buf()
        pan = [S.sbuf("pan%d" % i, [128, 16, 512], BF16) for i in range(2)]; bpan = [Buf(), Buf()]
        stg32 = [S.sbuf("stg32_%d" % i, [128, NT], F32) for i in range(2)]; bstg32 = [Buf(), Buf()]
        stg16 = [S.sbuf("stg16_%d" % i, [128, NT], BF16) for i in range(2)]; bstg16 = [Buf(), Buf()]
        stgT = [S.sbuf("stgT%d" % i, [128, 512], BF16) for i in range(3)]; bstgT = [Buf() for _ in range(3)]
        tmpA = S.sbuf("tmpA", [128, 512], F32); btA = Buf()
        tmpB = S.sbuf("tmpB", [128, 512], F32); btB = Buf()

        S.op("pool", lambda e: e.memset(ones[:], 1.0), writes=[bones])
        S.op("pool", lambda e: e.memset(epst[:], EPS), writes=[beps])
        S.dma("sp", g_sb[:], g1, writes=[bg])
        for ch in range(16):
            S.dma("sp", x_sb[:, ch, :], xT[ch * 128:(ch + 1) * 128, :], writes=[bx[ch]])
        S.dma("sp", rf_sb[:], rotf, writes=[brf])
        S.dma("sp", rt_sb[:], rott, writes=[brt])

        XQ, XK = IN_W, IN_W + 256
        panels = [
            (0, 512, [(j * 128, 128, PT, PT_DQ + j * 128, 0.125, 'bf') for j in range(4)], []),
            (512, 512, [(j * 128, 128, PT, PT_DQ + 512 + j * 128, 0.125, 'bf') for j in range(4)], []),
            (1024, 512, [(j * 128, 128, PT, PT_DK + j * 128, 1.0, 'bf') for j in range(4)], []),
            (1536, 512, [(j * 128, 128, PT, PT_DK + 512 + j * 128, 1.0, 'bf') for j in range(4)], []),
            (2048, 512, [], [(0, 512, PV_DV, 1.0, None)]),
            (2560, 512, [], [(0, 512, PV_DV + 512, 1.0, None)]),
            (3072, 512, [(0, 128, PT, PT_GQ, 0.125, 'bf'), (128, 128, PT, PT_GQ + 128, 0.125, 'bf'),
                         (256, 128, PT, PT_GK, 1.0, 'bf'), (384, 128, PT, PT_GK + 128, 1.0, 'bf')],
             [(256, 256, PV_GK, 1.0, None)]),
            (3584, 512, [], [(0, 512, PV_GV, 1.0, None)]),
            (4096, 512, [(j * 128, 128, PG, j * 128, 1.0, 'f32') for j in range(4)], []),
            (4608, 32, [(0, 32, PT, PT_GA, 1.0, 'bf')], []),
            ((4640, XQ), 512, [(0, 128, PT, PT_RQ, 1.0, ('rot', 256)), (128, 128, PT, PT_RQ + 128, 1.0, ('rot', 384))], []),
            ((4896, XK), 512, [(0, 128, PT, PT_RK, 0.125, ('rot', 256)), (128, 128, PT, PT_RK + 128, 0.125, ('rot', 384))],
             [(0, 256, PV_RK, 0.125, 256)]),
            (5152, 512, [], [(0, 512, PV_RV, 1.0, None)]),
            (5664, 512, [(j * 128, 128, PG, 512 + j * 128, 1.0, 'f32') for j in range(4)], []),
        ]

        def load_panel(pi):
            c0, ncols, _, _ = panels[pi]
            s = pi % 2
            if isinstance(c0, tuple):
                S.dma("pool", pan[s][:, :, 0:256], _chunked(w[:, c0[0]:c0[0] + 256]), writes=[bpan[s]])
                S.dma("pool", pan[s][:, :, 256:512], _chunked(w[:, c0[1]:c0[1] + 256]), writes=[bpan[s]])
            else:
                S.dma("pool", pan[s][:, :, 0:ncols], _chunked(w[:, c0:c0 + ncols]), writes=[bpan[s]])

        load_panel(0)
        _rms_rstd(S, nc, x_sb, bx, NT, [(0, 512), (512, 1024)], ones, bones, sq, bsq, [P[6], P[7]], [bP[6], bP[7]],
                  rstd, brstd, epst, beps, 1.0 / D)
        for ch in range(16):
            S.op("dve", lambda e: e.scalar_tensor_tensor(out=h_sb[:, ch, :], in0=x_sb[:, ch, :], scalar=g_sb[:, ch:ch + 1],
                                                         in1=rstd[:], op0=ALU.mult, op1=ALU.mult),
                 reads=[bx[ch], bg, brstd], writes=[bh[ch]])

        rot = [0]
        evc = [0]

        def next_ps():
            i = rot[0] % 6
            rot[0] += 1
            return P[i], bP[i]

        def evac(dst_ap, src_ap, scale, reads, writes):
            if evc[0] % 2 == 0:
                S.op("act", lambda e: e.activation(dst_ap, src_ap, AF.Copy, scale=float(scale)), reads=reads, writes=writes)
            else:
                S.op("dve", lambda e: e.tensor_scalar(out=dst_ap, in0=src_ap, scalar1=float(scale), scalar2=None, op0=ALU.mult),
                     reads=reads, writes=writes)
            evc[0] += 1

        n32 = n16 = nT = 0
        for pi, (c0, ncols, fm_jobs, tm_jobs) in enumerate(panels):
            s = pi % 2
            if pi + 1 < len(panels):
                load_panel(pi + 1)
            pn = pan[s]
            for (off, M, dst, row0, scale, kind) in fm_jobs:
                if kind == 'f32':
                    stg, bstg = stg32[n32 % 2], bstg32[n32 % 2]; n32 += 1
                else:
                    stg, bstg = stg16[n16 % 2], bstg16[n16 % 2]; n16 += 1
                for th in range(2):
                    ps, bps = next_ps()
                    for ch in range(16):
                        S.op("pe", lambda e: e.matmul(ps[0:M, :], pn[:, ch, off:off + M], h_sb[:, ch, th * 512:(th + 1) * 512],
                                                      start=(ch == 0), stop=(ch == 15)),
                             reads=[bpan[s], bh[ch]], writes=[bps])
                    if isinstance(kind, tuple):
                        soff = kind[1]
                        ps2, bps2 = next_ps()
                        for ch in range(16):
                            S.op("pe", lambda e: e.matmul(ps2[0:M, :], pn[:, ch, soff:soff + M], h_sb[:, ch, th * 512:(th + 1) * 512],
                                                          start=(ch == 0), stop=(ch == 15)),
                                 reads=[bpan[s], bh[ch]], writes=[bps2])
                        S.op("dve", lambda e: e.tensor_tensor(out=tmpA[:], in0=ps[:], in1=rf_sb[:, 0, th * 512:(th + 1) * 512], op=ALU.mult),
                             reads=[bps, brf], writes=[btA])
                        S.op("dve", lambda e: e.tensor_tensor(out=tmpB[:], in0=ps2[:], in1=rf_sb[:, 1, th * 512:(th + 1) * 512], op=ALU.mult),
                             reads=[bps2, brf], writes=[btB])
                        S.op("dve", lambda e: e.tensor_tensor(out=tmpA[:], in0=tmpA[:], in1=tmpB[:], op=ALU.add),
                             reads=[btA, btB], writes=[btA])
                        S.op("dve", lambda e: e.tensor_scalar(out=stg[:, th * 512:(th + 1) * 512], in0=tmpA[:], scalar1=float(scale),
                                                              scalar2=None, op0=ALU.mult),
                             reads=[btA], writes=[bstg])
                    else:
                        evac(stg[0:M, th * 512:(th + 1) * 512], ps[0:M, :], scale, [bps], [bstg])
                S.dma("sp", dst[row0:row0 + M, :], stg[0:M, :], reads=[bstg], owner=bstg)
            for (off, ncl, pv0, scale, soff) in tm_jobs:
                for tt in range(8):
                    ps, bps = next_ps()
                    for ch in range(16):
                        S.op("pe", lambda e: e.matmul(ps[:, 0:ncl], h_sb[:, ch, tt * 128:(tt + 1) * 128], pn[:, ch, off:off + ncl],
                                                      start=(ch == 0), stop=(ch == 15)),
                             reads=[bpan[s], bh[ch]], writes=[bps])
                    stt, bstt = stgT[nT % 3], bstgT[nT % 3]; nT += 1
                    if soff is not None:
                        ps2, bps2 = next_ps()
                        for ch in range(16):
                            S.op("pe", lambda e: e.matmul(ps2[:, 0:ncl], h_sb[:, ch, tt * 128:(tt + 1) * 128], pn[:, ch, soff:soff + ncl],
                                                          start=(ch == 0), stop=(ch == 15)),
                                 reads=[bpan[s], bh[ch]], writes=[bps2])
                        S.op("dve", lambda e: e.tensor_tensor(out=tmpA[:, 0:ncl], in0=ps[:, 0:ncl], in1=rt_sb[:, 0, tt, :], op=ALU.mult),
                             reads=[bps, brt], writes=[btA])
                        S.op("dve", lambda e: e.tensor_tensor(out=tmpB[:, 0:ncl], in0=ps2[:, 0:ncl], in1=rt_sb[:, 1, tt, :], op=ALU.mult),
                             reads=[bps2, brt], writes=[btB])
                        S.op("dve", lambda e: e.tensor_tensor(out=tmpA[:, 0:ncl], in0=tmpA[:, 0:ncl], in1=tmpB[:, 0:ncl], op=ALU.add),
                             reads=[btA, btB], writes=[btA])
                        S.op("dve", lambda e: e.tensor_scalar(out=stt[:, 0:ncl], in0=tmpA[:, 0:ncl], scalar1=float(scale), scalar2=None, op0=ALU.mult),
                             reads=[btA], writes=[bstt])
                    else:
                        evac(stt[:, 0:ncl], ps[:, 0:ncl], scale, [bps], [bstt])
                    S.dma("sp", PV[tt * 128:(tt + 1) * 128, pv0:pv0 + ncl], stt[:, 0:ncl], reads=[bstt], owner=bstt)


def emit_C(S, P, bP, Xin, Md, q, wo, g2, wup, cw, wdn, gF, xo, final):
    nc = S.nc
    PIECES = [(0, 512), (512, 1024), (1024, NTH)]
    with S.phase():
        x_sb = S.sbuf("x_sb", [128, 16, NTH], F32); bx = [Buf() for _ in range(16)]
        m_sb = S.sbuf("m_sb", [128, 16, NTH], BF16); bm = [Buf() for _ in range(16)]
        act_sb = S.sbuf("act_sb", [128, 22, NT], BF16); bact = [Buf() for _ in range(22)]
        g_sb = S.sbuf("g_sb", [128, 16], F32); bg = Buf()
        gF_sb = S.sbuf("gF_sb", [128, 16], F32); bgF = Buf()
        cw_sb = S.sbuf("cw_sb", [128, 88, 4], F32); bcw = Buf()
        ones = S.sbuf("ones", [128, 128], BF16); bones = Buf()
        epst = S.sbuf("epst", [128, 1], F32); beps = Buf()
        sq = [S.sbuf("sq%d" % i, [128, NTH], BF16) for i in range(2)]; bsq = [Buf(), Buf()]
        rstd = S.sbuf("rstd", [128, NTH], F32); brstd = Buf()
        pool_ = [S.sbuf("pool%d" % i, [128, 8192], BF16) for i in range(2)]; bpool = [Buf(), Buf()]
        ua = S.sbuf("ua", [128, NTH], F32); bua = Buf()
        ug = S.sbuf("ug", [128, NTH], F32); bug = Buf()
        ya = S.sbuf("ya", [128, NT], F32); bya = Buf()
        yg = S.sbuf("yg", [128, NT], F32); byg = Buf()

        S.op("pool", lambda e: e.memset(ones[:], 1.0), writes=[bones])
        S.op("pool", lambda e: e.memset(epst[:], EPS), writes=[beps])
        S.dma("sp", g_sb[:], g2, writes=[bg])
        S.dma("sp", gF_sb[:], gF, writes=[bgF])
        S.dma("sp", cw_sb[:], cw, writes=[bcw])
        c0 = q * NT
        for ch in range(16):
            S.dma("sp", m_sb[:, ch, 0:NT], Md[ch * 128:(ch + 1) * 128, c0:c0 + NT], writes=[bm[ch]])
        for ch in range(16):
            S.dma("sp", x_sb[:, ch, 0:NT], Xin[ch * 128:(ch + 1) * 128, c0:c0 + NT], writes=[bx[ch]])
        for (col, src) in ((NT, c0 - 1), (NT + 1, c0 + NT)):
            if 0 <= src < T:
                S.dma("sp", m_sb[:, :, col:col + 1], Md[:, src:src + 1].rearrange("(c p) o -> p c o", p=128), writes=bm, allow_slow_non_contiguous=True)
                S.dma("sp", x_sb[:, :, col:col + 1], Xin[:, src:src + 1].rearrange("(c p) o -> p c o", p=128), writes=bx, allow_slow_non_contiguous=True)
            else:
                S.op("pool", lambda e: e.memset(m_sb[:, :, col:col + 1], 0.0), writes=bm)
                S.op("pool", lambda e: e.memset(x_sb[:, :, col:col + 1], 0.0), writes=bx)

        jobs = []
        rot = [0]

        def next_ps(n=7):
            i = rot[0] % n
            rot[0] += 1
            return P[i], bP[i]

        def mk_wo(pi):
            def load(s):
                v = pool_[s][:].rearrange("p (c n) -> p c n", c=16)
                S.dma("pool", v, _chunked(wo[:, pi * 512:(pi + 1) * 512]), writes=[bpool[s]])

            def comp(s):
                v = pool_[s][:].rearrange("p (c n) -> p c n", c=16)
                for j in range(4):
                    fb = pi * 4 + j
                    for (c0, c1) in PIECES:
                        ps, bps = next_ps()
                        for ch in range(16):
                            S.op("pe", lambda e: e.matmul(ps[:, 0:c1 - c0], v[:, ch, j * 128:(j + 1) * 128], m_sb[:, ch, c0:c1],
                                                          start=(ch == 0), stop=(ch == 15)),
                                 reads=[bpool[s], bm[ch]], writes=[bps])
                        S.op("dve", lambda e: e.tensor_tensor(out=x_sb[:, fb, c0:c1], in0=ps[:, 0:c1 - c0], in1=x_sb[:, fb, c0:c1], op=ALU.add),
                             reads=[bps, bx[fb]], writes=[bx[fb]])
            return load, comp

        for pi in range(4):
            jobs.append(mk_wo(pi))

        def norm2(s_unused=None):
            _rms_rstd(S, nc, x_sb, bx, NTH, PIECES, ones, bones, sq, bsq, [P[5], P[6], P[7]], [bP[5], bP[6], bP[7]],
                      rstd, brstd, epst, beps, 1.0 / D)
            for ch in range(16):
                S.op("dve", lambda e: e.scalar_tensor_tensor(out=m_sb[:, ch, :], in0=x_sb[:, ch, :], scalar=g_sb[:, ch:ch + 1],
                                                             in1=rstd[:], op0=ALU.mult, op1=ALU.mult),
                     reads=[bx[ch], bg, brstd], writes=[bm[ch]])

        def mk_up(hf, pp):
            a0 = (hf * 22 + pp * 2) * 128

            def load(s):
                v = pool_[s][:].rearrange("p (c n) -> p c n", c=16)
                S.dma("pool", v[:, :, 0:256], _chunked(wup[:, a0:a0 + 256]), writes=[bpool[s]])
                S.dma("pool", v[:, :, 256:512], _chunked(wup[:, DFF + a0:DFF + a0 + 256]), writes=[bpool[s]])

            def comp(s):
                v = pool_[s][:].rearrange("p (c n) -> p c n", c=16)
                for j in range(2):
                    li = pp * 2 + j
                    blk = hf * 22 + li
                    for (coff, u, bu, y, by, wi) in ((j * 128, ua, bua, ya, bya, blk), (256 + j * 128, ug, bug, yg, byg, 44 + blk)):
                        pss = []
                        for (c0, c1) in PIECES:
                            if c0 == 1024:
                                ps, bps = P[7], bP[7]
                            else:
                                ps, bps = next_ps(7)
                            for ch in range(16):
                                S.op("pe", lambda e: e.matmul(ps[:, 0:c1 - c0], v[:, ch, coff:coff + 128], m_sb[:, ch, c0:c1],
                                                              start=(ch == 0), stop=(ch == 15)),
                                     reads=[bpool[s], bm[ch]], writes=[bps])
                            pss.append((ps, bps))
                        S.op("act", lambda e: e.activation(u[:, 1:513], pss[0][0][:, 0:512], AF.Copy), reads=[pss[0][1]], writes=[bu])
                        S.op("act", lambda e: e.activation(u[:, 513:1025], pss[1][0][:, 0:512], AF.Copy), reads=[pss[1][1]], writes=[bu])
                        S.op("act", lambda e: e.activation(u[:, 0:1], pss[2][0][:, 0:1], AF.Copy), reads=[pss[2][1]], writes=[bu])
                        S.op("act", lambda e: e.activation(u[:, 1025:1026], pss[2][0][:, 1:2], AF.Copy), reads=[pss[2][1]], writes=[bu])
                        S.op("dve", lambda e: e.tensor_scalar(out=y[:], in0=u[:, 1:1025], scalar1=cw_sb[:, wi, 1:2], scalar2=cw_sb[:, wi, 3:4],
                                                              op0=ALU.mult, op1=ALU.add), reads=[bu, bcw], writes=[by])
                        S.op("dve", lambda e: e.scalar_tensor_tensor(out=y[:], in0=u[:, 0:1024], scalar=cw_sb[:, wi, 0:1], in1=y[:],
                                                                     op0=ALU.mult, op1=ALU.add), reads=[bu, bcw, by], writes=[by])
                        S.op("dve", lambda e: e.scalar_tensor_tensor(out=y[:], in0=u[:, 2:1026], scalar=cw_sb[:, wi, 2:3], in1=y[:],
                                                                     op0=ALU.mult, op1=ALU.add), reads=[bu, bcw, by], writes=[by])
                    S.op("act", lambda e: e.activation(yg[:], yg[:], AF.Silu), reads=[byg], writes=[byg])
                    S.op("dve", lambda e: e.tensor_tensor(out=act_sb[:, li, :], in0=ya[:], in1=yg[:], op=ALU.mult),
                         reads=[bya, byg], writes=[bact[li]])
            return load, comp

        def mk_dn(hf, pj):
            def load(s):
                v = pool_[s][:, 0:22 * 256].rearrange("p (c n) -> p c n", c=22)
                src = wdn[hf * 22 * 128:(hf + 1) * 22 * 128, pj * 256:(pj + 1) * 256].rearrange("(c p) n -> p c n", p=128)
                S.dma("pool", v, src, writes=[bpool[s]])

            def comp(s):
                v = pool_[s][:, 0:22 * 256].rearrange("p (c n) -> p c n", c=22)
                for j in range(2):
                    fb = pj * 2 + j
                    for th in range(2):
                        ps, bps = next_ps()
                        for li in range(22):
                            S.op("pe", lambda e: e.matmul(ps[:], v[:, li, j * 128:(j + 1) * 128], act_sb[:, li, th * 512:(th + 1) * 512],
                                                          start=(li == 0), stop=(li == 21)),
                                 reads=[bpool[s], bact[li]], writes=[bps])
                        S.op("dve", lambda e: e.tensor_tensor(out=x_sb[:, fb, th * 512:(th + 1) * 512], in0=ps[:],
                                                              in1=x_sb[:, fb, th * 512:(th + 1) * 512], op=ALU.add),
                             reads=[bps, bx[fb]], writes=[bx[fb]])
            return load, comp

        jobs.append((None, norm2))
        for hf in range(2):
            for pp in range(11):
                jobs.append(mk_up(hf, pp))
            for pj in range(8):
                jobs.append(mk_dn(hf, pj))

        loads = [(i, j[0]) for i, j in enumerate(jobs) if j[0] is not None]
        slot_of = {}
        for k, (i, _) in enumerate(loads):
            slot_of[i] = k % 2
        nxt = 0
        loads[0][1](slot_of[loads[0][0]]); nxt = 1
        for i, (ld, comp) in enumerate(jobs):
            if ld is not None:
                if nxt < len(loads):
                    loads[nxt][1](slot_of[loads[nxt][0]]); nxt += 1
                comp(slot_of[i])
            else:
                comp()

        if final:
            _rms_rstd(S, nc, x_sb, bx, NT, [(0, 512), (512, 1024)], ones, bones, sq, bsq, [P[5], P[6]], [bP[5], bP[6]],
                      rstd, brstd, epst, beps, 1.0 / D)
            for ch in range(16):
                S.op("dve", lambda e: e.scalar_tensor_tensor(out=x_sb[:, ch, 0:NT], in0=x_sb[:, ch, 0:NT], scalar=gF_sb[:, ch:ch + 1],
                                                             in1=rstd[:, 0:NT], op0=ALU.mult, op1=ALU.mult),
                     reads=[bx[ch], bgF, brstd], writes=[bx[ch]])
        for ch in range(16):
            S.dma("sp", xo[ch * 128:(ch + 1) * 128, :], x_sb[:, ch, 0:NT], reads=[bx[ch]], owner=bx[ch])


def _vec16(v):
    return np.ascontiguousarray(np.asarray(v, np.float32).reshape(16, 128).T)


def _rot_tables():
    half = 32
    inv = (1.0 / (10000.0 ** np.linspace(0.0, 1.0, half, dtype=np.float32))).astype(np.float32)
    ang = np.arange(T, dtype=np.float32)[:, None] * inv[None, :]
    return np.cos(ang).astype(np.float32), np.sin(ang).astype(np.float32)


def _rot_core_tables(cos, sin, c):
    pos = (c % 4) * NT + np.arange(NT)
    r = np.arange(128)
    sgn_r = np.where(r % 64 < 32, -1.0, 1.0).astype(np.float32)
    rotf = np.empty((128, 2, NT), np.float32)
    rotf[:, 0, :] = cos[pos][:, r % 32].T
    rotf[:, 1, :] = (sin[pos][:, r % 32] * sgn_r[None, :]).T
    col = np.arange(256)
    sgn_c = np.where(col % 64 < 32, -1.0, 1.0).astype(np.float32)
    ct = cos[pos][:, col % 32].reshape(8, 128, 256).transpose(1, 0, 2)
    stb = (sin[pos][:, col % 32] * sgn_c[None, :]).reshape(8, 128, 256).transpose(1, 0, 2)
    rott = np.ascontiguousarray(np.stack([ct, stb], axis=1))
    return rotf, rott


def _w_ext(w_in_l):
    def swapped(c0):
        idx = np.arange(256)
        src = c0 + (idx // 64) * 64 + ((idx % 64) + 32) % 64
        return w_in_l[:, src]
    return np.ascontiguousarray(np.concatenate([w_in_l, swapped(4640), swapped(4896)], axis=1))


NEAR_D = (-128, 0, 128, 256, 384, 512)


def emit_B(S, P, bP, PTd, PGd, PVd, Md, btile, cb, lamp, lcst, gains, gw, rdec, cmask):
    nc = S.nc
    NCH = T // 128
    with S.phase():
        ones = S.sbuf("ones", [128, 128], BF16); bones = Buf()
        onesF = S.sbuf("onesF", [128, 512], F32); bonesF = Buf()
        epst = S.sbuf("epst", [128, 1], F32); beps = Buf()
        one1 = S.sbuf("one1", [128, 1], F32); bone1 = Buf()
        cm = S.sbuf("cm", [128, 4, 128], F32); bcm = Buf()
        M4 = S.sbuf("M4", [128, 2, 512], F32); bM4 = Buf()
        cb_sb = S.sbuf("cb_sb", [128, 2], F32); bcb = Buf()
        lamp_sb = S.sbuf("lamp_sb", [128, 256], F32); blamp = Buf()
        lc_sb = S.sbuf("lc_sb", [128, 2], F32); blc = Buf()
        gn_sb = S.sbuf("gn_sb", [128, 16], F32); bgn = Buf()
        rd_sb = S.sbuf("rd_sb", [128, 2], F32); brd = Buf()
        sm = S.sbuf("sm", [128, 16], F32); bsm = Buf()
        tmp64 = S.sbuf("tmp64", [128, 128], F32); bt64 = Buf()
        S.op("pool", lambda e: e.memset(ones[:], 1.0), writes=[bones])
        S.op("pool", lambda e: e.memset(onesF[:], 1.0), writes=[bonesF])
        S.op("pool", lambda e: e.memset(epst[:], EPS), writes=[beps])
        S.op("pool", lambda e: e.memset(one1[:], 1.0), writes=[bone1])
        S.dma("sp", cm[:], cmask, writes=[bcm])
        S.dma("sp", lamp_sb[:], lamp, writes=[blamp])
        S.dma("sp", lc_sb[:], lcst, writes=[blc])
        S.dma("sp", gn_sb[:], gains, writes=[bgn])
        for k in range(4):
            S.op("dve", lambda e: e.tensor_copy(M4[:, 0, k * 128:(k + 1) * 128], cm[:, 0, :]), reads=[bcm], writes=[bM4])
            S.op("dve", lambda e: e.tensor_copy(M4[:, 1, k * 128:(k + 1) * 128], cm[:, 1, :]), reads=[bcm], writes=[bM4])

        o_sb = S.sbuf("o_sb", [128, 512], F32); bo = Buf()
        sq_sb = S.sbuf("sq_sb", [128, 512], BF16); bsq = Buf()
        rs_sb = S.sbuf("rs_sb", [128, 512], F32); brs = Buf()
        stg = [S.sbuf("stg%d" % i, [128, 512], BF16) for i in range(2)]; bstg = [Buf(), Buf()]
        nst = [0]

        def finalize(src_ap, src_bufs, pss, bpss, scal_ap, scal_bufs, gate_ap, gate_bufs, dst_ap):
            S.op("act", lambda e: e.activation(o_sb[:], src_ap, AF.Copy), reads=src_bufs, writes=[bo])
            S.op("act", lambda e: e.activation(sq_sb[:], src_ap, AF.Square), reads=src_bufs, writes=[bsq])
            S.op("pe", lambda e: e.matmul(pss[:], ones[:], sq_sb[:], start=True, stop=True), reads=[bones, bsq], writes=[bpss])
            S.op("act", lambda e: e.activation(rs_sb[:], pss[:], AF.Ln, bias=epst[:, 0:1], scale=1.0 / 128), reads=[bpss, beps], writes=[brs])
            S.op("act", lambda e: e.activation(rs_sb[:], rs_sb[:], AF.Exp, scale=-0.5), reads=[brs], writes=[brs])
            S.op("dve", lambda e: e.tensor_tensor(out=o_sb[:], in0=o_sb[:], in1=rs_sb[:], op=ALU.mult), reads=[bo, brs], writes=[bo])
            sg, bsg = stg[nst[0] % 2], bstg[nst[0] % 2]; nst[0] += 1
            if gate_ap is not None:
                S.op("dve", lambda e: e.scalar_tensor_tensor(out=sg[:], in0=o_sb[:], scalar=scal_ap, in1=gate_ap, op0=ALU.mult, op1=ALU.mult),
                     reads=[bo] + scal_bufs + gate_bufs, writes=[bsg])
            else:
                S.op("dve", lambda e: e.tensor_scalar(out=sg[:], in0=o_sb[:], scalar1=scal_ap, scalar2=None, op0=ALU.mult),
                     reads=[bo] + scal_bufs, writes=[bsg])
            S.dma("sp", dst_ap, sg[:], reads=[bsg], owner=bsg)

        qd = S.sbuf("qd", [128, T], BF16); bqd = Buf()
        kd = S.sbuf("kd", [128, T], BF16); bkd = Buf()
        Qf0 = S.sbuf("Qf0", [128, T], BF16); bQf = Buf()
        Qb0 = S.sbuf("Qb0", [128, T], BF16); bQb = Buf()
        Kp = S.sbuf("Kp", [128, T], BF16); bKp = Buf()
        ktv = S.sbuf("ktv", [128, NCH, 192], BF16); bktv = Buf()
        kend = S.sbuf("kend", [128, NCH, 128], BF16); bkend = Buf()
        KVa = S.sbuf("KVa", [128, NCH, 128], F32); bKVf = Buf(); bKVb = Buf()
        Sbf = S.sbuf("Sbf", [128, NCH, 128], BF16); bSbf = Buf()
        gt_sb = S.sbuf("gt_sb", [128, T], F32); bgt = Buf()
        ga_sb = S.sbuf("ga_sb", [33, T], BF16); bga = Buf()
        gw_sb = S.sbuf("gw_sb", [33, 128], BF16); bgw = Buf()
        la4 = [S.sbuf("la4_%d" % i, [128, 4, 128], F32) for i in range(2)]; bla4 = [Buf(), Buf()]
        la_ret = S.sbuf("la_ret", [128, 4, 128], F32); blar = Buf()
        Epos = S.sbuf("Epos", [128, 512], F32); bEpos = Buf()
        Eneg = S.sbuf("Eneg", [128, 512], F32); bEneg = Buf()
        Eend = S.sbuf("Eend", [128, 4, 128], F32); bEend = Buf()
        Dall = S.sbuf("Dall", [128, NCH], F32); bD = Buf()
        Pf = S.sbuf("Pf", [128, 512], BF16); bPf = Buf()
        Pb = S.sbuf("Pb", [128, 512], BF16); bPb = Buf()

        S.op("pool", lambda e: e.memset(Qf0[:], 0.0), writes=[bQf])
        S.op("pool", lambda e: e.memset(Qb0[:], 0.0), writes=[bQb])
        S.op("pool", lambda e: e.memset(ga_sb[:], 1.0), writes=[bga])
        S.dma("sp", ga_sb[0:32, :], PTd[PT_GA:PT_GA + 32, :], writes=[bga])

        for kind, hg in [(kk_, hh_) for kk_ in range(2) for hh_ in range(4)]:
            qrow = (PT_GQ if kind == 0 else PT_RQ) + hg * 64
            krow = (PT_GK if kind == 0 else PT_RK) + hg * 64
            kcol = (PV_GK if kind == 0 else PV_RK) + hg * 64
            vcol = (PV_GV if kind == 0 else PV_RV) + hg * 128
            grow = (0 if kind == 0 else 512) + hg * 128
            for half in range(2):
                S.dma("sp", qd[half * 64:(half + 1) * 64, :], PTd[qrow:qrow + 64, :], writes=[bqd])
                S.dma("sp", kd[half * 64:(half + 1) * 64, :], PTd[krow:krow + 64, :], writes=[bkd])
            S.dma("sp", ktv[:, :, 0:64], PVd[:, kcol:kcol + 64].rearrange("(n p) c -> p n c", p=128), writes=[bktv])
            S.dma("sp", ktv[:, :, 64:192], PVd[:, vcol:vcol + 128].rearrange("(n p) c -> p n c", p=128), writes=[bktv])
            S.dma("sp", gt_sb[:], PGd[grow:grow + 128, :], writes=[bgt])
            if kind == 0:
                S.dma("pool", gw_sb[:], gw[hg], writes=[bgw])
            else:
                S.dma("sp", rd_sb[:], rdec[hg], writes=[brd])
                S.op("act", lambda e: e.activation(sm[:, 0:2], rd_sb[:], AF.Exp, scale=-1.0), reads=[brd], writes=[bsm])
                S.op("act", lambda e: e.activation(sm[:, 0:2], sm[:, 0:2], AF.Ln, bias=one1[:, 0:1]), reads=[bsm, bone1], writes=[bsm])
                S.op("dve", lambda e: e.tensor_scalar(out=sm[:, 0:2], in0=sm[:, 0:2], scalar1=-1.0, scalar2=None, op0=ALU.mult), reads=[bsm], writes=[bsm])
                for k in range(4):
                    S.op("dve", lambda e: e.tensor_scalar(out=la_ret[:, k, 0:64], in0=onesF[:, 0:64], scalar1=sm[:, 0:1], scalar2=None, op0=ALU.mult),
                         reads=[bonesF, bsm], writes=[blar])
                    S.op("dve", lambda e: e.tensor_scalar(out=la_ret[:, k, 64:128], in0=onesF[:, 0:64], scalar1=sm[:, 1:2], scalar2=None, op0=ALU.mult),
                         reads=[bonesF, bsm], writes=[blar])
            S.op("act", lambda e: e.activation(gt_sb[:], gt_sb[:], AF.Silu), reads=[bgt], writes=[bgt])
            for g in range(NCH // 4):
                cols = slice(g * 512, (g + 1) * 512)
                if kind == 0:
                    la, bla = la4[g % 2], bla4[g % 2]
                    zps, bz = P[0], bP[0]
                    for k in range(4):
                        n = g * 4 + k
                        S.op("pe", lambda e: e.matmul(zps[:, k * 128:(k + 1) * 128], ga_sb[:, n * 128:(n + 1) * 128], gw_sb[:], start=True, stop=True),
                             reads=[bga, bgw], writes=[bz])
                    lav = la[:].rearrange("p k c -> p (k c)")
                    S.op("act", lambda e: e.activation(lav, zps[:], AF.Exp, scale=-1.0), reads=[bz], writes=[bla])
                    S.op("act", lambda e: e.activation(lav, lav, AF.Ln, bias=one1[:, 0:1]), reads=[bla, bone1], writes=[bla])
                    S.op("dve", lambda e: e.tensor_scalar(out=lav, in0=lav, scalar1=-1.0 / 16.0, scalar2=None, op0=ALU.mult), reads=[bla], writes=[bla])
                else:
                    la, bla = la_ret, blar
                o5 = 4 * (g % 2)
                psA, bA, psB, bB, psC, bC, psKV, bKV = P[1 + o5], bP[1 + o5], P[2 + o5], bP[2 + o5], P[3 + o5], bP[3 + o5], P[4], bP[4]
                for k in range(4):
                    S.op("pe", lambda e: e.matmul(psA[:, k * 128:(k + 1) * 128], la[:, k, :], cm[:, 0, :], start=True, stop=True),
                         reads=[bla, bcm], writes=[bA])
                    S.op("pe", lambda e: e.matmul(psB[:, k * 128:(k + 1) * 128], la[:, k, :], cm[:, 1, :], start=True, stop=True),
                         reads=[bla, bcm], writes=[bB])
                    S.op("pe", lambda e: e.matmul(psC[:, k * 128:k * 128 + 64], cm[:, 2, :], la[:, k, 0:64], start=True, stop=True),
                         reads=[bla, bcm], writes=[bC])
                    S.op("pe", lambda e: e.matmul(psC[:, k * 128 + 64:(k + 1) * 128], cm[:, 3, :], la[:, k, 64:128], start=True, stop=True),
                         reads=[bla, bcm], writes=[bC])
                S.op("act", lambda e: e.activation(Epos[0:64, :], psA[0:64, :], AF.Exp), reads=[bA], writes=[bEpos])
                S.op("act", lambda e: e.activation(Epos[64:128, :], psB[64:128, :], AF.Exp), reads=[bB], writes=[bEpos])
                S.op("act", lambda e: e.activation(Eneg[0:64, :], psA[0:64, :], AF.Exp, scale=-1.0), reads=[bA], writes=[bEneg])
                S.op("act", lambda e: e.activation(Eneg[64:128, :], psB[64:128, :], AF.Exp, scale=-1.0), reads=[bB], writes=[bEneg])
                S.op("act", lambda e: e.activation(Eend[:].rearrange("p k c -> p (k c)"), psC[:], AF.Exp), reads=[bC], writes=[bEend])
                S.op("dve", lambda e: e.tensor_tensor(out=Qf0[0:64, cols], in0=qd[0:64, cols], in1=Epos[0:64, :], op=ALU.mult),
                     reads=[bqd, bEpos], writes=[bQf])
                S.op("dve", lambda e: e.tensor_tensor(out=Qb0[64:128, cols], in0=qd[64:128, cols], in1=Epos[64:128, :], op=ALU.mult),
                     reads=[bqd, bEpos], writes=[bQb])
                S.op("dve", lambda e: e.tensor_tensor(out=Kp[:, cols], in0=kd[:, cols], in1=Eneg[:], op=ALU.mult),
                     reads=[bkd, bEneg], writes=[bKp])
                S.op("dve", lambda e: e.tensor_tensor(out=kend[:, g * 4:(g + 1) * 4, 0:64], in0=ktv[:, g * 4:(g + 1) * 4, 0:64], in1=Eend[:, :, 0:64], op=ALU.mult),
                     reads=[bktv, bEend], writes=[bkend])
                S.op("dve", lambda e: e.tensor_tensor(out=kend[:, g * 4:(g + 1) * 4, 64:128], in0=ktv[:, g * 4:(g + 1) * 4, 0:64], in1=Eend[:, :, 64:128], op=ALU.mult),
                     reads=[bktv, bEend], writes=[bkend])
                Ev = Epos[:].rearrange("p (k c) -> p k c", c=128)
                S.op("dve", lambda e: e.tensor_copy(Dall[0:64, g * 4:(g + 1) * 4], Ev[0:64, :, 127]), reads=[bEpos], writes=[bD])
                S.op("dve", lambda e: e.tensor_copy(Dall[64:128, g * 4:(g + 1) * 4], Ev[64:128, :, 0]), reads=[bEpos], writes=[bD])
                for k in range(4):
                    n = g * 4 + k
                    S.op("pe", lambda e: e.matmul(psKV[:, k * 128:(k + 1) * 128], kend[:, n, :], ktv[:, n, 64:192], start=True, stop=True),
                         reads=[bkend, bktv], writes=[bKV])
                S.op("act", lambda e: e.activation(KVa[:, g * 4:(g + 1) * 4, :].rearrange("p k c -> p (k c)"), psKV[:], AF.Copy),
                     reads=[bKV], writes=[bKVf, bKVb])
            for i in range(1, NCH):
                n = i
                S.op("dve", lambda e: e.scalar_tensor_tensor(out=KVa[0:64, n, :], in0=KVa[0:64, n - 1, :], scalar=Dall[0:64, n:n + 1],
                                                             in1=KVa[0:64, n, :], op0=ALU.mult, op1=ALU.add),
                     reads=[bD, bKVf], writes=[bKVf])
                m = NCH - 1 - i
                S.op("dve", lambda e: e.scalar_tensor_tensor(out=KVa[64:128, m, :], in0=KVa[64:128, m + 1, :], scalar=Dall[64:128, m:m + 1],
                                                             in1=KVa[64:128, m, :], op0=ALU.mult, op1=ALU.add),
                     reads=[bD, bKVb], writes=[bKVb])
            S.op("pool", lambda e: e.memset(Sbf[0:64, 0, :], 0.0), writes=[bSbf])
            S.op("pool", lambda e: e.memset(Sbf[64:128, NCH - 1, :], 0.0), writes=[bSbf])
            S.op("dve", lambda e: e.tensor_copy(Sbf[0:64, 1:NCH, :], KVa[0:64, 0:NCH - 1, :]), reads=[bKVf], writes=[bSbf])
            S.op("dve", lambda e: e.tensor_copy(Sbf[64:128, 0:NCH - 1, :], KVa[64:128, 1:NCH, :]), reads=[bKVb], writes=[bSbf])
            for g in range(NCH // 4):
                psSf, bSf, psSb, bSb = P[0], bP[0], P[1], bP[1]
                psO, bO = (P[2], bP[2]) if g % 2 == 0 else (P[3], bP[3])
                for k in range(4):
                    n = g * 4 + k
                    cs = slice(n * 128, (n + 1) * 128)
                    S.op("pe", lambda e: e.matmul(psSf[:, k * 128:(k + 1) * 128], Kp[:, cs], Qf0[:, cs], start=True, stop=True),
                         reads=[bKp, bQf], writes=[bSf])
                    S.op("pe", lambda e: e.matmul(psSb[:, k * 128:(k + 1) * 128], Kp[:, cs], Qb0[:, cs], start=True, stop=True),
                         reads=[bKp, bQb], writes=[bSb])
                S.op("dve", lambda e: e.tensor_tensor(out=Pf[:], in0=psSf[:], in1=M4[:, 0, :], op=ALU.mult), reads=[bSf, bM4], writes=[bPf])
                S.op("dve", lambda e: e.tensor_tensor(out=Pb[:], in0=psSb[:], in1=M4[:, 1, :], op=ALU.mult), reads=[bSb, bM4], writes=[bPb])
                for k in range(4):
                    n = g * 4 + k
                    cs = slice(n * 128, (n + 1) * 128)
                    ko = slice(k * 128, (k + 1) * 128)
                    S.op("pe", lambda e: e.matmul(psO[:, ko], ktv[:, n, 64:192], Pf[:, ko], start=True, stop=False), reads=[bktv, bPf], writes=[bO])
                    S.op("pe", lambda e: e.matmul(psO[:, ko], ktv[:, n, 64:192], Pb[:, ko], start=False, stop=False), reads=[bktv, bPb], writes=[bO])
                    S.op("pe", lambda e: e.matmul(psO[:, ko], Sbf[:, n, :], Qf0[:, cs], start=False, stop=False), reads=[bSbf, bQf], writes=[bO])
                    S.op("pe", lambda e: e.matmul(psO[:, ko], Sbf[:, n, :], Qb0[:, cs], start=False, stop=True), reads=[bSbf, bQb], writes=[bO])
                gcol = 8 + kind * 4 + hg
                mrow = 1024 + kind * 512 + hg * 128
                finalize(psO[:], [bO], P[4], bP[4], gn_sb[:, gcol:gcol + 1], [bgn], gt_sb[:, g * 512:(g + 1) * 512], [bgt],
                         Md[mrow:mrow + 128, g * 512:(g + 1) * 512])

        S.op("dve", lambda e: e.tensor_tensor(out=tmp64[:, 0:64], in0=lamp_sb[:, 0:64], in1=lamp_sb[:, 64:128], op=ALU.mult), reads=[blamp], writes=[bt64])
        S.op("dve", lambda e: e.tensor_tensor(out=tmp64[:, 64:128], in0=lamp_sb[:, 128:192], in1=lamp_sb[:, 192:256], op=ALU.mult), reads=[blamp], writes=[bt64])
        S.op("dve", lambda e: e.reduce_sum(sm[:, 2:3], tmp64[:, 0:64], axis=mybir.AxisListType.X), reads=[bt64], writes=[bsm])
        S.op("dve", lambda e: e.reduce_sum(sm[:, 3:4], tmp64[:, 64:128], axis=mybir.AxisListType.X), reads=[bt64], writes=[bsm])
        S.op("act", lambda e: e.activation(sm[:, 2:4], sm[:, 2:4], AF.Exp), reads=[bsm], writes=[bsm])
        S.op("dve", lambda e: e.tensor_tensor(out=sm[:, 4:5], in0=sm[:, 3:4], in1=sm[:, 2:3], op=ALU.subtract), reads=[bsm], writes=[bsm])
        S.op("dve", lambda e: e.tensor_tensor(out=sm[:, 4:5], in0=sm[:, 4:5], in1=lc_sb[:, 0:1], op=ALU.subtract), reads=[bsm, blc], writes=[bsm])

        EB = S.sbuf("EB", [128, 6, 512], BF16); bEB = Buf()
        bst = S.sbuf("bst", [128, 512], F32); bbst = Buf()
        q1 = S.sbuf("q1", [64, T], BF16); q2 = S.sbuf("q2", [64, T], BF16)
        k1 = S.sbuf("k1", [64, T], BF16); k2 = S.sbuf("k2", [64, T], BF16)
        bq = [Buf() for _ in range(4)]
        vd = S.sbuf("vd", [128, NCH, 128], BF16); bvd = Buf()
        Et2 = S.sbuf("Et2", [128, 1024], BF16); bEt2 = Buf()
        EE = [S.sbuf("EE%d" % i, [128, 1024], BF16) for i in range(2)]; bEE = [Buf(), Buf()]
        r1 = S.sbuf("r1", [128, 512], F32); br1 = Buf()
        r2 = S.sbuf("r2", [128, 512], F32); br2 = Buf()
        od = S.sbuf("od", [128, 512], F32); bod = Buf()
        steps = [(b, qb, kt_) for b in range(8) for qb in range(T // 512) for kt_ in range(NCH)]
        loaded = [-1]
        pO = [(P[4], bP[4]), (P[5], bP[5])]
        pR = [(P[6], bP[6]), (P[7], bP[7])]

        def load_batch(b):
            rows = (PT_DQ + b * 128, PT_DQ + b * 128 + 64, PT_DK + b * 128, PT_DK + b * 128 + 64)
            for i, tl in enumerate((q1, q2, k1, k2)):
                S.dma("sp", tl[:], PTd[rows[i]:rows[i] + 64, :], writes=[bq[i]])
            S.dma("sp", vd[:], PVd[:, PV_DV + b * 128:PV_DV + (b + 1) * 128].rearrange("(n p) c -> p n c", p=128), writes=[bvd])
            S.dma("sp", cb_sb[:], cb[b], writes=[bcb])
            for i in range(6):
                S.dma("sp", bst[:], btile[b, i], writes=[bbst])
                S.op("act", lambda e: e.activation(EB[:, i, :], bst[:], AF.Exp), reads=[bbst], writes=[bEB])
            S.op("dve", lambda e: e.tensor_tensor(out=sm[:, 5:6], in0=gn_sb[:, b:b + 1], in1=lc_sb[:, 1:2], op=ALU.mult), reads=[bgn, blc], writes=[bsm])
            loaded[0] = b

        def scores(i):
            b, qb, kt_ = steps[i]
            if loaded[0] != b:
                load_batch(b)
            qs = slice(qb * 512, (qb + 1) * 512)
            ks = slice(kt_ * 128, (kt_ + 1) * 128)
            d = kt_ * 128 - qb * 512
            j = i % 2
            ps2 = S.SS[:, j * 1024:(j + 1) * 1024]
            bb = [bP[2 * j], bP[2 * j + 1]]
            for si, (kk, qq) in enumerate(((k1, q1), (k2, q2))):
                S.op("pe", lambda e: e.matmul(ps2[:, si * 512:(si + 1) * 512], kk[:, ks], qq[:, qs], start=True, stop=True),
                     reads=[bq[2 + si], bq[si]], writes=[bb[si]])
            Ecur, bEcur = EE[j], bEE[j]
            if d in NEAR_D:
                S.op("act", lambda e: e.activation(Et2[:], ps2, AF.Exp), reads=bb, writes=[bEt2])
                for si in range(2):
                    S.op("dve", lambda e: e.tensor_tensor(out=Ecur[:, si * 512:(si + 1) * 512], in0=Et2[:, si * 512:(si + 1) * 512],
                                                          in1=EB[:, NEAR_D.index(d), :], op=ALU.mult),
                         reads=[bEt2, bEB], writes=[bEcur])
            else:
                col = 0 if d < 0 else 1
                S.op("act", lambda e: e.activation(Ecur[:], ps2, AF.Exp, bias=cb_sb[:, col:col + 1]), reads=bb + [bcb], writes=[bEcur])

        def pv(i):
            b, qb, kt_ = steps[i]
            Ecur, bEcur = EE[i % 2], bEE[i % 2]
            for si in range(2):
                Eh = Ecur[:, si * 512:(si + 1) * 512]
                S.op("pe", lambda e: e.matmul(pO[si][0][:], vd[:, kt_, :], Eh, start=(kt_ == 0), stop=(kt_ == NCH - 1)),
                     reads=[bvd, bEcur], writes=[pO[si][1]])
                if kt_ == 0:
                    S.op("dve", lambda e: e.tensor_copy(pR[si][0][:], Eh), reads=[bEcur], writes=[pR[si][1]])
                else:
                    S.op("dve", lambda e: e.tensor_tensor(out=pR[si][0][:], in0=pR[si][0][:], in1=Eh, op=ALU.add),
                         reads=[bEcur, pR[si][1]], writes=[pR[si][1]])

        def fin(i):
            b, qb, kt_ = steps[i]
            qs = slice(qb * 512, (qb + 1) * 512)
            for si, (r, br) in enumerate(((r1, br1), (r2, br2))):
                S.op("dve", lambda e: e.tensor_copy(r[:], pR[si][0][:]), reads=[pR[si][1]], writes=[br])
                S.op("pe", lambda e: e.matmul(pR[si][0][:], onesF[:, 0:128], r[:], start=True, stop=True),
                     reads=[bonesF, br], writes=[pR[si][1]])
                S.op("dve", lambda e: e.reciprocal(r[:], pR[si][0][:]), reads=[pR[si][1]], writes=[br])
                S.op("dve", lambda e: e.tensor_tensor(out=r[:], in0=pO[si][0][:], in1=r[:], op=ALU.mult), reads=[pO[si][1], br], writes=[br])
            S.op("dve", lambda e: e.scalar_tensor_tensor(out=od[:], in0=r2[:], scalar=sm[:, 4:5], in1=r1[:], op0=ALU.mult, op1=ALU.add),
                 reads=[br1, br2, bsm], writes=[bod])
            finalize(od[:], [bod], pR[0][0], pR[0][1], sm[:, 5:6], [bsm], None, [], Md[b * 128:(b + 1) * 128, qs])

        scores(0)
        for i in range(len(steps)):
            nxt_same = (i + 1 < len(steps)) and steps[i + 1][0] == steps[i][0]
            if nxt_same:
                scores(i + 1)
            pv(i)
            if steps[i][2] == NCH - 1:
                fin(i)
            if (i + 1 < len(steps)) and not nxt_same:
                scores(i + 1)


def _t5_bucket(rel):
    half, max_exact = 16, 8
    ret = np.where(rel > 0, half, 0)
    n = np.abs(rel)
    nf = np.maximum(n, 1).astype(np.float32)
    large = max_exact + (np.log(nf / max_exact) / math.log(128 / max_exact) * (half - max_exact)).astype(np.int32)
    large = np.minimum(large, half - 1)
    return ret + np.where(n < max_exact, n, large)


def _cmask():
    j = np.arange(128)[:, None]
    i = np.arange(128)[None, :]
    m = np.stack([(j <= i), (j >= i), (j > i), (j < i)], axis=1).astype(np.float32)
    return np.ascontiguousarray(m)


def build_F(depth, final):
    nc = bass.Bass("TRN2", target_bir_lowering=False)
    def din(name, shape, dt=F32):
        return nc.dram_tensor(name, shape, dt, kind="ExternalInput").ap()
    x0 = din("x0", [D, T])
    w_ext = din("w_ext", [depth, D, W_EXT])
    wo = din("wo", [depth, D, D])
    wup = din("wup", [depth, D, 2 * DFF])
    wdn = din("wdn", [depth, DFF, D])
    g1 = din("g1", [depth, 128, 16])
    g2 = din("g2", [depth, 128, 16])
    gF = din("gF", [128, 16])
    cw = din("cw", [depth, 128, 88, 4])
    rotf = din("rotf", [128, 2, T])
    rott = din("rott", [128, 2, T // 128, 256])
    btile = din("btile", [8, 6, 128, 512])
    cb = din("cb", [8, 128, 2])
    lamp = din("lamp", [depth, 128, 256])
    lcst = din("lcst", [depth, 128, 2])
    gains = din("gains", [depth, 128, 16])
    gw = din("gw", [depth, 4, 33, 128])
    rdec = din("rdec", [depth, 4, 128, 2])
    cmask = din("cmask", [128, 4, 128])
    y = nc.dram_tensor("y", [D, T], F32, kind="ExternalOutput").ap()
    Xs = [nc.dram_tensor("Xa", [D, T], F32).ap(), nc.dram_tensor("Xb", [D, T], F32).ap()]
    PTd = nc.dram_tensor("PTd", [NFM, T], BF16).ap()
    PGd = nc.dram_tensor("PGd", [1024, T], F32).ap()
    PVd = nc.dram_tensor("PVd", [T, NTM], BF16).ap()
    Md = nc.dram_tensor("Md", [D, T], BF16).ap()
    with contextlib.ExitStack() as st:
        S = Sched(nc, st)
        SS = S.psum("SS", [128, 2048])
        S.SS = SS
        P = [SS[:, i * 512:(i + 1) * 512] for i in range(4)] + [S.psum("P%d" % i, [128, 512]) for i in range(4, 8)]
        bP = [Buf() for _ in range(8)]
        for L in range(depth):
            Xin = x0 if L == 0 else Xs[(L - 1) % 2]
            Xout = y if L == depth - 1 else Xs[L % 2]
            for q in range(4):
                qs = slice(q * NT, (q + 1) * NT)
                emit_A(S, P, bP, Xin[:, qs], g1[L], w_ext[L], rotf[:, :, qs], rott[:, :, q * 8:(q + 1) * 8, :],
                       PTd[:, qs], PGd[:, qs], PVd[qs, :])
            emit_B(S, P, bP, PTd, PGd, PVd, Md, btile, cb, lamp[L], lcst[L], gains[L], gw[L], rdec[L], cmask)
            for q in range(4):
                qs = slice(q * NT, (q + 1) * NT)
                emit_C(S, P, bP, Xin, Md, q, wo[L], g2[L], wup[L], cw[L], wdn[L], gF, Xout[:, qs], final and L == depth - 1)
        S.finish()
        build_F.n_inst = S.n_inst
    return nc


def prep_F(p, depth):
    cos, sin = _rot_tables()
    r = np.arange(128)
    sgn_r = np.where(r % 64 < 32, -1.0, 1.0).astype(np.float32)
    rotf = np.empty((128, 2, T), np.float32)
    rotf[:, 0, :] = cos[:, r % 32].T
    rotf[:, 1, :] = (sin[:, r % 32] * sgn_r[None, :]).T
    col = np.arange(256)
    sgn_c = np.where(col % 64 < 32, -1.0, 1.0).astype(np.float32)
    ct = cos[:, col % 32].reshape(T // 128, 128, 256).transpose(1, 0, 2)
    stb = (sin[:, col % 32] * sgn_c[None, :]).reshape(T // 128, 128, 256).transpose(1, 0, 2)
    rott = np.ascontiguousarray(np.stack([ct, stb], axis=1))
    bucket = _t5_bucket(np.arange(-(T - 1), T))
    kl = np.arange(128)[:, None]
    ql = np.arange(512)[None, :]
    btile = np.stack([np.stack([p['rel_bias'][bucket[d + kl - ql + T - 1], h] for d in NEAR_D]) for h in range(8)]).astype(np.float32)
    cb = np.empty((8, 128, 2), np.float32)
    cb[:, :, 0] = p['rel_bias'][15, :][:, None]
    cb[:, :, 1] = p['rel_bias'][31, :][:, None]
    lamp = np.ascontiguousarray(np.broadcast_to(p['diff_lambda'][:depth].reshape(depth, 1, 256), (depth, 128, 256))).astype(np.float32)
    lcst = np.empty((depth, 128, 2), np.float32)
    for L in range(depth):
        lam_init = 0.8 - 0.6 * math.exp(-0.3 * L)
        lcst[L, :, 0] = lam_init
        lcst[L, :, 1] = 1.0 - lam_init
    gains = np.ascontiguousarray(p['head_gain'][:depth].reshape(depth, 16, 128).transpose(0, 2, 1)).astype(np.float32)
    gw = np.zeros((depth, 4, 33, 128), np.float32)
    rdec = np.empty((depth, 4, 128, 2), np.float32)
    for L in range(depth):
        for hg in range(4):
            gw[L, hg, 0:16, 0:64] = p['gla_gate_w'][L][0][:, hg * 64:(hg + 1) * 64]
            gw[L, hg, 16:32, 64:128] = p['gla_gate_w'][L][1][:, hg * 64:(hg + 1) * 64]
            gw[L, hg, 32, 0:64] = p['gla_gate_b'][L][0][hg * 64:(hg + 1) * 64]
            gw[L, hg, 32, 64:128] = p['gla_gate_b'][L][1][hg * 64:(hg + 1) * 64]
            rdec[L, hg, :, 0] = p['ret_decay_logit'][L][0, hg]
            rdec[L, hg, :, 1] = p['ret_decay_logit'][L][1, hg]
    cw = np.empty((depth, 128, 88, 4), np.float32)
    for L in range(depth):
        for i in range(3):
            cw[L, :, :, i] = p['conv_w'][L][i].reshape(88, 128).T
        cw[L, :, :, 3] = p['conv_b'][L].reshape(88, 128).T
    shared = {
        "w_ext": np.ascontiguousarray(np.stack([_w_ext(p['w_in'][L]) for L in range(depth)])),
        "wo": np.ascontiguousarray(p['w_o'][:depth]), "wup": np.ascontiguousarray(p['w_up'][:depth]),
        "wdn": np.ascontiguousarray(p['w_down'][:depth]),
        "g1": np.stack([_vec16(p['ln1_g'][L]) for L in range(depth)]), "g2": np.stack([_vec16(p['ln2_g'][L]) for L in range(depth)]),
        "gF": _vec16(p['final_g']), "cw": cw, "rotf": rotf, "rott": rott, "btile": btile, "cb": cb, "lamp": lamp, "lcst": lcst,
        "gains": gains, "gw": gw, "rdec": rdec, "cmask": _cmask(),
    }
    x = p['x'].astype(np.float32)
    xT = [np.ascontiguousarray(x[b].T) for b in range(NB)]
    return [dict(shared, x0=xT[c // 4]) for c in range(NCORE)]


_PROG = {}


def run_F(p, depth, final):
    key = (depth, final)
    if key not in _PROG:
        _PROG[key] = build_F(depth, final)
    res = run_bass_kernel_spmd(_PROG[key], prep_F(p, depth), core_ids=list(range(NCORE)))
    return np.stack([np.asarray(res.results[4 * b]["y"]).T for b in range(NB)]).astype(np.float32)


def kernel(**p):
    p = {k: np.asarray(v) for k, v in p.items()}
    return np.ascontiguousarray(run_F(p, DEPTH, True))
```

```python
import contextlib
import math
import numpy as np
import ml_dtypes
import concourse.bass as bass
import concourse.mybir as mybir
from concourse.bass_utils import run_bass_kernel_spmd

F32 = mybir.dt.float32
BF16 = mybir.dt.bfloat16
AF = mybir.ActivationFunctionType
ALU = mybir.AluOpType
NPBF = ml_dtypes.bfloat16

D = 2048
T = 4096
NB = 2
NCORE = 8
NT = 1024
NTH = NT + 2
DEPTH = 4
IN_W = 6176
DFF = 5632
EPS = 1e-6
PT_DQ, PT_DK, PT_GQ, PT_GK, PT_RQ, PT_RK, PT_GA = 0, 1024, 2048, 2304, 2560, 2816, 3072
NFM = 3104
PV_DV, PV_GK, PV_GV, PV_RK, PV_RV = 0, 1024, 1280, 1792, 2048
NTM = 2560
W_EXT = IN_W + 512


class Buf:
    __slots__ = ("name", "last_w", "readers", "dma_sem", "dma_cnt")

    def __init__(self, name=""):
        self.name = name
        self.last_w = None
        self.readers = []
        self.dma_sem = None
        self.dma_cnt = 0


class Sched:
    def __init__(self, nc, stack):
        self.nc = nc
        self.stack = stack
        self.eng = {"pe": nc.tensor, "act": nc.scalar, "dve": nc.vector, "pool": nc.gpsimd, "sp": nc.sync}
        self.sem, self.cnt, self.seen = {}, {}, {}
        for k in self.eng:
            self.sem[k] = stack.enter_context(nc.semaphore("s_" + k))
            self.cnt[k] = 0
            self.seen[k] = {}
        self.nsem = len(self.eng)
        self.all_dma = []
        self.sem_pool = []
        self.gstack = stack
        self.n_inst = 0

    def barrier(self):
        deps = [(b.dma_sem, b.dma_cnt, "dma") for b in self.all_dma]
        for k in self.eng:
            if self.cnt[k] > 0:
                deps.append((self.sem[k], self.cnt[k], k))
        for e in self.eng:
            self._wait(e, deps)

    @contextlib.contextmanager
    def phase(self):
        outer = self.stack
        n0 = len(self.all_dma)
        with contextlib.ExitStack() as ps:
            self.stack = ps
            yield
            self.barrier()
            for b in self.all_dma[n0:]:
                self.sem_pool.append((b.dma_sem, b.dma_cnt))
            del self.all_dma[n0:]
        self.stack = outer

    def sbuf(self, name, shape, dt):
        self.n_tiles = getattr(self, "n_tiles", 0) + 1
        return self.stack.enter_context(self.nc.sbuf_tensor("%s_%d" % (name, self.n_tiles), shape, dt))

    def psum(self, name, shape, dt=F32):
        return self.stack.enter_context(self.nc.psum_tensor(name, shape, dt))

    def _wait(self, e, deps):
        best = {}
        for d in deps:
            if d is None:
                continue
            sem, val, en = d
            if e == "pe" and en == "pe":
                continue
            key = id(sem)
            if key not in best or best[key][1] < val:
                best[key] = (sem, val)
        for key, (sem, val) in best.items():
            if self.seen[e].get(key, 0) >= val:
                continue
            self.eng[e].wait_ge(sem, val)
            self.seen[e][key] = val

    @staticmethod
    def _deps(reads, writes):
        deps = []
        for b in reads:
            deps.append(b.last_w)
        for b in writes:
            deps.append(b.last_w)
            deps.extend(b.readers)
        return deps

    @staticmethod
    def _mark(tok, reads, writes):
        for b in writes:
            b.last_w = tok
            b.readers = []
        for b in reads:
            if b not in writes:
                b.readers = [r for r in b.readers if r[2] != tok[2] or tok[2] == "dma"] + [tok]

    def op(self, e, fn, reads=(), writes=()):
        self._wait(e, self._deps(reads, writes))
        ins = fn(self.eng[e])
        self.cnt[e] += 1
        ins.then_inc(self.sem[e], 1)
        self._mark((self.sem[e], self.cnt[e], e), reads, writes)
        self.n_inst += 1
        return ins

    def dma(self, e, out, in_, reads=(), writes=(), owner=None, **kw):
        self._wait(e, self._deps(reads, writes))
        if owner is None:
            owner = (list(writes) + list(reads))[0]
        if owner.dma_sem is None:
            if self.sem_pool:
                owner.dma_sem, owner.dma_cnt = self.sem_pool.pop()
            else:
                owner.dma_sem = self.gstack.enter_context(self.nc.semaphore("d_%d" % self.nsem))
                self.nsem += 1
            self.all_dma.append(owner)
        owner.dma_cnt += 16
        ins = self.eng[e].dma_start(out=out, in_=in_, **kw)
        ins.then_inc(owner.dma_sem, 16)
        self._mark((owner.dma_sem, owner.dma_cnt, "dma"), reads, writes)
        self.n_inst += 1
        return ins

    def finish(self):
        deps = [(b.dma_sem, b.dma_cnt, "dma") for b in self.all_dma]
        for k in self.eng:
            if k != "sp" and self.cnt[k] > 0:
                deps.append((self.sem[k], self.cnt[k], k))
        self._wait("sp", deps)


def _chunked(ap2d):
    return ap2d.rearrange("(c p) n -> p c n", p=128)


def _rms_rstd(S, nc, x_sb, bx, ncols, pieces, ones, bones, sq, bsq, pss, bpss, rstd, brstd, epst, beps, dscale):
    for ch in range(16):
        s = ch % 2
        S.op("act", lambda e: e.activation(sq[s][:, 0:ncols], x_sb[:, ch, 0:ncols], AF.Square),
             reads=[bx[ch]], writes=[bsq[s]])
        for pi, (c0, c1) in enumerate(pieces):
            S.op("pe", lambda e: e.matmul(pss[pi][:, 0:c1 - c0], ones[:], sq[s][:, c0:c1],
                                          start=(ch == 0), stop=(ch == 15)),
                 reads=[bsq[s], bones], writes=[bpss[pi]])
    for pi, (c0, c1) in enumerate(pieces):
        S.op("act", lambda e: e.activation(rstd[:, c0:c1], pss[pi][:, 0:c1 - c0], AF.Ln, bias=epst[:, 0:1], scale=dscale),
             reads=[bpss[pi], beps], writes=[brstd])
    S.op("act", lambda e: e.activation(rstd[:, 0:ncols], rstd[:, 0:ncols], AF.Exp, scale=-0.5),
         reads=[brstd], writes=[brstd])


def emit_A(S, P, bP, xT, g1, w, rotf, rott, PT, PG, PV):
    nc = S.nc
    with S.phase():
        x_sb = S.sbuf("x_sb", [128, 16, NT], F32); bx = [Buf() for _ in range(16)]
        h_sb = S.sbuf("h_sb", [128, 16, NT], BF16); bh = [Buf() for _ in range(16)]
        g_sb = S.sbuf("g_sb", [128, 16], F32); bg = Buf()
        rf_sb = S.sbuf("rf_sb", [128, 2, NT], F32); brf = Buf()
        rt_sb = S.sbuf("rt_sb", [128, 2, 8, 256], F32); brt = Buf()
        ones = S.sbuf("ones", [128, 128], BF16); bones = Buf()
        epst = S.sbuf("epst", [128, 1], F32); beps = Buf()
        sq = [S.sbuf("sq%d" % i, [128, NT], BF16) for i in range(2)]; bsq = [Buf(), Buf()]
        rstd = S.sbuf("rstd", [128, NT], F32); brstd = Buf()
        pan = [S.sbuf("pan%d" % i, [128, 16, 512], BF16) for i in range(2)]; bpan = [Buf(), Buf()]
        stg32 = [S.sbuf("stg32_%d" % i, [128, NT], F32) for i in range(2)]; bstg32 = [Buf(), Buf()]
        stg16 = [S.sbuf("stg16_%d" % i, [128, NT], BF16) for i in range(2)]; bstg16 = [Buf(), Buf()]
        stgT = [S.sbuf("stgT%d" % i, [128, 512], BF16) for i in range(3)]; bstgT = [Buf() for _ in range(3)]
        tmpA = S.sbuf("tmpA", [128, 512], F32); btA = Buf()
        tmpB = S.sbuf("tmpB", [128, 512], F32); btB = Buf()

        S.op("pool", lambda e: e.memset(ones[:], 1.0), writes=[bones])
        S.op("pool", lambda e: e.memset(epst[:], EPS), writes=[beps])
        S.dma("sp", g_sb[:], g1, writes=[bg])
        for ch in range(16):
            S.dma("sp", x_sb[:, ch, :], xT[ch * 128:(ch + 1) * 128, :], writes=[bx[ch]])
        S.dma("sp", rf_sb[:], rotf, writes=[brf])
        S.dma("sp", rt_sb[:], rott, writes=[brt])

        XQ, XK = IN_W, IN_W + 256
        panels = [
            (0, 512, [(j * 128, 128, PT, PT_DQ + j * 128, 0.125, 'bf') for j in range(4)], []),
            (512, 512, [(j * 128, 128, PT, PT_DQ + 512 + j * 128, 0.125, 'bf') for j in range(4)], []),
            (1024, 512, [(j * 128, 128, PT, PT_DK + j * 128, 1.0, 'bf') for j in range(4)], []),
            (1536, 512, [(j * 128, 128, PT, PT_DK + 512 + j * 128, 1.0, 'bf') for j in range(4)], []),
            (2048, 512, [], [(0, 512, PV_DV, 1.0, None)]),
            (2560, 512, [], [(0, 512, PV_DV + 512, 1.0, None)]),
            (3072, 512, [(0, 128, PT, PT_GQ, 0.125, 'bf'), (128, 128, PT, PT_GQ + 128, 0.125, 'bf'),
                         (256, 128, PT, PT_GK, 1.0, 'bf'), (384, 128, PT, PT_GK + 128, 1.0, 'bf')],
             [(256, 256, PV_GK, 1.0, None)]),
            (3584, 512, [], [(0, 512, PV_GV, 1.0, None)]),
            (4096, 512, [(j * 128, 128, PG, j * 128, 1.0, 'f32') for j in range(4)], []),
            (4608, 32, [(0, 32, PT, PT_GA, 1.0, 'bf')], []),
            ((4640, XQ), 512, [(0, 128, PT, PT_RQ, 1.0, ('rot', 256)), (128, 128, PT, PT_RQ + 128, 1.0, ('rot', 384))], []),
            ((4896, XK), 512, [(0, 128, PT, PT_RK, 0.125, ('rot', 256)), (128, 128, PT, PT_RK + 128, 0.125, ('rot', 384))],
             [(0, 256, PV_RK, 0.125, 256)]),
            (5152, 512, [], [(0, 512, PV_RV, 1.0, None)]),
            (5664, 512, [(j * 128, 128, PG, 512 + j * 128, 1.0, 'f32') for j in range(4)], []),
        ]

        def load_panel(pi):
            c0, ncols, _, _ = panels[pi]
            s = pi % 2
            if isinstance(c0, tuple):
                S.dma("pool", pan[s][:, :, 0:256], _chunked(w[:, c0[0]:c0[0] + 256]), writes=[bpan[s]])
                S.dma("pool", pan[s][:, :, 256:512], _chunked(w[:, c0[1]:c0[1] + 256]), writes=[bpan[s]])
            else:
                S.dma("pool", pan[s][:, :, 0:ncols], _chunked(w[:, c0:c0 + ncols]), writes=[bpan[s]])

        load_panel(0)
        _rms_rstd(S, nc, x_sb, bx, NT, [(0, 512), (512, 1024)], ones, bones, sq, bsq, [P[6], P[7]], [bP[6], bP[7]],
                  rstd, brstd, epst, beps, 1.0 / D)
        for ch in range(16):
            S.op("dve", lambda e: e.scalar_tensor_tensor(out=h_sb[:, ch, :], in0=x_sb[:, ch, :], scalar=g_sb[:, ch:ch + 1],
                                                         in1=rstd[:], op0=ALU.mult, op1=ALU.mult),
                 reads=[bx[ch], bg, brstd], writes=[bh[ch]])

        rot = [0]
        evc = [0]

        def next_ps():
            i = rot[0] % 6
            rot[0] += 1
            return P[i], bP[i]

        def evac(dst_ap, src_ap, scale, reads, writes):
            if evc[0] % 2 == 0:
                S.op("act", lambda e: e.activation(dst_ap, src_ap, AF.Copy, scale=float(scale)), reads=reads, writes=writes)
            else:
                S.op("dve", lambda e: e.tensor_scalar(out=dst_ap, in0=src_ap, scalar1=float(scale), scalar2=None, op0=ALU.mult),
                     reads=reads, writes=writes)
            evc[0] += 1

        n32 = n16 = nT = 0
        for pi, (c0, ncols, fm_jobs, tm_jobs) in enumerate(panels):
            s = pi % 2
            if pi + 1 < len(panels):
                load_panel(pi + 1)
            pn = pan[s]
            for (off, M, dst, row0, scale, kind) in fm_jobs:
                if kind == 'f32':
                    stg, bstg = stg32[n32 % 2], bstg32[n32 % 2]; n32 += 1
                else:
                    stg, bstg = stg16[n16 % 2], bstg16[n16 % 2]; n16 += 1
                for th in range(2):
                    ps, bps = next_ps()
                    for ch in range(16):
                        S.op("pe", lambda e: e.matmul(ps[0:M, :], pn[:, ch, off:off + M], h_sb[:, ch, th * 512:(th + 1) * 512],
                                                      start=(ch == 0), stop=(ch == 15)),
                             reads=[bpan[s], bh[ch]], writes=[bps])
                    if isinstance(kind, tuple):
                        soff = kind[1]
                        ps2, bps2 = next_ps()
                        for ch in range(16):
                            S.op("pe", lambda e: e.matmul(ps2[0:M, :], pn[:, ch, soff:soff + M], h_sb[:, ch, th * 512:(th + 1) * 512],
                                                          start=(ch == 0), stop=(ch == 15)),
                                 reads=[bpan[s], bh[ch]], writes=[bps2])
                        S.op("dve", lambda e: e.tensor_tensor(out=tmpA[:], in0=ps[:], in1=rf_sb[:, 0, th * 512:(th + 1) * 512], op=ALU.mult),
                             reads=[bps, brf], writes=[btA])
                        S.op("dve", lambda e: e.tensor_tensor(out=tmpB[:], in0=ps2[:], in1=rf_sb[:, 1, th * 512:(th + 1) * 512], op=ALU.mult),
                             reads=[bps2, brf], writes=[btB])
                        S.op("dve", lambda e: e.tensor_tensor(out=tmpA[:], in0=tmpA[:], in1=tmpB[:], op=ALU.add),
                             reads=[btA, btB], writes=[btA])
                        S.op("dve", lambda e: e.tensor_scalar(out=stg[:, th * 512:(th + 1) * 512], in0=tmpA[:], scalar1=float(scale),
                                                              scalar2=None, op0=ALU.mult),
                             reads=[btA], writes=[bstg])
                    else:
                        evac(stg[0:M, th * 512:(th + 1) * 512], ps[0:M, :], scale, [bps], [bstg])
                S.dma("sp", dst[row0:row0 + M, :], stg[0:M, :], reads=[bstg], owner=bstg)
            for (off, ncl, pv0, scale, soff) in tm_jobs:
                for tt in range(8):
                    ps, bps = next_ps()
                    for ch in range(16):
                        S.op("pe", lambda e: e.matmul(ps[:, 0:ncl], h_sb[:, ch, tt * 128:(tt + 1) * 128], pn[:, ch, off:off + ncl],
                                                      start=(ch == 0), stop=(ch == 15)),
                             reads=[bpan[s], bh[ch]], writes=[bps])
                    stt, bstt = stgT[nT % 3], bstgT[nT % 3]; nT += 1
                    if soff is not None:
                        ps2, bps2 = next_ps()
                        for ch in range(16):
                            S.op("pe", lambda e: e.matmul(ps2[:, 0:ncl], h_sb[:, ch, tt * 128:(tt + 1) * 128], pn[:, ch, soff:soff + ncl],
                                                          start=(ch == 0), stop=(ch == 15)),
                                 reads=[bpan[s], bh[ch]], writes=[bps2])
                        S.op("dve", lambda e: e.tensor_tensor(out=tmpA[:, 0:ncl], in0=ps[:, 0:ncl], in1=rt_sb[:, 0, tt, :], op=ALU.mult),
                             reads=[bps, brt], writes=[btA])
                        S.op("dve", lambda e: e.tensor_tensor(out=tmpB[:, 0:ncl], in0=ps2[:, 0:ncl], in1=rt_sb[:, 1, tt, :], op=ALU.mult),
                             reads=[bps2, brt], writes=[btB])
                        S.op("dve", lambda e: e.tensor_tensor(out=tmpA[:, 0:ncl], in0=tmpA[:, 0:ncl], in1=tmpB[:, 0:ncl], op=ALU.add),
                             reads=[btA, btB], writes=[btA])
                        S.op("dve", lambda e: e.tensor_scalar(out=stt[:, 0:ncl], in0=tmpA[:, 0:ncl], scalar1=float(scale), scalar2=None, op0=ALU.mult),
                             reads=[btA], writes=[bstt])
                    else:
                        evac(stt[:, 0:ncl], ps[:, 0:ncl], scale, [bps], [bstt])
                    S.dma("sp", PV[tt * 128:(tt + 1) * 128, pv0:pv0 + ncl], stt[:, 0:ncl], reads=[bstt], owner=bstt)


def emit_C(S, P, bP, Xin, Md, q, wo, g2, wup, cw, wdn, gF, xo, final):
    nc = S.nc
    PIECES = [(0, 512), (512, 1024), (1024, NTH)]
    with S.phase():
        x_sb = S.sbuf("x_sb", [128, 16, NTH], F32); bx = [Buf() for _ in range(16)]
        m_sb = S.sbuf("m_sb", [128, 16, NTH], BF16); bm = [Buf() for _ in range(16)]
        act_sb = S.sbuf("act_sb", [128, 22, NT], BF16); bact = [Buf() for _ in range(22)]
        g_sb = S.sbuf("g_sb", [128, 16], F32); bg = Buf()
        gF_sb = S.sbuf("gF_sb", [128, 16], F32); bgF = Buf()
        cw_sb = S.sbuf("cw_sb", [128, 88, 4], F32); bcw = Buf()
        ones = S.sbuf("ones", [128, 128], BF16); bones = Buf()
        epst = S.sbuf("epst", [128, 1], F32); beps = Buf()
        sq = [S.sbuf("sq%d" % i, [128, NTH], BF16) for i in range(2)]; bsq = [Buf(), Buf()]
        rstd = S.sbuf("rstd", [128, NTH], F32); brstd = Buf()
        pool_ = [S.sbuf("pool%d" % i, [128, 8192], BF16) for i in range(2)]; bpool = [Buf(), Buf()]
        ua = S.sbuf("ua", [128, NTH], F32); bua = Buf()
        ug = S.sbuf("ug", [128, NTH], F32); bug = Buf()
        ya = S.sbuf("ya", [128, NT], F32); bya = Buf()
        yg = S.sbuf("yg", [128, NT], F32); byg = Buf()

        S.op("pool", lambda e: e.memset(ones[:], 1.0), writes=[bones])
        S.op("pool", lambda e: e.memset(epst[:], EPS), writes=[beps])
        S.dma("sp", g_sb[:], g2, writes=[bg])
        S.dma("sp", gF_sb[:], gF, writes=[bgF])
        S.dma("sp", cw_sb[:], cw, writes=[bcw])
        c0 = q * NT
        for ch in range(16):
            S.dma("sp", m_sb[:, ch, 0:NT], Md[ch * 128:(ch + 1) * 128, c0:c0 + NT], writes=[bm[ch]])
        for ch in range(16):
            S.dma("sp", x_sb[:, ch, 0:NT], Xin[ch * 128:(ch + 1) * 128, c0:c0 + NT], writes=[bx[ch]])
        for (col, src) in ((NT, c0 - 1), (NT + 1, c0 + NT)):
            if 0 <= src < T:
                S.dma("sp", m_sb[:, :, col:col + 1], Md[:, src:src + 1].rearrange("(c p) o -> p c o", p=128), writes=bm, allow_slow_non_contiguous=True)
                S.dma("sp", x_sb[:, :, col:col + 1], Xin[:, src:src + 1].rearrange("(c p) o -> p c o", p=128), writes=bx, allow_slow_non_contiguous=True)
            else:
                S.op("pool", lambda e: e.memset(m_sb[:, :, col:col + 1], 0.0), writes=bm)
                S.op("pool", lambda e: e.memset(x_sb[:, :, col:col + 1], 0.0), writes=bx)

        jobs = []
        rot = [0]

        def next_ps(n=7):
            i = rot[0] % n
            rot[0] += 1
            return P[i], bP[i]

        def mk_wo(pi):
            def load(s):
                v = pool_[s][:].rearrange("p (c n) -> p c n", c=16)
                S.dma("pool", v, _chunked(wo[:, pi * 512:(pi + 1) * 512]), writes=[bpool[s]])

            def comp(s):
                v = pool_[s][:].rearrange("p (c n) -> p c n", c=16)
                for j in range(4):
                    fb = pi * 4 + j
                    for (c0, c1) in PIECES:
                        ps, bps = next_ps()
                        for ch in range(16):
                            S.op("pe", lambda e: e.matmul(ps[:, 0:c1 - c0], v[:, ch, j * 128:(j + 1) * 128], m_sb[:, ch, c0:c1],
                                                          start=(ch == 0), stop=(ch == 15)),
                                 reads=[bpool[s], bm[ch]], writes=[bps])
                        S.op("dve", lambda e: e.tensor_tensor(out=x_sb[:, fb, c0:c1], in0=ps[:, 0:c1 - c0], in1=x_sb[:, fb, c0:c1], op=ALU.add),
                             reads=[bps, bx[fb]], writes=[bx[fb]])
            return load, comp

        for pi in range(4):
            jobs.append(mk_wo(pi))

        def norm2(s_unused=None):
            _rms_rstd(S, nc, x_sb, bx, NTH, PIECES, ones, bones, sq, bsq, [P[5], P[6], P[7]], [bP[5], bP[6], bP[7]],
                      rstd, brstd, epst, beps, 1.0 / D)
            for ch in range(16):
                S.op("dve", lambda e: e.scalar_tensor_tensor(out=m_sb[:, ch, :], in0=x_sb[:, ch, :], scalar=g_sb[:, ch:ch + 1],
                                                             in1=rstd[:], op0=ALU.mult, op1=ALU.mult),
                     reads=[bx[ch], bg, brstd], writes=[bm[ch]])

        def mk_up(hf, pp):
            a0 = (hf * 22 + pp * 2) * 128

            def load(s):
                v = pool_[s][:].rearrange("p (c n) -> p c n", c=16)
                S.dma("pool", v[:, :, 0:256], _chunked(wup[:, a0:a0 + 256]), writes=[bpool[s]])
                S.dma("pool", v[:, :, 256:512], _chunked(wup[:, DFF + a0:DFF + a0 + 256]), writes=[bpool[s]])

            def comp(s):
                v = pool_[s][:].rearrange("p (c n) -> p c n", c=16)
                for j in range(2):
                    li = pp * 2 + j
                    blk = hf * 22 + li
                    for (coff, u, bu, y, by, wi) in ((j * 128, ua, bua, ya, bya, blk), (256 + j * 128, ug, bug, yg, byg, 44 + blk)):
                        pss = []
                        for (c0, c1) in PIECES:
                            if c0 == 1024:
                                ps, bps = P[7], bP[7]
                            else:
                                ps, bps = next_ps(7)
                            for ch in range(16):
                                S.op("pe", lambda e: e.matmul(ps[:, 0:c1 - c0], v[:, ch, coff:coff + 128], m_sb[:, ch, c0:c1],
                                                              start=(ch == 0), stop=(ch == 15)),
                                     reads=[bpool[s], bm[ch]], writes=[bps])
                            pss.append((ps, bps))
                        S.op("act", lambda e: e.activation(u[:, 1:513], pss[0][0][:, 0:512], AF.Copy), reads=[pss[0][1]], writes=[bu])
                        S.op("act", lambda e: e.activation(u[:, 513:1025], pss[1][0][:, 0:512], AF.Copy), reads=[pss[1][1]], writes=[bu])
                        S.op("act", lambda e: e.activation(u[:, 0:1], pss[2][0][:, 0:1], AF.Copy), reads=[pss[2][1]], writes=[bu])
                        S.op("act", lambda e: e.activation(u[:, 1025:1026], pss[2][0][:, 1:2], AF.Copy), reads=[pss[2][1]], writes=[bu])
                        S.op("dve", lambda e: e.tensor_scalar(out=y[:], in0=u[:, 1:1025], scalar1=cw_sb[:, wi, 1:2], scalar2=cw_sb[:, wi, 3:4],
                                                              op0=ALU.mult, op1=ALU.add), reads=[bu, bcw], writes=[by])
                        S.op("dve", lambda e: e.scalar_tensor_tensor(out=y[:], in0=u[:, 0:1024], scalar=cw_sb[:, wi, 0:1], in1=y[:],
                                                                     op0=ALU.mult, op1=ALU.add), reads=[bu, bcw, by], writes=[by])
                        S.op("dve", lambda e: e.scalar_tensor_tensor(out=y[:], in0=u[:, 2:1026], scalar=cw_sb[:, wi, 2:3], in1=y[:],
                                                                     op0=ALU.mult, op1=ALU.add), reads=[bu, bcw, by], writes=[by])
                    S.op("act", lambda e: e.activation(yg[:], yg[:], AF.Silu), reads=[byg], writes=[byg])
                    S.op("dve", lambda e: e.tensor_tensor(out=act_sb[:, li, :], in0=ya[:], in1=yg[:], op=ALU.mult),
                         reads=[bya, byg], writes=[bact[li]])
            return load, comp

        def mk_dn(hf, pj):
            def load(s):
                v = pool_[s][:, 0:22 * 256].rearrange("p (c n) -> p c n", c=22)
                src = wdn[hf * 22 * 128:(hf + 1) * 22 * 128, pj * 256:(pj + 1) * 256].rearrange("(c p) n -> p c n", p=128)
                S.dma("pool", v, src, writes=[bpool[s]])

            def comp(s):
                v = pool_[s][:, 0:22 * 256].rearrange("p (c n) -> p c n", c=22)
                for j in range(2):
                    fb = pj * 2 + j
                    for th in range(2):
                        ps, bps = next_ps()
                        for li in range(22):
                            S.op("pe", lambda e: e.matmul(ps[:], v[:, li, j * 128:(j + 1) * 128], act_sb[:, li, th * 512:(th + 1) * 512],
                                                          start=(li == 0), stop=(li == 21)),
                                 reads=[bpool[s], bact[li]], writes=[bps])
                        S.op("dve", lambda e: e.tensor_tensor(out=x_sb[:, fb, th * 512:(th + 1) * 512], in0=ps[:],
                                                              in1=x_sb[:, fb, th * 512:(th + 1) * 512], op=ALU.add),
                             reads=[bps, bx[fb]], writes=[bx[fb]])
            return load, comp

        jobs.append((None, norm2))
        for hf in range(2):
            for pp in range(11):
                jobs.append(mk_up(hf, pp))
            for pj in range(8):
                jobs.append(mk_dn(hf, pj))

        loads = [(i, j[0]) for i, j in enumerate(jobs) if j[0] is not None]
        slot_of = {}
        for k, (i, _) in enumerate(loads):
            slot_of[i] = k % 2
        nxt = 0
        loads[0][1](slot_of[loads[0][0]]); nxt = 1
        for i, (ld, comp) in enumerate(jobs):
            if ld is not None:
                if nxt < len(loads):
                    loads[nxt][1](slot_of[loads[nxt][0]]); nxt += 1
                comp(slot_of[i])
            else:
                comp()

        if final:
            _rms_rstd(S, nc, x_sb, bx, NT, [(0, 512), (512, 1024)], ones, bones, sq, bsq, [P[5], P[6]], [bP[5], bP[6]],
                      rstd, brstd, epst, beps, 1.0 / D)
            for ch in range(16):
                S.op("dve", lambda e: e.scalar_tensor_tensor(out=x_sb[:, ch, 0:NT], in0=x_sb[:, ch, 0:NT], scalar=gF_sb[:, ch:ch + 1],
                                                             in1=rstd[:, 0:NT], op0=ALU.mult, op1=ALU.mult),
                     reads=[bx[ch], bgF, brstd], writes=[bx[ch]])
        for ch in range(16):
            S.dma("sp", xo[ch * 128:(ch + 1) * 128, :], x_sb[:, ch, 0:NT], reads=[bx[ch]], owner=bx[ch])


def _vec16(v):
    return np.ascontiguousarray(np.asarray(v, np.float32).reshape(16, 128).T)


def _rot_tables():
    half = 32
    inv = (1.0 / (10000.0 ** np.linspace(0.0, 1.0, half, dtype=np.float32))).astype(np.float32)
    ang = np.arange(T, dtype=np.float32)[:, None] * inv[None, :]
    return np.cos(ang).astype(np.float32), np.sin(ang).astype(np.float32)


def _rot_core_tables(cos, sin, c):
    pos = (c % 4) * NT + np.arange(NT)
    r = np.arange(128)
    sgn_r = np.where(r % 64 < 32, -1.0, 1.0).astype(np.float32)
    rotf = np.empty((128, 2, NT), np.float32)
    rotf[:, 0, :] = cos[pos][:, r % 32].T
    rotf[:, 1, :] = (sin[pos][:, r % 32] * sgn_r[None, :]).T
    col = np.arange(256)
    sgn_c = np.where(col % 64 < 32, -1.0, 1.0).astype(np.float32)
    ct = cos[pos][:, col % 32].reshape(8, 128, 256).transpose(1, 0, 2)
    stb = (sin[pos][:, col % 32] * sgn_c[None, :]).reshape(8, 128, 256).transpose(1, 0, 2)
    rott = np.ascontiguousarray(np.stack([ct, stb], axis=1))
    return rotf, rott


def _w_ext(w_in_l):
    def swapped(c0):
        idx = np.arange(256)
        src = c0 + (idx // 64) * 64 + ((idx % 64) + 32) % 64
        return w_in_l[:, src]
    return np.ascontiguousarray(np.concatenate([w_in_l, swapped(4640), swapped(4896)], axis=1))


NEAR_D = (-128, 0, 128, 256, 384, 512)


def emit_B(S, P, bP, PTd, PGd, PVd, Md, btile, cb, lamp, lcst, gains, gw, rdec, cmask):
    nc = S.nc
    NCH = T // 128
    with S.phase():
        ones = S.sbuf("ones", [128, 128], BF16); bones = Buf()
        onesF = S.sbuf("onesF", [128, 512], F32); bonesF = Buf()
        epst = S.sbuf("epst", [128, 1], F32); beps = Buf()
        one1 = S.sbuf("one1", [128, 1], F32); bone1 = Buf()
        cm = S.sbuf("cm", [128, 4, 128], F32); bcm = Buf()
        M4 = S.sbuf("M4", [128, 2, 512], F32); bM4 = Buf()
        cb_sb = S.sbuf("cb_sb", [128, 2], F32); bcb = Buf()
        lamp_sb = S.sbuf("lamp_sb", [128, 256], F32); blamp = Buf()
        lc_sb = S.sbuf("lc_sb", [128, 2], F32); blc = Buf()
        gn_sb = S.sbuf("gn_sb", [128, 16], F32); bgn = Buf()
        rd_sb = S.sbuf("rd_sb", [128, 2], F32); brd = Buf()
        sm = S.sbuf("sm", [128, 16], F32); bsm = Buf()
        tmp64 = S.sbuf("tmp64", [128, 128], F32); bt64 = Buf()
        S.op("pool", lambda e: e.memset(ones[:], 1.0), writes=[bones])
        S.op("pool", lambda e: e.memset(onesF[:], 1.0), writes=[bonesF])
        S.op("pool", lambda e: e.memset(epst[:], EPS), writes=[beps])
        S.op("pool", lambda e: e.memset(one1[:], 1.0), writes=[bone1])
        S.dma("sp", cm[:], cmask, writes=[bcm])
        S.dma("sp", lamp_sb[:], lamp, writes=[blamp])
        S.dma("sp", lc_sb[:], lcst, writes=[blc])
        S.dma("sp", gn_sb[:], gains, writes=[bgn])
        for k in range(4):
            S.op("dve", lambda e: e.tensor_copy(M4[:, 0, k * 128:(k + 1) * 128], cm[:, 0, :]), reads=[bcm], writes=[bM4])
            S.op("dve", lambda e: e.tensor_copy(M4[:, 1, k * 128:(k + 1) * 128], cm[:, 1, :]), reads=[bcm], writes=[bM4])

        o_sb = S.sbuf("o_sb", [128, 512], F32); bo = Buf()
        sq_sb = S.sbuf("sq_sb", [128, 512], BF16); bsq = Buf()
        rs_sb = S.sbuf("rs_sb", [128, 512], F32); brs = Buf()
        stg = [S.sbuf("stg%d" % i, [128, 512], BF16) for i in range(2)]; bstg = [Buf(), Buf()]
        nst = [0]

        def finalize(src_ap, src_bufs, pss, bpss, scal_ap, scal_bufs, gate_ap, gate_bufs, dst_ap):
            S.op("act", lambda e: e.activation(o_sb[:], src_ap, AF.Copy), reads=src_bufs, writes=[bo])
            S.op("act", lambda e: e.activation(sq_sb[:], src_ap, AF.Square), reads=src_bufs, writes=[bsq])
            S.op("pe", lambda e: e.matmul(pss[:], ones[:], sq_sb[:], start=True, stop=True), reads=[bones, bsq], writes=[bpss])
            S.op("act", lambda e: e.activation(rs_sb[:], pss[:], AF.Ln, bias=epst[:, 0:1], scale=1.0 / 128), reads=[bpss, beps], writes=[brs])
            S.op("act", lambda e: e.activation(rs_sb[:], rs_sb[:], AF.Exp, scale=-0.5), reads=[brs], writes=[brs])
            S.op("dve", lambda e: e.tensor_tensor(out=o_sb[:], in0=o_sb[:], in1=rs_sb[:], op=ALU.mult), reads=[bo, brs], writes=[bo])
            sg, bsg = stg[nst[0] % 2], bstg[nst[0] % 2]; nst[0] += 1
            if gate_ap is not None:
                S.op("dve", lambda e: e.scalar_tensor_tensor(out=sg[:], in0=o_sb[:], scalar=scal_ap, in1=gate_ap, op0=ALU.mult, op1=ALU.mult),
                     reads=[bo] + scal_bufs + gate_bufs, writes=[bsg])
            else:
                S.op("dve", lambda e: e.tensor_scalar(out=sg[:], in0=o_sb[:], scalar1=scal_ap, scalar2=None, op0=ALU.mult),
                     reads=[bo] + scal_bufs, writes=[bsg])
            S.dma("sp", dst_ap, sg[:], reads=[bsg], owner=bsg)

        qd = S.sbuf("qd", [128, T], BF16); bqd = Buf()
        kd = S.sbuf("kd", [128, T], BF16); bkd = Buf()
        Qf0 = S.sbuf("Qf0", [128, T], BF16); bQf = Buf()
        Qb0 = S.sbuf("Qb0", [128, T], BF16); bQb = Buf()
        Kp = S.sbuf("Kp", [128, T], BF16); bKp = Buf()
        ktv = S.sbuf("ktv", [128, NCH, 192], BF16); bktv = Buf()
        kend = S.sbuf("kend", [128, NCH, 128], BF16); bkend = Buf()
        KVa = S.sbuf("KVa", [128, NCH, 128], F32); bKVf = Buf(); bKVb = Buf()
        Sbf = S.sbuf("Sbf", [128, NCH, 128], BF16); bSbf = Buf()
        gt_sb = S.sbuf("gt_sb", [128, T], F32); bgt = Buf()
        ga_sb = S.sbuf("ga_sb", [33, T], BF16); bga = Buf()
        gw_sb = S.sbuf("gw_sb", [33, 128], BF16); bgw = Buf()
        la4 = [S.sbuf("la4_%d" % i, [128, 4, 128], F32) for i in range(2)]; bla4 = [Buf(), Buf()]
        la_ret = S.sbuf("la_ret", [128, 4, 128], F32); blar = Buf()
        Epos = S.sbuf("Epos", [128, 512], F32); bEpos = Buf()
        Eneg = S.sbuf("Eneg", [128, 512], F32); bEneg = Buf()
        Eend = S.sbuf("Eend", [128, 4, 128], F32); bEend = Buf()
        Dall = S.sbuf("Dall", [128, NCH], F32); bD = Buf()
        Pf = S.sbuf("Pf", [128, 512], BF16); bPf = Buf()
        Pb = S.sbuf("Pb", [128, 512], BF16); bPb = Buf()

        S.op("pool", lambda e: e.memset(Qf0[:], 0.0), writes=[bQf])
        S.op("pool", lambda e: e.memset(Qb0[:], 0.0), writes=[bQb])
        S.op("pool", lambda e: e.memset(ga_sb[:], 1.0), writes=[bga])
        S.dma("sp", ga_sb[0:32, :], PTd[PT_GA:PT_GA + 32, :], writes=[bga])

        for kind, hg in [(kk_, hh_) for kk_ in range(2) for hh_ in range(4)]:
            qrow = (PT_GQ if kind == 0 else PT_RQ) + hg * 64
            krow = (PT_GK if kind == 0 else PT_RK) + hg * 64
            kcol = (PV_GK if kind == 0 else PV_RK) + hg * 64
            vcol = (PV_GV if kind == 0 else PV_RV) + hg * 128
            grow = (0 if kind == 0 else 512) + hg * 128
            for half in range(2):
                S.dma("sp", qd[half * 64:(half + 1) * 64, :], PTd[qrow:qrow + 64, :], writes=[bqd])
                S.dma("sp", kd[half * 64:(half + 1) * 64, :], PTd[krow:krow + 64, :], writes=[bkd])
            S.dma("sp", ktv[:, :, 0:64], PVd[:, kcol:kcol + 64].rearrange("(n p) c -> p n c", p=128), writes=[bktv])
            S.dma("sp", ktv[:, :, 64:192], PVd[:, vcol:vcol + 128].rearrange("(n p) c -> p n c", p=128), writes=[bktv])
            S.dma("sp", gt_sb[:], PGd[grow:grow + 128, :], writes=[bgt])
            if kind == 0:
                S.dma("pool", gw_sb[:], gw[hg], writes=[bgw])
            else:
                S.dma("sp", rd_sb[:], rdec[hg], writes=[brd])
                S.op("act", lambda e: e.activation(sm[:, 0:2], rd_sb[:], AF.Exp, scale=-1.0), reads=[brd], writes=[bsm])
                S.op("act", lambda e: e.activation(sm[:, 0:2], sm[:, 0:2], AF.Ln, bias=one1[:, 0:1]), reads=[bsm, bone1], writes=[bsm])
                S.op("dve", lambda e: e.tensor_scalar(out=sm[:, 0:2], in0=sm[:, 0:2], scalar1=-1.0, scalar2=None, op0=ALU.mult), reads=[bsm], writes=[bsm])
                for k in range(4):
                    S.op("dve", lambda e: e.tensor_scalar(out=la_ret[:, k, 0:64], in0=onesF[:, 0:64], scalar1=sm[:, 0:1], scalar2=None, op0=ALU.mult),
                         reads=[bonesF, bsm], writes=[blar])
                    S.op("dve", lambda e: e.tensor_scalar(out=la_ret[:, k, 64:128], in0=onesF[:, 0:64], scalar1=sm[:, 1:2], scalar2=None, op0=ALU.mult),
                         reads=[bonesF, bsm], writes=[blar])
            S.op("act", lambda e: e.activation(gt_sb[:], gt_sb[:], AF.Silu), reads=[bgt], writes=[bgt])
            for g in range(NCH // 4):
                cols = slice(g * 512, (g + 1) * 512)
                if kind == 0:
                    la, bla = la4[g % 2], bla4[g % 2]
                    zps, bz = P[0], bP[0]
                    for k in range(4):
                        n = g * 4 + k
                        S.op("pe", lambda e: e.matmul(zps[:, k * 128:(k + 1) * 128], ga_sb[:, n * 128:(n + 1) * 128], gw_sb[:], start=True, stop=True),
                             reads=[bga, bgw], writes=[bz])
                    lav = la[:].rearrange("p k c -> p (k c)")
                    S.op("act", lambda e: e.activation(lav, zps[:], AF.Exp, scale=-1.0), reads=[bz], writes=[bla])
                    S.op("act", lambda e: e.activation(lav, lav, AF.Ln, bias=one1[:, 0:1]), reads=[bla, bone1], writes=[bla])
                    S.op("dve", lambda e: e.tensor_scalar(out=lav, in0=lav, scalar1=-1.0 / 16.0, scalar2=None, op0=ALU.mult), reads=[bla], writes=[bla])
                else:
                    la, bla = la_ret, blar
                o5 = 4 * (g % 2)
                psA, bA, psB, bB, psC, bC, psKV, bKV = P[1 + o5], bP[1 + o5], P[2 + o5], bP[2 + o5], P[3 + o5], bP[3 + o5], P[4], bP[4]
                for k in range(4):
                    S.op("pe", lambda e: e.matmul(psA[:, k * 128:(k + 1) * 128], la[:, k, :], cm[:, 0, :], start=True, stop=True),
                         reads=[bla, bcm], writes=[bA])
                    S.op("pe", lambda e: e.matmul(psB[:, k * 128:(k + 1) * 128], la[:, k, :], cm[:, 1, :], start=True, stop=True),
                         reads=[bla, bcm], writes=[bB])
                    S.op("pe", lambda e: e.matmul(psC[:, k * 128:k * 128 + 64], cm[:, 2, :], la[:, k, 0:64], start=True, stop=True),
                         reads=[bla, bcm], writes=[bC])
                    S.op("pe", lambda e: e.matmul(psC[:, k * 128 + 64:(k + 1) * 128], cm[:, 3, :], la[:, k, 64:128], start=True, stop=True),
                         reads=[bla, bcm], writes=[bC])
                S.op("act", lambda e: e.activation(Epos[0:64, :], psA[0:64, :], AF.Exp), reads=[bA], writes=[bEpos])
                S.op("act", lambda e: e.activation(Epos[64:128, :], psB[64:128, :], AF.Exp), reads=[bB], writes=[bEpos])
                S.op("act", lambda e: e.activation(Eneg[0:64, :], psA[0:64, :], AF.Exp, scale=-1.0), reads=[bA], writes=[bEneg])
                S.op("act", lambda e: e.activation(Eneg[64:128, :], psB[64:128, :], AF.Exp, scale=-1.0), reads=[bB], writes=[bEneg])
                S.op("act", lambda e: e.activation(Eend[:].rearrange("p k c -> p (k c)"), psC[:], AF.Exp), reads=[bC], writes=[bEend])
                S.op("dve", lambda e: e.tensor_tensor(out=Qf0[0:64, cols], in0=qd[0:64, cols], in1=Epos[0:64, :], op=ALU.mult),
                     reads=[bqd, bEpos], writes=[bQf])
                S.op("dve", lambda e: e.tensor_tensor(out=Qb0[64:128, cols], in0=qd[64:128, cols], in1=Epos[64:128, :], op=ALU.mult),
                     reads=[bqd, bEpos], writes=[bQb])
                S.op("dve", lambda e: e.tensor_tensor(out=Kp[:, cols], in0=kd[:, cols], in1=Eneg[:], op=ALU.mult),
                     reads=[bkd, bEneg], writes=[bKp])
                S.op("dve", lambda e: e.tensor_tensor(out=kend[:, g * 4:(g + 1) * 4, 0:64], in0=ktv[:, g * 4:(g + 1) * 4, 0:64], in1=Eend[:, :, 0:64], op=ALU.mult),
                     reads=[bktv, bEend], writes=[bkend])
                S.op("dve", lambda e: e.tensor_tensor(out=kend[:, g * 4:(g + 1) * 4, 64:128], in0=ktv[:, g * 4:(g + 1) * 4, 0:64], in1=Eend[:, :, 64:128], op=ALU.mult),
                     reads=[bktv, bEend], writes=[bkend])
                Ev = Epos[:].rearrange("p (k c) -> p k c", c=128)
                S.op("dve", lambda e: e.tensor_copy(Dall[0:64, g * 4:(g + 1) * 4], Ev[0:64, :, 127]), reads=[bEpos], writes=[bD])
                S.op("dve", lambda e: e.tensor_copy(Dall[64:128, g * 4:(g + 1) * 4], Ev[64:128, :, 0]), reads=[bEpos], writes=[bD])
                for k in range(4):
                    n = g * 4 + k
                    S.op("pe", lambda e: e.matmul(psKV[:, k * 128:(k + 1) * 128], kend[:, n, :], ktv[:, n, 64:192], start=True, stop=True),
                         reads=[bkend, bktv], writes=[bKV])
                S.op("act", lambda e: e.activation(KVa[:, g * 4:(g + 1) * 4, :].rearrange("p k c -> p (k c)"), psKV[:], AF.Copy),
                     reads=[bKV], writes=[bKVf, bKVb])
            for i in range(1, NCH):
                n = i
                S.op("dve", lambda e: e.scalar_tensor_tensor(out=KVa[0:64, n, :], in0=KVa[0:64, n - 1, :], scalar=Dall[0:64, n:n + 1],
                                                             in1=KVa[0:64, n, :], op0=ALU.mult, op1=ALU.add),
                     reads=[bD, bKVf], writes=[bKVf])
                m = NCH - 1 - i
                S.op("dve", lambda e: e.scalar_tensor_tensor(out=KVa[64:128, m, :], in0=KVa[64:128, m + 1, :], scalar=Dall[64:128, m:m + 1],
                                                             in1=KVa[64:128, m, :], op0=ALU.mult, op1=ALU.add),
                     reads=[bD, bKVb], writes=[bKVb])
            S.op("pool", lambda e: e.memset(Sbf[0:64, 0, :], 0.0), writes=[bSbf])
            S.op("pool", lambda e: e.memset(Sbf[64:128, NCH - 1, :], 0.0), writes=[bSbf])
            S.op("dve", lambda e: e.tensor_copy(Sbf[0:64, 1:NCH, :], KVa[0:64, 0:NCH - 1, :]), reads=[bKVf], writes=[bSbf])
            S.op("dve", lambda e: e.tensor_copy(Sbf[64:128, 0:NCH - 1, :], KVa[64:128, 1:NCH, :]), reads=[bKVb], writes=[bSbf])
            for g in range(NCH // 4):
                psSf, bSf, psSb, bSb = P[0], bP[0], P[1], bP[1]
                psO, bO = (P[2], bP[2]) if g % 2 == 0 else (P[3], bP[3])
                for k in range(4):
                    n = g * 4 + k
                    cs = slice(n * 128, (n + 1) * 128)
                    S.op("pe", lambda e: e.matmul(psSf[:, k * 128:(k + 1) * 128], Kp[:, cs], Qf0[:, cs], start=True, stop=True),
                         reads=[bKp, bQf], writes=[bSf])
                    S.op("pe", lambda e: e.matmul(psSb[:, k * 128:(k + 1) * 128], Kp[:, cs], Qb0[:, cs], start=True, stop=True),
                         reads=[bKp, bQb], writes=[bSb])
                S.op("dve", lambda e: e.tensor_tensor(out=Pf[:], in0=psSf[:], in1=M4[:, 0, :], op=ALU.mult), reads=[bSf, bM4], writes=[bPf])
                S.op("dve", lambda e: e.tensor_tensor(out=Pb[:], in0=psSb[:], in1=M4[:, 1, :], op=ALU.mult), reads=[bSb, bM4], writes=[bPb])
                for k in range(4):
                    n = g * 4 + k
                    cs = slice(n * 128, (n + 1) * 128)
                    ko = slice(k * 128, (k + 1) * 128)
                    S.op("pe", lambda e: e.matmul(psO[:, ko], ktv[:, n, 64:192], Pf[:, ko], start=True, stop=False), reads=[bktv, bPf], writes=[bO])
                    S.op("pe", lambda e: e.matmul(psO[:, ko], ktv[:, n, 64:192], Pb[:, ko], start=False, stop=False), reads=[bktv, bPb], writes=[bO])
                    S.op("pe", lambda e: e.matmul(psO[:, ko], Sbf[:, n, :], Qf0[:, cs], start=False, stop=False), reads=[bSbf, bQf], writes=[bO])
                    S.op("pe", lambda e: e.matmul(psO[:, ko], Sbf[:, n, :], Qb0[:, cs], start=False, stop=True), reads=[bSbf, bQb], writes=[bO])
                gcol = 8 + kind * 4 + hg
                mrow = 1024 + kind * 512 + hg * 128
                finalize(psO[:], [bO], P[4], bP[4], gn_sb[:, gcol:gcol + 1], [bgn], gt_sb[:, g * 512:(g + 1) * 512], [bgt],
                         Md[mrow:mrow + 128, g * 512:(g + 1) * 512])

        S.op("dve", lambda e: e.tensor_tensor(out=tmp64[:, 0:64], in0=lamp_sb[:, 0:64], in1=lamp_sb[:, 64:128], op=ALU.mult), reads=[blamp], writes=[bt64])
        S.op("dve", lambda e: e.tensor_tensor(out=tmp64[:, 64:128], in0=lamp_sb[:, 128:192], in1=lamp_sb[:, 192:256], op=ALU.mult), reads=[blamp], writes=[bt64])
        S.op("dve", lambda e: e.reduce_sum(sm[:, 2:3], tmp64[:, 0:64], axis=mybir.AxisListType.X), reads=[bt64], writes=[bsm])
        S.op("dve", lambda e: e.reduce_sum(sm[:, 3:4], tmp64[:, 64:128], axis=mybir.AxisListType.X), reads=[bt64], writes=[bsm])
        S.op("act", lambda e: e.activation(sm[:, 2:4], sm[:, 2:4], AF.Exp), reads=[bsm], writes=[bsm])
        S.op("dve", lambda e: e.tensor_tensor(out=sm[:, 4:5], in0=sm[:, 3:4], in1=sm[:, 2:3], op=ALU.subtract), reads=[bsm], writes=[bsm])
        S.op("dve", lambda e: e.tensor_tensor(out=sm[:, 4:5], in0=sm[:, 4:5], in1=lc_sb[:, 0:1], op=ALU.subtract), reads=[bsm, blc], writes=[bsm])

        EB = S.sbuf("EB", [128, 6, 512], BF16); bEB = Buf()
        bst = S.sbuf("bst", [128, 512], F32); bbst = Buf()
        q1p = S.sbuf("q1p", [128, T], BF16); q2p = S.sbuf("q2p", [128, T], BF16)
        kT = S.sbuf("kT", [128, T], BF16)
        bq = [Buf() for _ in range(4)]
        S.op("pool", lambda e: e.memset(q1p[64:128, :], 0.0), writes=[bq[0]])
        S.op("pool", lambda e: e.memset(q2p[0:64, :], 0.0), writes=[bq[1]])
        vd = S.sbuf("vd", [128, NCH, 128], BF16); bvd = Buf()
        Et2 = S.sbuf("Et2", [128, 1024], BF16); bEt2 = Buf()
        EE = [S.sbuf("EE%d" % i, [128, 1024], BF16) for i in range(2)]; bEE = [Buf(), Buf()]
        r1 = S.sbuf("r1", [128, 512], F32); br1 = Buf()
        r2 = S.sbuf("r2", [128, 512], F32); br2 = Buf()
        od = S.sbuf("od", [128, 512], F32); bod = Buf()
        steps = [(b, qb, kt_) for b in range(8) for qb in range(T // 512) for kt_ in range(NCH)]
        loaded = [-1]
        pO = [(P[4], bP[4]), (P[5], bP[5])]
        pR = [(P[6], bP[6]), (P[7], bP[7])]

        def load_batch(b):
            r0 = PT_DQ + b * 128
            S.dma("sp", q1p[0:64, :], PTd[r0:r0 + 64, :], writes=[bq[0]])
            S.dma("sp", q2p[64:128, :], PTd[r0 + 64:r0 + 128, :], writes=[bq[1]])
            S.dma("sp", kT[:], PTd[PT_DK + b * 128:PT_DK + (b + 1) * 128, :], writes=[bq[2]])
            S.dma("sp", vd[:], PVd[:, PV_DV + b * 128:PV_DV + (b + 1) * 128].rearrange("(n p) c -> p n c", p=128), writes=[bvd])
            S.dma("sp", cb_sb[:], cb[b], writes=[bcb])
            for i in range(6):
                S.dma("sp", bst[:], btile[b, i], writes=[bbst])
                S.op("act", lambda e: e.activation(EB[:, i, :], bst[:], AF.Exp), reads=[bbst], writes=[bEB])
            S.op("dve", lambda e: e.tensor_tensor(out=sm[:, 5:6], in0=gn_sb[:, b:b + 1], in1=lc_sb[:, 1:2], op=ALU.mult), reads=[bgn, blc], writes=[bsm])
            loaded[0] = b

        def scores(i):
            b, qb, kt_ = steps[i]
            if loaded[0] != b:
                load_batch(b)
            qs = slice(qb * 512, (qb + 1) * 512)
            ks = slice(kt_ * 128, (kt_ + 1) * 128)
            d = kt_ * 128 - qb * 512
            j = i % 2
            ps2 = S.SS[:, j * 1024:(j + 1) * 1024]
            bb = [bP[2 * j], bP[2 * j + 1]]
            for si, qq in enumerate((q1p, q2p)):
                S.op("pe", lambda e: e.matmul(ps2[:, si * 512:(si + 1) * 512], kT[:, ks], qq[:, qs], start=True, stop=True),
                     reads=[bq[2], bq[si]], writes=[bb[si]])
            Ecur, bEcur = EE[j], bEE[j]
            if d in NEAR_D:
                S.op("act", lambda e: e.activation(Et2[:], ps2, AF.Exp), reads=bb, writes=[bEt2])
                for si in range(2):
                    S.op("dve", lambda e: e.tensor_tensor(out=Ecur[:, si * 512:(si + 1) * 512], in0=Et2[:, si * 512:(si + 1) * 512],
                                                          in1=EB[:, NEAR_D.index(d), :], op=ALU.mult),
                         reads=[bEt2, bEB], writes=[bEcur])
            else:
                col = 0 if d < 0 else 1
                S.op("act", lambda e: e.activation(Ecur[:], ps2, AF.Exp, bias=cb_sb[:, col:col + 1]), reads=bb + [bcb], writes=[bEcur])

        def pv(i):
            b, qb, kt_ = steps[i]
            Ecur, bEcur = EE[i % 2], bEE[i % 2]
            for si in range(2):
                Eh = Ecur[:, si * 512:(si + 1) * 512]
                S.op("pe", lambda e: e.matmul(pO[si][0][:], vd[:, kt_, :], Eh, start=(kt_ == 0), stop=(kt_ == NCH - 1)),
                     reads=[bvd, bEcur], writes=[pO[si][1]])
                S.op("pe", lambda e: e.matmul(pR[si][0][:], ones[:], Eh, start=(kt_ == 0), stop=(kt_ == NCH - 1)),
                     reads=[bones, bEcur], writes=[pR[si][1]])

        def fin(i):
            b, qb, kt_ = steps[i]
            qs = slice(qb * 512, (qb + 1) * 512)
            for si, (r, br) in enumerate(((r1, br1), (r2, br2))):
                S.op("dve", lambda e: e.reciprocal(r[:], pR[si][0][:]), reads=[pR[si][1]], writes=[br])
                S.op("dve", lambda e: e.tensor_tensor(out=r[:], in0=pO[si][0][:], in1=r[:], op=ALU.mult), reads=[pO[si][1], br], writes=[br])
            S.op("dve", lambda e: e.scalar_tensor_tensor(out=od[:], in0=r2[:], scalar=sm[:, 4:5], in1=r1[:], op0=ALU.mult, op1=ALU.add),
                 reads=[br1, br2, bsm], writes=[bod])
            finalize(od[:], [bod], pR[0][0], pR[0][1], sm[:, 5:6], [bsm], None, [], Md[b * 128:(b + 1) * 128, qs])

        scores(0)
        for i in range(len(steps)):
            nxt_same = (i + 1 < len(steps)) and steps[i + 1][0] == steps[i][0]
            if nxt_same:
                scores(i + 1)
            pv(i)
            if steps[i][2] == NCH - 1:
                fin(i)
            if (i + 1 < len(steps)) and not nxt_same:
                scores(i + 1)


def _t5_bucket(rel):
    half, max_exact = 16, 8
    ret = np.where(rel > 0, half, 0)
    n = np.abs(rel)
    nf = np.maximum(n, 1).astype(np.float32)
    large = max_exact + (np.log(nf / max_exact) / math.log(128 / max_exact) * (half - max_exact)).astype(np.int32)
    large = np.minimum(large, half - 1)
    return ret + np.where(n < max_exact, n, large)


def _cmask():
    j = np.arange(128)[:, None]
    i = np.arange(128)[None, :]
    m = np.stack([(j <= i), (j >= i), (j > i), (j < i)], axis=1).astype(np.float32)
    return np.ascontiguousarray(m)


def build_F(depth, final):
    nc = bass.Bass("TRN2", target_bir_lowering=False)
    def din(name, shape, dt=F32):
        return nc.dram_tensor(name, shape, dt, kind="ExternalInput").ap()
    x0 = din("x0", [D, T])
    w_ext = din("w_ext", [depth, D, W_EXT])
    wo = din("wo", [depth, D, D])
    wup = din("wup", [depth, D, 2 * DFF])
    wdn = din("wdn", [depth, DFF, D])
    g1 = din("g1", [depth, 128, 16])
    g2 = din("g2", [depth, 128, 16])
    gF = din("gF", [128, 16])
    cw = din("cw", [depth, 128, 88, 4])
    rotf = din("rotf", [128, 2, T])
    rott = din("rott", [128, 2, T // 128, 256])
    btile = din("btile", [8, 6, 128, 512])
    cb = din("cb", [8, 128, 2])
    lamp = din("lamp", [depth, 128, 256])
    lcst = din("lcst", [depth, 128, 2])
    gains = din("gains", [depth, 128, 16])
    gw = din("gw", [depth, 4, 33, 128])
    rdec = din("rdec", [depth, 4, 128, 2])
    cmask = din("cmask", [128, 4, 128])
    y = nc.dram_tensor("y", [D, T], F32, kind="ExternalOutput").ap()
    Xs = [nc.dram_tensor("Xa", [D, T], F32).ap(), nc.dram_tensor("Xb", [D, T], F32).ap()]
    PTd = nc.dram_tensor("PTd", [NFM, T], BF16).ap()
    PGd = nc.dram_tensor("PGd", [1024, T], F32).ap()
    PVd = nc.dram_tensor("PVd", [T, NTM], BF16).ap()
    Md = nc.dram_tensor("Md", [D, T], BF16).ap()
    with contextlib.ExitStack() as st:
        S = Sched(nc, st)
        SS = S.psum("SS", [128, 2048])
        S.SS = SS
        P = [SS[:, i * 512:(i + 1) * 512] for i in range(4)] + [S.psum("P%d" % i, [128, 512]) for i in range(4, 8)]
        bP = [Buf() for _ in range(8)]
        for L in range(depth):
            Xin = x0 if L == 0 else Xs[(L - 1) % 2]
            Xout = y if L == depth - 1 else Xs[L % 2]
            for q in range(4):
                qs = slice(q * NT, (q + 1) * NT)
                emit_A(S, P, bP, Xin[:, qs], g1[L], w_ext[L], rotf[:, :, qs], rott[:, :, q * 8:(q + 1) * 8, :],
                       PTd[:, qs], PGd[:, qs], PVd[qs, :])
            emit_B(S, P, bP, PTd, PGd, PVd, Md, btile, cb, lamp[L], lcst[L], gains[L], gw[L], rdec[L], cmask)
            for q in range(4):
                qs = slice(q * NT, (q + 1) * NT)
                emit_C(S, P, bP, Xin, Md, q, wo[L], g2[L], wup[L], cw[L], wdn[L], gF, Xout[:, qs], final and L == depth - 1)
        S.finish()
        build_F.n_inst = S.n_inst
    return nc


def prep_F(p, depth):
    cos, sin = _rot_tables()
    r = np.arange(128)
    sgn_r = np.where(r % 64 < 32, -1.0, 1.0).astype(np.float32)
    rotf = np.empty((128, 2, T), np.float32)
    rotf[:, 0, :] = cos[:, r % 32].T
    rotf[:, 1, :] = (sin[:, r % 32] * sgn_r[None, :]).T
    col = np.arange(256)
    sgn_c = np.where(col % 64 < 32, -1.0, 1.0).astype(np.float32)
    ct = cos[:, col % 32].reshape(T // 128, 128, 256).transpose(1, 0, 2)
    stb = (sin[:, col % 32] * sgn_c[None, :]).reshape(T // 128, 128, 256).transpose(1, 0, 2)
    rott = np.ascontiguousarray(np.stack([ct, stb], axis=1))
    bucket = _t5_bucket(np.arange(-(T - 1), T))
    kl = np.arange(128)[:, None]
    ql = np.arange(512)[None, :]
    btile = np.stack([np.stack([p['rel_bias'][bucket[d + kl - ql + T - 1], h] for d in NEAR_D]) for h in range(8)]).astype(np.float32)
    cb = np.empty((8, 128, 2), np.float32)
    cb[:, :, 0] = p['rel_bias'][15, :][:, None]
    cb[:, :, 1] = p['rel_bias'][31, :][:, None]
    lamp = np.ascontiguousarray(np.broadcast_to(p['diff_lambda'][:depth].reshape(depth, 1, 256), (depth, 128, 256))).astype(np.float32)
    lcst = np.empty((depth, 128, 2), np.float32)
    for L in range(depth):
        lam_init = 0.8 - 0.6 * math.exp(-0.3 * L)
        lcst[L, :, 0] = lam_init
        lcst[L, :, 1] = 1.0 - lam_init
    gains = np.ascontiguousarray(p['head_gain'][:depth].reshape(depth, 16, 128).transpose(0, 2, 1)).astype(np.float32)
    gw = np.zeros((depth, 4, 33, 128), np.float32)
    rdec = np.empty((depth, 4, 128, 2), np.float32)
    for L in range(depth):
        for hg in range(4):
            gw[L, hg, 0:16, 0:64] = p['gla_gate_w'][L][0][:, hg * 64:(hg + 1) * 64]
            gw[L, hg, 16:32, 64:128] = p['gla_gate_w'][L][1][:, hg * 64:(hg + 1) * 64]
            gw[L, hg, 32, 0:64] = p['gla_gate_b'][L][0][hg * 64:(hg + 1) * 64]
            gw[L, hg, 32, 64:128] = p['gla_gate_b'][L][1][hg * 64:(hg + 1) * 64]
            rdec[L, hg, :, 0] = p['ret_decay_logit'][L][0, hg]
            rdec[L, hg, :, 1] = p['ret_decay_logit'][L][1, hg]
    cw = np.empty((depth, 128, 88, 4), np.float32)
    for L in range(depth):
        for i in range(3):
            cw[L, :, :, i] = p['conv_w'][L][i].reshape(88, 128).T
        cw[L, :, :, 3] = p['conv_b'][L].reshape(88, 128).T
    shared = {
        "w_ext": np.ascontiguousarray(np.stack([_w_ext(p['w_in'][L]) for L in range(depth)])),
        "wo": np.ascontiguousarray(p['w_o'][:depth]), "wup": np.ascontiguousarray(p['w_up'][:depth]),
        "wdn": np.ascontiguousarray(p['w_down'][:depth]),
        "g1": np.stack([_vec16(p['ln1_g'][L]) for L in range(depth)]), "g2": np.stack([_vec16(p['ln2_g'][L]) for L in range(depth)]),
        "gF": _vec16(p['final_g']), "cw": cw, "rotf": rotf, "rott": rott, "btile": btile, "cb": cb, "lamp": lamp, "lcst": lcst,
        "gains": gains, "gw": gw, "rdec": rdec, "cmask": _cmask(),
    }
    x = p['x'].astype(np.float32)
    xT = [np.ascontiguousarray(x[b].T) for b in range(NB)]
    return [dict(shared, x0=xT[c // 4]) for c in range(NCORE)]


_PROG = {}


def run_F(p, depth, final):
    key = (depth, final)
    if key not in _PROG:
        _PROG[key] = build_F(depth, final)
    res = run_bass_kernel_spmd(_PROG[key], prep_F(p, depth), core_ids=list(range(NCORE)))
    return np.stack([np.asarray(res.results[4 * b]["y"]).T for b in range(NB)]).astype(np.float32)


def kernel(**p):
    p = {k: np.asarray(v) for k, v in p.items()}
    return np.ascontiguousarray(run_F(p, DEPTH, True))
```

```python
import contextlib
import math
import numpy as np
import ml_dtypes
import concourse.bass as bass
import concourse.mybir as mybir
from concourse.bass_utils import run_bass_kernel_spmd

F32 = mybir.dt.float32
BF16 = mybir.dt.bfloat16
AF = mybir.ActivationFunctionType
ALU = mybir.AluOpType
NPBF = ml_dtypes.bfloat16

D = 2048
T = 4096
NB = 2
NCORE = 8
NT = 1024
NTH = NT + 2
DEPTH = 4
IN_W = 6176
DFF = 5632
EPS = 1e-6
PT_DQ, PT_DK, PT_GQ, PT_GK, PT_RQ, PT_RK, PT_GA = 0, 1024, 2048, 2304, 2560, 2816, 3072
NFM = 3104
PV_DV, PV_GK, PV_GV, PV_RK, PV_RV = 0, 1024, 1280, 1792, 2048
NTM = 2560
W_EXT = IN_W + 512


class Buf:
    __slots__ = ("name", "last_w", "readers", "dma_sem", "dma_cnt")

    def __init__(self, name=""):
        self.name = name
        self.last_w = None
        self.readers = []
        self.dma_sem = None
        self.dma_cnt = 0


class Sched:
    def __init__(self, nc, stack):
        self.nc = nc
        self.stack = stack
        self.eng = {"pe": nc.tensor, "act": nc.scalar, "dve": nc.vector, "pool": nc.gpsimd, "sp": nc.sync}
        self.sem, self.cnt, self.seen = {}, {}, {}
        for k in self.eng:
            self.sem[k] = stack.enter_context(nc.semaphore("s_" + k))
            self.cnt[k] = 0
            self.seen[k] = {}
        self.nsem = len(self.eng)
        self.all_dma = []
        self.sem_pool = []
        self.gstack = stack
        self.n_inst = 0

    def barrier(self):
        deps = [(b.dma_sem, b.dma_cnt, "dma") for b in self.all_dma]
        for k in self.eng:
            if self.cnt[k] > 0:
                deps.append((self.sem[k], self.cnt[k], k))
        for e in self.eng:
            self._wait(e, deps)

    @contextlib.contextmanager
    def phase(self):
        outer = self.stack
        n0 = len(self.all_dma)
        with contextlib.ExitStack() as ps:
            self.stack = ps
            yield
            self.barrier()
            for b in self.all_dma[n0:]:
                self.sem_pool.append((b.dma_sem, b.dma_cnt))
            del self.all_dma[n0:]
        self.stack = outer

    def sbuf(self, name, shape, dt):
        self.n_tiles = getattr(self, "n_tiles", 0) + 1
        return self.stack.enter_context(self.nc.sbuf_tensor("%s_%d" % (name, self.n_tiles), shape, dt))

    def psum(self, name, shape, dt=F32):
        return self.stack.enter_context(self.nc.psum_tensor(name, shape, dt))

    def _wait(self, e, deps):
        best = {}
        for d in deps:
            if d is None:
                continue
            sem, val, en = d
            if e == "pe" and en == "pe":
                continue
            key = id(sem)
            if key not in best or best[key][1] < val:
                best[key] = (sem, val)
        for key, (sem, val) in best.items():
            if self.seen[e].get(key, 0) >= val:
                continue
            self.eng[e].wait_ge(sem, val)
            self.seen[e][key] = val

    @staticmethod
    def _deps(reads, writes):
        deps = []
        for b in reads:
            deps.append(b.last_w)
        for b in writes:
            deps.append(b.last_w)
            deps.extend(b.readers)
        return deps

    @staticmethod
    def _mark(tok, reads, writes):
        for b in writes:
            b.last_w = tok
            b.readers = []
        for b in reads:
            if b not in writes:
                b.readers = [r for r in b.readers if r[2] != tok[2] or tok[2] == "dma"] + [tok]

    def op(self, e, fn, reads=(), writes=(), also=()):
        self._wait(e, self._deps(reads, writes) + self._deps((), also))
        ins = fn(self.eng[e])
        self.cnt[e] += 1
        ins.then_inc(self.sem[e], 1)
        self._mark((self.sem[e], self.cnt[e], e), reads, writes)
        self.n_inst += 1
        return ins

    def dma(self, e, out, in_, reads=(), writes=(), owner=None, **kw):
        self._wait(e, self._deps(reads, writes))
        if owner is None:
            owner = (list(writes) + list(reads))[0]
        if owner.dma_sem is None:
            if self.sem_pool:
                owner.dma_sem, owner.dma_cnt = self.sem_pool.pop()
            else:
                owner.dma_sem = self.gstack.enter_context(self.nc.semaphore("d_%d" % self.nsem))
                self.nsem += 1
            self.all_dma.append(owner)
        owner.dma_cnt += 16
        ins = self.eng[e].dma_start(out=out, in_=in_, **kw)
        ins.then_inc(owner.dma_sem, 16)
        self._mark((owner.dma_sem, owner.dma_cnt, "dma"), reads, writes)
        self.n_inst += 1
        return ins

    def finish(self):
        deps = [(b.dma_sem, b.dma_cnt, "dma") for b in self.all_dma]
        for k in self.eng:
            if k != "sp" and self.cnt[k] > 0:
                deps.append((self.sem[k], self.cnt[k], k))
        self._wait("sp", deps)


def _chunked(ap2d):
    return ap2d.rearrange("(c p) n -> p c n", p=128)


def _rms_rstd(S, nc, x_sb, bx, ncols, pieces, ones, bones, sq, bsq, pss, bpss, rstd, brstd, epst, beps, dscale):
    for ch in range(16):
        s = ch % 2
        S.op("act", lambda e: e.activation(sq[s][:, 0:ncols], x_sb[:, ch, 0:ncols], AF.Square),
             reads=[bx[ch]], writes=[bsq[s]])
        for pi, (c0, c1) in enumerate(pieces):
            S.op("pe", lambda e: e.matmul(pss[pi][:, 0:c1 - c0], ones[:], sq[s][:, c0:c1],
                                          start=(ch == 0), stop=(ch == 15)),
                 reads=[bsq[s], bones], writes=[bpss[pi]])
    for pi, (c0, c1) in enumerate(pieces):
        S.op("act", lambda e: e.activation(rstd[:, c0:c1], pss[pi][:, 0:c1 - c0], AF.Ln, bias=epst[:, 0:1], scale=dscale),
             reads=[bpss[pi], beps], writes=[brstd])
    S.op("act", lambda e: e.activation(rstd[:, 0:ncols], rstd[:, 0:ncols], AF.Exp, scale=-0.5),
         reads=[brstd], writes=[brstd])


def emit_A(S, P, bP, xT, g1, w, rotf, rott, PT, PG, PV):
    nc = S.nc
    with S.phase():
        x_sb = S.sbuf("x_sb", [128, 16, NT], F32); bx = [Buf() for _ in range(16)]
        h_sb = S.sbuf("h_sb", [128, 16, NT], BF16); bh = [Buf() for _ in range(16)]
        g_sb = S.sbuf("g_sb", [128, 16], F32); bg = Buf()
        rf_sb = S.sbuf("rf_sb", [128, 2, NT], F32); brf = Buf()
        rt_sb = S.sbuf("rt_sb", [128, 2, 8, 256], F32); brt = Buf()
        ones = S.sbuf("ones", [128, 128], BF16); bones = Buf()
        epst = S.sbuf("epst", [128, 1], F32); beps = Buf()
        sq = [S.sbuf("sq%d" % i, [128, NT], BF16) for i in range(2)]; bsq = [Buf(), Buf()]
        rstd = S.sbuf("rstd", [128, NT], F32); brstd = Buf()
        pan = [S.sbuf("pan%d" % i, [128, 16, 512], BF16) for i in range(2)]; bpan = [Buf(), Buf()]
        stg32 = [S.sbuf("stg32_%d" % i, [128, NT], F32) for i in range(2)]; bstg32 = [Buf(), Buf()]
        stg16 = [S.sbuf("stg16_%d" % i, [128, NT], BF16) for i in range(2)]; bstg16 = [Buf(), Buf()]
        stgT = [S.sbuf("stgT%d" % i, [128, 512], BF16) for i in range(3)]; bstgT = [Buf() for _ in range(3)]
        tmpA = S.sbuf("tmpA", [128, 512], F32); btA = Buf()
        tmpB = S.sbuf("tmpB", [128, 512], F32); btB = Buf()

        S.op("pool", lambda e: e.memset(ones[:], 1.0), writes=[bones])
        S.op("pool", lambda e: e.memset(epst[:], EPS), writes=[beps])
        S.dma("sp", g_sb[:], g1, writes=[bg])
        for ch in range(16):
            S.dma("sp", x_sb[:, ch, :], xT[ch * 128:(ch + 1) * 128, :], writes=[bx[ch]])
        S.dma("sp", rf_sb[:], rotf, writes=[brf])
        S.dma("sp", rt_sb[:], rott, writes=[brt])

        XQ, XK = IN_W, IN_W + 256
        panels = [
            (0, 512, [(j * 128, 128, PT, PT_DQ + j * 128, 0.125, 'bf') for j in range(4)], []),
            (512, 512, [(j * 128, 128, PT, PT_DQ + 512 + j * 128, 0.125, 'bf') for j in range(4)], []),
            (1024, 512, [(j * 128, 128, PT, PT_DK + j * 128, 1.0, 'bf') for j in range(4)], []),
            (1536, 512, [(j * 128, 128, PT, PT_DK + 512 + j * 128, 1.0, 'bf') for j in range(4)], []),
            (2048, 512, [], [(0, 512, PV_DV, 1.0, None)]),
            (2560, 512, [], [(0, 512, PV_DV + 512, 1.0, None)]),
            (3072, 512, [(0, 128, PT, PT_GQ, 0.125, 'bf'), (128, 128, PT, PT_GQ + 128, 0.125, 'bf'),
                         (256, 128, PT, PT_GK, 1.0, 'bf'), (384, 128, PT, PT_GK + 128, 1.0, 'bf')],
             [(256, 256, PV_GK, 1.0, None)]),
            (3584, 512, [], [(0, 512, PV_GV, 1.0, None)]),
            (4096, 512, [(j * 128, 128, PG, j * 128, 1.0, 'f32') for j in range(4)], []),
            (4608, 32, [(0, 32, PT, PT_GA, 1.0, 'bf')], []),
            ((4640, XQ), 512, [(0, 128, PT, PT_RQ, 1.0, ('rot', 256)), (128, 128, PT, PT_RQ + 128, 1.0, ('rot', 384))], []),
            ((4896, XK), 512, [(0, 128, PT, PT_RK, 0.125, ('rot', 256)), (128, 128, PT, PT_RK + 128, 0.125, ('rot', 384))],
             [(0, 256, PV_RK, 0.125, 256)]),
            (5152, 512, [], [(0, 512, PV_RV, 1.0, None)]),
            (5664, 512, [(j * 128, 128, PG, 512 + j * 128, 1.0, 'f32') for j in range(4)], []),
        ]

        def load_panel(pi):
            c0, ncols, _, _ = panels[pi]
            s = pi % 2
            if isinstance(c0, tuple):
                S.dma("pool", pan[s][:, :, 0:256], _chunked(w[:, c0[0]:c0[0] + 256]), writes=[bpan[s]])
                S.dma("pool", pan[s][:, :, 256:512], _chunked(w[:, c0[1]:c0[1] + 256]), writes=[bpan[s]])
            else:
                S.dma("pool", pan[s][:, :, 0:ncols], _chunked(w[:, c0:c0 + ncols]), writes=[bpan[s]])

        load_panel(0)
        _rms_rstd(S, nc, x_sb, bx, NT, [(0, 512), (512, 1024)], ones, bones, sq, bsq, [P[6], P[7]], [bP[6], bP[7]],
                  rstd, brstd, epst, beps, 1.0 / D)
        for ch in range(16):
            S.op("dve", lambda e: e.scalar_tensor_tensor(out=h_sb[:, ch, :], in0=x_sb[:, ch, :], scalar=g_sb[:, ch:ch + 1],
                                                         in1=rstd[:], op0=ALU.mult, op1=ALU.mult),
                 reads=[bx[ch], bg, brstd], writes=[bh[ch]])

        rot = [0]
        evc = [0]

        def next_ps():
            i = rot[0] % 6
            rot[0] += 1
            return P[i], bP[i]

        def evac(dst_ap, src_ap, scale, reads, writes):
            if evc[0] % 2 == 0:
                S.op("act", lambda e: e.activation(dst_ap, src_ap, AF.Copy, scale=float(scale)), reads=reads, writes=writes)
            else:
                S.op("dve", lambda e: e.tensor_scalar(out=dst_ap, in0=src_ap, scalar1=float(scale), scalar2=None, op0=ALU.mult),
                     reads=reads, writes=writes)
            evc[0] += 1

        n32 = n16 = nT = 0
        for pi, (c0, ncols, fm_jobs, tm_jobs) in enumerate(panels):
            s = pi % 2
            if pi + 1 < len(panels):
                load_panel(pi + 1)
            pn = pan[s]
            for (off, M, dst, row0, scale, kind) in fm_jobs:
                if kind == 'f32':
                    stg, bstg = stg32[n32 % 2], bstg32[n32 % 2]; n32 += 1
                else:
                    stg, bstg = stg16[n16 % 2], bstg16[n16 % 2]; n16 += 1
                for th in range(2):
                    ps, bps = next_ps()
                    for ch in range(16):
                        S.op("pe", lambda e: e.matmul(ps[0:M, :], pn[:, ch, off:off + M], h_sb[:, ch, th * 512:(th + 1) * 512],
                                                      start=(ch == 0), stop=(ch == 15)),
                             reads=[bpan[s], bh[ch]], writes=[bps])
                    if isinstance(kind, tuple):
                        soff = kind[1]
                        ps2, bps2 = next_ps()
                        for ch in range(16):
                            S.op("pe", lambda e: e.matmul(ps2[0:M, :], pn[:, ch, soff:soff + M], h_sb[:, ch, th * 512:(th + 1) * 512],
                                                          start=(ch == 0), stop=(ch == 15)),
                                 reads=[bpan[s], bh[ch]], writes=[bps2])
                        S.op("dve", lambda e: e.tensor_tensor(out=tmpA[:], in0=ps[:], in1=rf_sb[:, 0, th * 512:(th + 1) * 512], op=ALU.mult),
                             reads=[bps, brf], writes=[btA])
                        S.op("dve", lambda e: e.tensor_tensor(out=tmpB[:], in0=ps2[:], in1=rf_sb[:, 1, th * 512:(th + 1) * 512], op=ALU.mult),
                             reads=[bps2, brf], writes=[btB])
                        S.op("dve", lambda e: e.tensor_tensor(out=tmpA[:], in0=tmpA[:], in1=tmpB[:], op=ALU.add),
                             reads=[btA, btB], writes=[btA])
                        S.op("dve", lambda e: e.tensor_scalar(out=stg[:, th * 512:(th + 1) * 512], in0=tmpA[:], scalar1=float(scale),
                                                              scalar2=None, op0=ALU.mult),
                             reads=[btA], writes=[bstg])
                    else:
                        evac(stg[0:M, th * 512:(th + 1) * 512], ps[0:M, :], scale, [bps], [bstg])
                S.dma("sp", dst[row0:row0 + M, :], stg[0:M, :], reads=[bstg], owner=bstg)
            for (off, ncl, pv0, scale, soff) in tm_jobs:
                for tt in range(8):
                    ps, bps = next_ps()
                    for ch in range(16):
                        S.op("pe", lambda e: e.matmul(ps[:, 0:ncl], h_sb[:, ch, tt * 128:(tt + 1) * 128], pn[:, ch, off:off + ncl],
                                                      start=(ch == 0), stop=(ch == 15)),
                             reads=[bpan[s], bh[ch]], writes=[bps])
                    stt, bstt = stgT[nT % 3], bstgT[nT % 3]; nT += 1
                    if soff is not None:
                        ps2, bps2 = next_ps()
                        for ch in range(16):
                            S.op("pe", lambda e: e.matmul(ps2[:, 0:ncl], h_sb[:, ch, tt * 128:(tt + 1) * 128], pn[:, ch, soff:soff + ncl],
                                                          start=(ch == 0), stop=(ch == 15)),
                                 reads=[bpan[s], bh[ch]], writes=[bps2])
                        S.op("dve", lambda e: e.tensor_tensor(out=tmpA[:, 0:ncl], in0=ps[:, 0:ncl], in1=rt_sb[:, 0, tt, :], op=ALU.mult),
                             reads=[bps, brt], writes=[btA])
                        S.op("dve", lambda e: e.tensor_tensor(out=tmpB[:, 0:ncl], in0=ps2[:, 0:ncl], in1=rt_sb[:, 1, tt, :], op=ALU.mult),
                             reads=[bps2, brt], writes=[btB])
                        S.op("dve", lambda e: e.tensor_tensor(out=tmpA[:, 0:ncl], in0=tmpA[:, 0:ncl], in1=tmpB[:, 0:ncl], op=ALU.add),
                             reads=[btA, btB], writes=[btA])
                        S.op("dve", lambda e: e.tensor_scalar(out=stt[:, 0:ncl], in0=tmpA[:, 0:ncl], scalar1=float(scale), scalar2=None, op0=ALU.mult),
                             reads=[btA], writes=[bstt])
                    else:
                        evac(stt[:, 0:ncl], ps[:, 0:ncl], scale, [bps], [bstt])
                    S.dma("sp", PV[tt * 128:(tt + 1) * 128, pv0:pv0 + ncl], stt[:, 0:ncl], reads=[bstt], owner=bstt)


def emit_C(S, P, bP, Xin, Md, q, wo, g2, wup, cw, wdn, gF, xo, final):
    nc = S.nc
    PIECES = [(0, 512), (512, 1024), (1024, NTH)]
    with S.phase():
        x_sb = S.sbuf("x_sb", [128, 16, NTH], F32); bx = [Buf() for _ in range(16)]
        m_sb = S.sbuf("m_sb", [128, 16, NTH], BF16); bm = [Buf() for _ in range(16)]
        act_sb = S.sbuf("act_sb", [128, 22, NT], BF16); bact = [Buf() for _ in range(22)]
        g_sb = S.sbuf("g_sb", [128, 16], F32); bg = Buf()
        gF_sb = S.sbuf("gF_sb", [128, 16], F32); bgF = Buf()
        cw_sb = S.sbuf("cw_sb", [128, 88, 4], F32); bcw = Buf()
        ones = S.sbuf("ones", [128, 128], BF16); bones = Buf()
        epst = S.sbuf("epst", [128, 1], F32); beps = Buf()
        sq = [S.sbuf("sq%d" % i, [128, NTH], BF16) for i in range(2)]; bsq = [Buf(), Buf()]
        rstd = S.sbuf("rstd", [128, NTH], F32); brstd = Buf()
        pool_ = [S.sbuf("pool%d" % i, [128, 8192], BF16) for i in range(2)]; bpool = [Buf(), Buf()]
        ua = S.sbuf("ua", [128, NTH], F32); bua = Buf()
        ug = S.sbuf("ug", [128, NTH], F32); bug = Buf()
        ya = S.sbuf("ya", [128, NT], F32); bya = Buf()
        yg = S.sbuf("yg", [128, NT], F32); byg = Buf()

        S.op("pool", lambda e: e.memset(ones[:], 1.0), writes=[bones])
        S.op("pool", lambda e: e.memset(epst[:], EPS), writes=[beps])
        S.dma("sp", g_sb[:], g2, writes=[bg])
        S.dma("sp", gF_sb[:], gF, writes=[bgF])
        S.dma("sp", cw_sb[:], cw, writes=[bcw])
        c0 = q * NT
        for ch in range(16):
            S.dma("sp", m_sb[:, ch, 0:NT], Md[ch * 128:(ch + 1) * 128, c0:c0 + NT], writes=[bm[ch]])
        for ch in range(16):
            S.dma("sp", x_sb[:, ch, 0:NT], Xin[ch * 128:(ch + 1) * 128, c0:c0 + NT], writes=[bx[ch]])
        for (col, src) in ((NT, c0 - 1), (NT + 1, c0 + NT)):
            if 0 <= src < T:
                S.dma("sp", m_sb[:, :, col:col + 1], Md[:, src:src + 1].rearrange("(c p) o -> p c o", p=128), writes=bm, allow_slow_non_contiguous=True)
                S.dma("sp", x_sb[:, :, col:col + 1], Xin[:, src:src + 1].rearrange("(c p) o -> p c o", p=128), writes=bx, allow_slow_non_contiguous=True)
            else:
                S.op("pool", lambda e: e.memset(m_sb[:, :, col:col + 1], 0.0), writes=bm)
                S.op("pool", lambda e: e.memset(x_sb[:, :, col:col + 1], 0.0), writes=bx)

        jobs = []
        rot = [0]

        def next_ps(n=7):
            i = rot[0] % n
            rot[0] += 1
            return P[i], bP[i]

        def mk_wo(pi):
            def load(s):
                v = pool_[s][:].rearrange("p (c n) -> p c n", c=16)
                S.dma("pool", v, _chunked(wo[:, pi * 512:(pi + 1) * 512]), writes=[bpool[s]])

            def comp(s):
                v = pool_[s][:].rearrange("p (c n) -> p c n", c=16)
                for j in range(4):
                    fb = pi * 4 + j
                    for (c0, c1) in PIECES:
                        ps, bps = next_ps()
                        for ch in range(16):
                            S.op("pe", lambda e: e.matmul(ps[:, 0:c1 - c0], v[:, ch, j * 128:(j + 1) * 128], m_sb[:, ch, c0:c1],
                                                          start=(ch == 0), stop=(ch == 15)),
                                 reads=[bpool[s], bm[ch]], writes=[bps])
                        S.op("dve", lambda e: e.tensor_tensor(out=x_sb[:, fb, c0:c1], in0=ps[:, 0:c1 - c0], in1=x_sb[:, fb, c0:c1], op=ALU.add),
                             reads=[bps, bx[fb]], writes=[bx[fb]])
            return load, comp

        for pi in range(4):
            jobs.append(mk_wo(pi))

        def norm2(s_unused=None):
            _rms_rstd(S, nc, x_sb, bx, NTH, PIECES, ones, bones, sq, bsq, [P[5], P[6], P[7]], [bP[5], bP[6], bP[7]],
                      rstd, brstd, epst, beps, 1.0 / D)
            for ch in range(16):
                S.op("dve", lambda e: e.scalar_tensor_tensor(out=m_sb[:, ch, :], in0=x_sb[:, ch, :], scalar=g_sb[:, ch:ch + 1],
                                                             in1=rstd[:], op0=ALU.mult, op1=ALU.mult),
                     reads=[bx[ch], bg, brstd], writes=[bm[ch]])

        def mk_up(hf, pp):
            a0 = (hf * 22 + pp * 2) * 128

            def load(s):
                v = pool_[s][:].rearrange("p (c n) -> p c n", c=16)
                S.dma("pool", v[:, :, 0:256], _chunked(wup[:, a0:a0 + 256]), writes=[bpool[s]])
                S.dma("pool", v[:, :, 256:512], _chunked(wup[:, DFF + a0:DFF + a0 + 256]), writes=[bpool[s]])

            def comp(s):
                v = pool_[s][:].rearrange("p (c n) -> p c n", c=16)
                for j in range(2):
                    li = pp * 2 + j
                    blk = hf * 22 + li
                    for (coff, u, bu, y, by, wi) in ((j * 128, ua, bua, ya, bya, blk), (256 + j * 128, ug, bug, yg, byg, 44 + blk)):
                        pss = []
                        for (c0, c1) in PIECES:
                            if c0 == 1024:
                                ps, bps = P[7], bP[7]
                            else:
                                ps, bps = next_ps(7)
                            for ch in range(16):
                                S.op("pe", lambda e: e.matmul(ps[:, 0:c1 - c0], v[:, ch, coff:coff + 128], m_sb[:, ch, c0:c1],
                                                              start=(ch == 0), stop=(ch == 15)),
                                     reads=[bpool[s], bm[ch]], writes=[bps])
                            pss.append((ps, bps))
                        S.op("act", lambda e: e.activation(u[:, 1:513], pss[0][0][:, 0:512], AF.Copy), reads=[pss[0][1]], writes=[bu])
                        S.op("act", lambda e: e.activation(u[:, 513:1025], pss[1][0][:, 0:512], AF.Copy), reads=[pss[1][1]], writes=[bu])
                        S.op("act", lambda e: e.activation(u[:, 0:1], pss[2][0][:, 0:1], AF.Copy), reads=[pss[2][1]], writes=[bu])
                        S.op("act", lambda e: e.activation(u[:, 1025:1026], pss[2][0][:, 1:2], AF.Copy), reads=[pss[2][1]], writes=[bu])
                        S.op("dve", lambda e: e.tensor_scalar(out=y[:], in0=u[:, 1:1025], scalar1=cw_sb[:, wi, 1:2], scalar2=cw_sb[:, wi, 3:4],
                                                              op0=ALU.mult, op1=ALU.add), reads=[bu, bcw], writes=[by])
                        S.op("dve", lambda e: e.scalar_tensor_tensor(out=y[:], in0=u[:, 0:1024], scalar=cw_sb[:, wi, 0:1], in1=y[:],
                                                                     op0=ALU.mult, op1=ALU.add), reads=[bu, bcw, by], writes=[by])
                        S.op("dve", lambda e: e.scalar_tensor_tensor(out=y[:], in0=u[:, 2:1026], scalar=cw_sb[:, wi, 2:3], in1=y[:],
                                                                     op0=ALU.mult, op1=ALU.add), reads=[bu, bcw, by], writes=[by])
                    S.op("act", lambda e: e.activation(yg[:], yg[:], AF.Silu), reads=[byg], writes=[byg])
                    S.op("dve", lambda e: e.tensor_tensor(out=act_sb[:, li, :], in0=ya[:], in1=yg[:], op=ALU.mult),
                         reads=[bya, byg], writes=[bact[li]])
            return load, comp

        def mk_dn(hf, pj):
            def load(s):
                v = pool_[s][:, 0:22 * 256].rearrange("p (c n) -> p c n", c=22)
                src = wdn[hf * 22 * 128:(hf + 1) * 22 * 128, pj * 256:(pj + 1) * 256].rearrange("(c p) n -> p c n", p=128)
                S.dma("pool", v, src, writes=[bpool[s]])

            def comp(s):
                v = pool_[s][:, 0:22 * 256].rearrange("p (c n) -> p c n", c=22)
                for j in range(2):
                    fb = pj * 2 + j
                    for th in range(2):
                        ps, bps = next_ps()
                        for li in range(22):
                            S.op("pe", lambda e: e.matmul(ps[:], v[:, li, j * 128:(j + 1) * 128], act_sb[:, li, th * 512:(th + 1) * 512],
                                                          start=(li == 0), stop=(li == 21)),
                                 reads=[bpool[s], bact[li]], writes=[bps])
                        S.op("dve", lambda e: e.tensor_tensor(out=x_sb[:, fb, th * 512:(th + 1) * 512], in0=ps[:],
                                                              in1=x_sb[:, fb, th * 512:(th + 1) * 512], op=ALU.add),
                             reads=[bps, bx[fb]], writes=[bx[fb]])
            return load, comp

        jobs.append((None, norm2))
        for hf in range(2):
            for pp in range(11):
                jobs.append(mk_up(hf, pp))
            for pj in range(8):
                jobs.append(mk_dn(hf, pj))

        loads = [(i, j[0]) for i, j in enumerate(jobs) if j[0] is not None]
        slot_of = {}
        for k, (i, _) in enumerate(loads):
            slot_of[i] = k % 2
        nxt = 0
        loads[0][1](slot_of[loads[0][0]]); nxt = 1
        for i, (ld, comp) in enumerate(jobs):
            if ld is not None:
                if nxt < len(loads):
                    loads[nxt][1](slot_of[loads[nxt][0]]); nxt += 1
                comp(slot_of[i])
            else:
                comp()

        if final:
            _rms_rstd(S, nc, x_sb, bx, NT, [(0, 512), (512, 1024)], ones, bones, sq, bsq, [P[5], P[6]], [bP[5], bP[6]],
                      rstd, brstd, epst, beps, 1.0 / D)
            for ch in range(16):
                S.op("dve", lambda e: e.scalar_tensor_tensor(out=x_sb[:, ch, 0:NT], in0=x_sb[:, ch, 0:NT], scalar=gF_sb[:, ch:ch + 1],
                                                             in1=rstd[:, 0:NT], op0=ALU.mult, op1=ALU.mult),
                     reads=[bx[ch], bgF, brstd], writes=[bx[ch]])
        for ch in range(16):
            S.dma("sp", xo[ch * 128:(ch + 1) * 128, :], x_sb[:, ch, 0:NT], reads=[bx[ch]], owner=bx[ch])


def _vec16(v):
    return np.ascontiguousarray(np.asarray(v, np.float32).reshape(16, 128).T)


def _rot_tables():
    half = 32
    inv = (1.0 / (10000.0 ** np.linspace(0.0, 1.0, half, dtype=np.float32))).astype(np.float32)
    ang = np.arange(T, dtype=np.float32)[:, None] * inv[None, :]
    return np.cos(ang).astype(np.float32), np.sin(ang).astype(np.float32)


def _rot_core_tables(cos, sin, c):
    pos = (c % 4) * NT + np.arange(NT)
    r = np.arange(128)
    sgn_r = np.where(r % 64 < 32, -1.0, 1.0).astype(np.float32)
    rotf = np.empty((128, 2, NT), np.float32)
    rotf[:, 0, :] = cos[pos][:, r % 32].T
    rotf[:, 1, :] = (sin[pos][:, r % 32] * sgn_r[None, :]).T
    col = np.arange(256)
    sgn_c = np.where(col % 64 < 32, -1.0, 1.0).astype(np.float32)
    ct = cos[pos][:, col % 32].reshape(8, 128, 256).transpose(1, 0, 2)
    stb = (sin[pos][:, col % 32] * sgn_c[None, :]).reshape(8, 128, 256).transpose(1, 0, 2)
    rott = np.ascontiguousarray(np.stack([ct, stb], axis=1))
    return rotf, rott


def _w_ext(w_in_l):
    def swapped(c0):
        idx = np.arange(256)
        src = c0 + (idx // 64) * 64 + ((idx % 64) + 32) % 64
        return w_in_l[:, src]
    return np.ascontiguousarray(np.concatenate([w_in_l, swapped(4640), swapped(4896)], axis=1))


NEAR_D = (-128, 0, 128, 256, 384, 512)


def emit_B(S, P, bP, PTd, PGd, PVd, Md, btile, cb, lamp, lcst, gains, gw, rdec, cmask):
    nc = S.nc
    NCH = T // 128
    with S.phase():
        ones = S.sbuf("ones", [128, 128], BF16); bones = Buf()
        onesF = S.sbuf("onesF", [128, 512], F32); bonesF = Buf()
        epst = S.sbuf("epst", [128, 1], F32); beps = Buf()
        one1 = S.sbuf("one1", [128, 1], F32); bone1 = Buf()
        cm = S.sbuf("cm", [128, 4, 128], F32); bcm = Buf()
        M4 = S.sbuf("M4", [128, 2, 512], F32); bM4 = Buf()
        cb_sb = S.sbuf("cb_sb", [128, 2], F32); bcb = Buf()
        lamp_sb = S.sbuf("lamp_sb", [128, 256], F32); blamp = Buf()
        lc_sb = S.sbuf("lc_sb", [128, 2], F32); blc = Buf()
        gn_sb = S.sbuf("gn_sb", [128, 16], F32); bgn = Buf()
        rd_sb = S.sbuf("rd_sb", [128, 2], F32); brd = Buf()
        sm = S.sbuf("sm", [128, 16], F32); bsm = Buf()
        tmp64 = S.sbuf("tmp64", [128, 128], F32); bt64 = Buf()
        S.op("pool", lambda e: e.memset(ones[:], 1.0), writes=[bones])
        S.op("pool", lambda e: e.memset(onesF[:], 1.0), writes=[bonesF])
        S.op("pool", lambda e: e.memset(epst[:], EPS), writes=[beps])
        S.op("pool", lambda e: e.memset(one1[:], 1.0), writes=[bone1])
        S.dma("sp", cm[:], cmask, writes=[bcm])
        S.dma("sp", lamp_sb[:], lamp, writes=[blamp])
        S.dma("sp", lc_sb[:], lcst, writes=[blc])
        S.dma("sp", gn_sb[:], gains, writes=[bgn])
        for k in range(4):
            S.op("dve", lambda e: e.tensor_copy(M4[:, 0, k * 128:(k + 1) * 128], cm[:, 0, :]), reads=[bcm], writes=[bM4])
            S.op("dve", lambda e: e.tensor_copy(M4[:, 1, k * 128:(k + 1) * 128], cm[:, 1, :]), reads=[bcm], writes=[bM4])

        o_sb = S.sbuf("o_sb", [128, 512], F32); bo = Buf()
        sq_sb = S.sbuf("sq_sb", [128, 512], BF16); bsq = Buf()
        rs_sb = S.sbuf("rs_sb", [128, 512], F32); brs = Buf()
        stg = [S.sbuf("stg%d" % i, [128, 512], BF16) for i in range(2)]; bstg = [Buf(), Buf()]
        nst = [0]

        def finalize(src_ap, src_bufs, pss, bpss, scal_ap, scal_bufs, gate_ap, gate_bufs, dst_ap):
            S.op("act", lambda e: e.activation(o_sb[:], src_ap, AF.Copy), reads=src_bufs, writes=[bo])
            S.op("act", lambda e: e.activation(sq_sb[:], src_ap, AF.Square), reads=src_bufs, writes=[bsq])
            S.op("pe", lambda e: e.matmul(pss[:], ones[:], sq_sb[:], start=True, stop=True), reads=[bones, bsq], writes=[bpss])
            S.op("act", lambda e: e.activation(rs_sb[:], pss[:], AF.Ln, bias=epst[:, 0:1], scale=1.0 / 128), reads=[bpss, beps], writes=[brs])
            S.op("act", lambda e: e.activation(rs_sb[:], rs_sb[:], AF.Exp, scale=-0.5), reads=[brs], writes=[brs])
            S.op("dve", lambda e: e.tensor_tensor(out=o_sb[:], in0=o_sb[:], in1=rs_sb[:], op=ALU.mult), reads=[bo, brs], writes=[bo])
            sg, bsg = stg[nst[0] % 2], bstg[nst[0] % 2]; nst[0] += 1
            if gate_ap is not None:
                S.op("dve", lambda e: e.scalar_tensor_tensor(out=sg[:], in0=o_sb[:], scalar=scal_ap, in1=gate_ap, op0=ALU.mult, op1=ALU.mult),
                     reads=[bo] + scal_bufs + gate_bufs, writes=[bsg])
            else:
                S.op("dve", lambda e: e.tensor_scalar(out=sg[:], in0=o_sb[:], scalar1=scal_ap, scalar2=None, op0=ALU.mult),
                     reads=[bo] + scal_bufs, writes=[bsg])
            S.dma("sp", dst_ap, sg[:], reads=[bsg], owner=bsg)

        qd = S.sbuf("qd", [128, T], BF16); bqd = Buf()
        kd = S.sbuf("kd", [128, T], BF16); bkd = Buf()
        Qf0 = S.sbuf("Qf0", [128, T], BF16); bQf = Buf()
        Qb0 = S.sbuf("Qb0", [128, T], BF16); bQb = Buf()
        Kp = S.sbuf("Kp", [128, T], BF16); bKp = Buf()
        ktv = S.sbuf("ktv", [128, NCH, 192], BF16); bktv = Buf()
        kend = S.sbuf("kend", [128, NCH, 128], BF16); bkend = Buf()
        KVa = S.sbuf("KVa", [128, NCH, 128], F32); bKVf = Buf(); bKVb = Buf()
        Sbf = S.sbuf("Sbf", [128, NCH, 128], BF16); bSbf = Buf()
        gt_sb = S.sbuf("gt_sb", [128, T], F32); bgt = Buf()
        ga_sb = S.sbuf("ga_sb", [33, T], BF16); bga = Buf()
        gw_sb = S.sbuf("gw_sb", [33, 128], BF16); bgw = Buf()
        la4 = [S.sbuf("la4_%d" % i, [128, 4, 128], F32) for i in range(2)]; bla4 = [Buf(), Buf()]
        la_ret = S.sbuf("la_ret", [128, 4, 128], F32); blar = Buf()
        Epos = S.sbuf("Epos", [128, 512], F32); bEpos = Buf()
        Eneg = S.sbuf("Eneg", [128, 512], F32); bEneg = Buf()
        Eend = S.sbuf("Eend", [128, 4, 128], F32); bEend = Buf()
        Dall = S.sbuf("Dall", [128, NCH], F32); bD = Buf()
        Pf = S.sbuf("Pf", [128, 512], BF16); bPf = Buf()
        Pb = S.sbuf("Pb", [128, 512], BF16); bPb = Buf()

        S.op("pool", lambda e: e.memset(Qf0[:], 0.0), writes=[bQf])
        S.op("pool", lambda e: e.memset(Qb0[:], 0.0), writes=[bQb])
        S.op("pool", lambda e: e.memset(ga_sb[:], 1.0), writes=[bga])
        S.dma("sp", ga_sb[0:32, :], PTd[PT_GA:PT_GA + 32, :], writes=[bga])

        for kind, hg in [(kk_, hh_) for kk_ in range(2) for hh_ in range(4)]:
            qrow = (PT_GQ if kind == 0 else PT_RQ) + hg * 64
            krow = (PT_GK if kind == 0 else PT_RK) + hg * 64
            kcol = (PV_GK if kind == 0 else PV_RK) + hg * 64
            vcol = (PV_GV if kind == 0 else PV_RV) + hg * 128
            grow = (0 if kind == 0 else 512) + hg * 128
            for half in range(2):
                S.dma("sp", qd[half * 64:(half + 1) * 64, :], PTd[qrow:qrow + 64, :], writes=[bqd])
                S.dma("sp", kd[half * 64:(half + 1) * 64, :], PTd[krow:krow + 64, :], writes=[bkd])
            S.dma("sp", ktv[:, :, 0:64], PVd[:, kcol:kcol + 64].rearrange("(n p) c -> p n c", p=128), writes=[bktv])
            S.dma("sp", ktv[:, :, 64:192], PVd[:, vcol:vcol + 128].rearrange("(n p) c -> p n c", p=128), writes=[bktv])
            S.dma("sp", gt_sb[:], PGd[grow:grow + 128, :], writes=[bgt])
            if kind == 0:
                S.dma("pool", gw_sb[:], gw[hg], writes=[bgw])
            else:
                S.dma("sp", rd_sb[:], rdec[hg], writes=[brd])
                S.op("act", lambda e: e.activation(sm[:, 0:2], rd_sb[:], AF.Exp, scale=-1.0), reads=[brd], writes=[bsm])
                S.op("act", lambda e: e.activation(sm[:, 0:2], sm[:, 0:2], AF.Ln, bias=one1[:, 0:1]), reads=[bsm, bone1], writes=[bsm])
                S.op("dve", lambda e: e.tensor_scalar(out=sm[:, 0:2], in0=sm[:, 0:2], scalar1=-1.0, scalar2=None, op0=ALU.mult), reads=[bsm], writes=[bsm])
                for k in range(4):
                    S.op("dve", lambda e: e.tensor_scalar(out=la_ret[:, k, 0:64], in0=onesF[:, 0:64], scalar1=sm[:, 0:1], scalar2=None, op0=ALU.mult),
                         reads=[bonesF, bsm], writes=[blar])
                    S.op("dve", lambda e: e.tensor_scalar(out=la_ret[:, k, 64:128], in0=onesF[:, 0:64], scalar1=sm[:, 1:2], scalar2=None, op0=ALU.mult),
                         reads=[bonesF, bsm], writes=[blar])
            S.op("act", lambda e: e.activation(gt_sb[:], gt_sb[:], AF.Silu), reads=[bgt], writes=[bgt])
            for g in range(NCH // 4):
                cols = slice(g * 512, (g + 1) * 512)
                if kind == 0:
                    la, bla = la4[g % 2], bla4[g % 2]
                    zps, bz = P[0], bP[0]
                    for k in range(4):
                        n = g * 4 + k
                        S.op("pe", lambda e: e.matmul(zps[:, k * 128:(k + 1) * 128], ga_sb[:, n * 128:(n + 1) * 128], gw_sb[:], start=True, stop=True),
                             reads=[bga, bgw], writes=[bz])
                    lav = la[:].rearrange("p k c -> p (k c)")
                    S.op("act", lambda e: e.activation(lav, zps[:], AF.Exp, scale=-1.0), reads=[bz], writes=[bla])
                    S.op("act", lambda e: e.activation(lav, lav, AF.Ln, bias=one1[:, 0:1]), reads=[bla, bone1], writes=[bla])
                    S.op("dve", lambda e: e.tensor_scalar(out=lav, in0=lav, scalar1=-1.0 / 16.0, scalar2=None, op0=ALU.mult), reads=[bla], writes=[bla])
                else:
                    la, bla = la_ret, blar
                psA, bA, psB, bB, psC, bC, psKV, bKV = P[1], bP[1], P[2], bP[2], P[3], bP[3], P[4], bP[4]
                for k in range(4):
                    S.op("pe", lambda e: e.matmul(psA[:, k * 128:(k + 1) * 128], la[:, k, :], cm[:, 0, :], start=True, stop=True),
                         reads=[bla, bcm], writes=[bA])
                    S.op("pe", lambda e: e.matmul(psB[:, k * 128:(k + 1) * 128], la[:, k, :], cm[:, 1, :], start=True, stop=True),
                         reads=[bla, bcm], writes=[bB])
                    S.op("pe", lambda e: e.matmul(psC[:, k * 128:k * 128 + 64], cm[:, 2, :], la[:, k, 0:64], start=True, stop=True),
                         reads=[bla, bcm], writes=[bC])
                    S.op("pe", lambda e: e.matmul(psC[:, k * 128 + 64:(k + 1) * 128], cm[:, 3, :], la[:, k, 64:128], start=True, stop=True),
                         reads=[bla, bcm], writes=[bC])
                S.op("act", lambda e: e.activation(Epos[0:64, :], psA[0:64, :], AF.Exp), reads=[bA], writes=[bEpos])
                S.op("act", lambda e: e.activation(Epos[64:128, :], psB[64:128, :], AF.Exp), reads=[bB], writes=[bEpos])
                S.op("act", lambda e: e.activation(Eneg[0:64, :], psA[0:64, :], AF.Exp, scale=-1.0), reads=[bA], writes=[bEneg])
                S.op("act", lambda e: e.activation(Eneg[64:128, :], psB[64:128, :], AF.Exp, scale=-1.0), reads=[bB], writes=[bEneg])
                S.op("act", lambda e: e.activation(Eend[:].rearrange("p k c -> p (k c)"), psC[:], AF.Exp), reads=[bC], writes=[bEend])
                S.op("dve", lambda e: e.tensor_tensor(out=Qf0[0:64, cols], in0=qd[0:64, cols], in1=Epos[0:64, :], op=ALU.mult),
                     reads=[bqd, bEpos], writes=[bQf])
                S.op("dve", lambda e: e.tensor_tensor(out=Qb0[64:128, cols], in0=qd[64:128, cols], in1=Epos[64:128, :], op=ALU.mult),
                     reads=[bqd, bEpos], writes=[bQb])
                S.op("dve", lambda e: e.tensor_tensor(out=Kp[:, cols], in0=kd[:, cols], in1=Eneg[:], op=ALU.mult),
                     reads=[bkd, bEneg], writes=[bKp])
                S.op("dve", lambda e: e.tensor_tensor(out=kend[:, g * 4:(g + 1) * 4, 0:64], in0=ktv[:, g * 4:(g + 1) * 4, 0:64], in1=Eend[:, :, 0:64], op=ALU.mult),
                     reads=[bktv, bEend], writes=[bkend])
                S.op("dve", lambda e: e.tensor_tensor(out=kend[:, g * 4:(g + 1) * 4, 64:128], in0=ktv[:, g * 4:(g + 1) * 4, 0:64], in1=Eend[:, :, 64:128], op=ALU.mult),
                     reads=[bktv, bEend], writes=[bkend])
                Ev = Epos[:].rearrange("p (k c) -> p k c", c=128)
                S.op("dve", lambda e: e.tensor_copy(Dall[0:64, g * 4:(g + 1) * 4], Ev[0:64, :, 127]), reads=[bEpos], writes=[bD])
                S.op("dve", lambda e: e.tensor_copy(Dall[64:128, g * 4:(g + 1) * 4], Ev[64:128, :, 0]), reads=[bEpos], writes=[bD])
                for k in range(4):
                    n = g * 4 + k
                    S.op("pe", lambda e: e.matmul(psKV[:, k * 128:(k + 1) * 128], kend[:, n, :], ktv[:, n, 64:192], start=True, stop=True),
                         reads=[bkend, bktv], writes=[bKV])
                S.op("act", lambda e: e.activation(KVa[:, g * 4:(g + 1) * 4, :].rearrange("p k c -> p (k c)"), psKV[:], AF.Copy),
                     reads=[bKV], writes=[bKVf, bKVb])
            for i in range(1, NCH):
                n = i
                S.op("dve", lambda e: e.scalar_tensor_tensor(out=KVa[0:64, n, :], in0=KVa[0:64, n - 1, :], scalar=Dall[0:64, n:n + 1],
                                                             in1=KVa[0:64, n, :], op0=ALU.mult, op1=ALU.add),
                     reads=[bD, bKVf], writes=[bKVf])
                m = NCH - 1 - i
                S.op("dve", lambda e: e.scalar_tensor_tensor(out=KVa[64:128, m, :], in0=KVa[64:128, m + 1, :], scalar=Dall[64:128, m:m + 1],
                                                             in1=KVa[64:128, m, :], op0=ALU.mult, op1=ALU.add),
                     reads=[bD, bKVb], writes=[bKVb])
            S.op("pool", lambda e: e.memset(Sbf[0:64, 0, :], 0.0), writes=[bSbf])
            S.op("pool", lambda e: e.memset(Sbf[64:128, NCH - 1, :], 0.0), writes=[bSbf])
            S.op("dve", lambda e: e.tensor_copy(Sbf[0:64, 1:NCH, :], KVa[0:64, 0:NCH - 1, :]), reads=[bKVf], writes=[bSbf])
            S.op("dve", lambda e: e.tensor_copy(Sbf[64:128, 0:NCH - 1, :], KVa[64:128, 1:NCH, :]), reads=[bKVb], writes=[bSbf])
            for g in range(NCH // 4):
                psSf, bSf, psSb, bSb = P[0], bP[0], P[1], bP[1]
                psO, bO = (P[2], bP[2]) if g % 2 == 0 else (P[3], bP[3])
                for k in range(4):
                    n = g * 4 + k
                    cs = slice(n * 128, (n + 1) * 128)
                    S.op("pe", lambda e: e.matmul(psSf[:, k * 128:(k + 1) * 128], Kp[:, cs], Qf0[:, cs], start=True, stop=True),
                         reads=[bKp, bQf], writes=[bSf])
                    S.op("pe", lambda e: e.matmul(psSb[:, k * 128:(k + 1) * 128], Kp[:, cs], Qb0[:, cs], start=True, stop=True),
                         reads=[bKp, bQb], writes=[bSb])
                S.op("dve", lambda e: e.tensor_tensor(out=Pf[:], in0=psSf[:], in1=M4[:, 0, :], op=ALU.mult), reads=[bSf, bM4], writes=[bPf])
                S.op("dve", lambda e: e.tensor_tensor(out=Pb[:], in0=psSb[:], in1=M4[:, 1, :], op=ALU.mult), reads=[bSb, bM4], writes=[bPb])
                for k in range(4):
                    n = g * 4 + k
                    cs = slice(n * 128, (n + 1) * 128)
                    ko = slice(k * 128, (k + 1) * 128)
                    S.op("pe", lambda e: e.matmul(psO[:, ko], ktv[:, n, 64:192], Pf[:, ko], start=True, stop=False), reads=[bktv, bPf], writes=[bO])
                    S.op("pe", lambda e: e.matmul(psO[:, ko], ktv[:, n, 64:192], Pb[:, ko], start=False, stop=False), reads=[bktv, bPb], writes=[bO])
                    S.op("pe", lambda e: e.matmul(psO[:, ko], Sbf[:, n, :], Qf0[:, cs], start=False, stop=False), reads=[bSbf, bQf], writes=[bO])
                    S.op("pe", lambda e: e.matmul(psO[:, ko], Sbf[:, n, :], Qb0[:, cs], start=False, stop=True), reads=[bSbf, bQb], writes=[bO])
                gcol = 8 + kind * 4 + hg
                mrow = 1024 + kind * 512 + hg * 128
                finalize(psO[:], [bO], P[4], bP[4], gn_sb[:, gcol:gcol + 1], [bgn], gt_sb[:, g * 512:(g + 1) * 512], [bgt],
                         Md[mrow:mrow + 128, g * 512:(g + 1) * 512])

        S.op("dve", lambda e: e.tensor_tensor(out=tmp64[:, 0:64], in0=lamp_sb[:, 0:64], in1=lamp_sb[:, 64:128], op=ALU.mult), reads=[blamp], writes=[bt64])
        S.op("dve", lambda e: e.tensor_tensor(out=tmp64[:, 64:128], in0=lamp_sb[:, 128:192], in1=lamp_sb[:, 192:256], op=ALU.mult), reads=[blamp], writes=[bt64])
        S.op("dve", lambda e: e.reduce_sum(sm[:, 2:3], tmp64[:, 0:64], axis=mybir.AxisListType.X), reads=[bt64], writes=[bsm])
        S.op("dve", lambda e: e.reduce_sum(sm[:, 3:4], tmp64[:, 64:128], axis=mybir.AxisListType.X), reads=[bt64], writes=[bsm])
        S.op("act", lambda e: e.activation(sm[:, 2:4], sm[:, 2:4], AF.Exp), reads=[bsm], writes=[bsm])
        S.op("dve", lambda e: e.tensor_tensor(out=sm[:, 4:5], in0=sm[:, 3:4], in1=sm[:, 2:3], op=ALU.subtract), reads=[bsm], writes=[bsm])
        S.op("dve", lambda e: e.tensor_tensor(out=sm[:, 4:5], in0=sm[:, 4:5], in1=lc_sb[:, 0:1], op=ALU.subtract), reads=[bsm, blc], writes=[bsm])

        EB = S.sbuf("EB", [128, 6, 512], BF16); bEB = Buf()
        bst = S.sbuf("bst", [128, 512], F32); bbst = Buf()
        q1 = S.sbuf("q1", [64, T], BF16); q2 = S.sbuf("q2", [64, T], BF16)
        k1 = S.sbuf("k1", [64, T], BF16); k2 = S.sbuf("k2", [64, T], BF16)
        bq = [Buf() for _ in range(4)]
        vd = S.sbuf("vd", [128, NCH, 128], BF16); bvd = Buf()
        Et = [S.sbuf("Et%d" % i, [128, 512], BF16) for i in range(2)]; bEt = [Buf(), Buf()]
        E = [S.sbuf("E%d" % i, [128, 512], BF16) for i in range(4)]; bE = [Buf() for _ in range(4)]
        r1 = S.sbuf("r1", [128, 512], F32); br1 = Buf()
        r2 = S.sbuf("r2", [128, 512], F32); br2 = Buf()
        od = S.sbuf("od", [128, 512], F32); bod = Buf()
        steps = [(b, qb, kt_) for b in range(8) for qb in range(T // 512) for kt_ in range(NCH)]
        loaded = [-1]
        pO = [(P[4], bP[4]), (P[5], bP[5])]
        pR = [(P[6], bP[6]), (P[7], bP[7])]
        Eof = {}

        def load_batch(b):
            rows = (PT_DQ + b * 128, PT_DQ + b * 128 + 64, PT_DK + b * 128, PT_DK + b * 128 + 64)
            for i, tl in enumerate((q1, q2, k1, k2)):
                S.dma("sp", tl[:], PTd[rows[i]:rows[i] + 64, :], writes=[bq[i]])
            S.dma("sp", vd[:], PVd[:, PV_DV + b * 128:PV_DV + (b + 1) * 128].rearrange("(n p) c -> p n c", p=128), writes=[bvd])
            S.dma("sp", cb_sb[:], cb[b], writes=[bcb])
            for i in range(6):
                S.dma("sp", bst[:], btile[b, i], writes=[bbst])
                S.op("act", lambda e: e.activation(EB[:, i, :], bst[:], AF.Exp), reads=[bbst], writes=[bEB])
            S.op("dve", lambda e: e.tensor_tensor(out=sm[:, 5:6], in0=gn_sb[:, b:b + 1], in1=lc_sb[:, 1:2], op=ALU.mult), reads=[bgn, blc], writes=[bsm])
            loaded[0] = b

        def scores(i):
            b, qb, kt_ = steps[i]
            if loaded[0] != b:
                load_batch(b)
            qs = slice(qb * 512, (qb + 1) * 512)
            ks = slice(kt_ * 128, (kt_ + 1) * 128)
            d = kt_ * 128 - qb * 512
            for si, (kk, qq) in enumerate(((k1, q1), (k2, q2))):
                ps, bps = P[(i % 2) * 2 + si], bP[(i % 2) * 2 + si]
                S.op("pe", lambda e: e.matmul(ps[:], kk[:, ks], qq[:, qs], start=True, stop=True),
                     reads=[bq[2 + si], bq[si]], writes=[bps], also=([bP[(i % 2) * 2 + 1]] if si == 0 else []))
                ei = (i % 2) * 2 + si
                Ecur, bEcur = E[ei], bE[ei]
                if d in NEAR_D:
                    et, bet = Et[si], bEt[si]
                    S.op("act", lambda e: e.activation(et[:], ps[:], AF.Exp), reads=[bps], writes=[bet])
                    S.op("dve", lambda e: e.tensor_tensor(out=Ecur[:], in0=et[:], in1=EB[:, NEAR_D.index(d), :], op=ALU.mult),
                         reads=[bet, bEB], writes=[bEcur])
                else:
                    col = 0 if d < 0 else 1
                    S.op("act", lambda e: e.activation(Ecur[:], ps[:], AF.Exp, bias=cb_sb[:, col:col + 1]), reads=[bps, bcb], writes=[bEcur])

        def pv(i):
            b, qb, kt_ = steps[i]
            for si in range(2):
                ei = (i % 2) * 2 + si
                Ecur, bEcur = E[ei], bE[ei]
                S.op("pe", lambda e: e.matmul(pO[si][0][:], vd[:, kt_, :], Ecur[:], start=(kt_ == 0), stop=(kt_ == NCH - 1)),
                     reads=[bvd, bEcur], writes=[pO[si][1]], also=([bE[(i % 2) * 2 + 1]] if si == 0 else []))
                S.op("pe", lambda e: e.matmul(pR[si][0][:], ones[:], Ecur[:], start=(kt_ == 0), stop=(kt_ == NCH - 1)),
                     reads=[bones, bEcur], writes=[pR[si][1]])

        def fin(i):
            b, qb, kt_ = steps[i]
            qs = slice(qb * 512, (qb + 1) * 512)
            S.op("dve", lambda e: e.reciprocal(r1[:], pR[0][0][:]), reads=[pR[0][1]], writes=[br1])
            S.op("dve", lambda e: e.reciprocal(r2[:], pR[1][0][:]), reads=[pR[1][1]], writes=[br2])
            S.op("dve", lambda e: e.tensor_tensor(out=r1[:], in0=pO[0][0][:], in1=r1[:], op=ALU.mult), reads=[pO[0][1], br1], writes=[br1])
            S.op("dve", lambda e: e.tensor_tensor(out=r2[:], in0=pO[1][0][:], in1=r2[:], op=ALU.mult), reads=[pO[1][1], br2], writes=[br2])
            S.op("dve", lambda e: e.scalar_tensor_tensor(out=od[:], in0=r2[:], scalar=sm[:, 4:5], in1=r1[:], op0=ALU.mult, op1=ALU.add),
                 reads=[br1, br2, bsm], writes=[bod])
            finalize(od[:], [bod], pR[0][0], pR[0][1], sm[:, 5:6], [bsm], None, [], Md[b * 128:(b + 1) * 128, qs])

        scores(0)
        for i in range(len(steps)):
            nxt_same = (i + 1 < len(steps)) and steps[i + 1][0] == steps[i][0]
            if nxt_same:
                scores(i + 1)
            pv(i)
            if steps[i][2] == NCH - 1:
                fin(i)
            if (i + 1 < len(steps)) and not nxt_same:
                scores(i + 1)


def _t5_bucket(rel):
    half, max_exact = 16, 8
    ret = np.where(rel > 0, half, 0)
    n = np.abs(rel)
    nf = np.maximum(n, 1).astype(np.float32)
    large = max_exact + (np.log(nf / max_exact) / math.log(128 / max_exact) * (half - max_exact)).astype(np.int32)
    large = np.minimum(large, half - 1)
    return ret + np.where(n < max_exact, n, large)


def _cmask():
    j = np.arange(128)[:, None]
    i = np.arange(128)[None, :]
    m = np.stack([(j <= i), (j >= i), (j > i), (j < i)], axis=1).astype(np.float32)
    return np.ascontiguousarray(m)


def build_F(depth, final):
    nc = bass.Bass("TRN2", target_bir_lowering=False)
    def din(name, shape, dt=F32):
        return nc.dram_tensor(name, shape, dt, kind="ExternalInput").ap()
    x0 = din("x0", [D, T])
    w_ext = din("w_ext", [depth, D, W_EXT])
    wo = din("wo", [depth, D, D])
    wup = din("wup", [depth, D, 2 * DFF])
    wdn = din("wdn", [depth, DFF, D])
    g1 = din("g1", [depth, 128, 16])
    g2 = din("g2", [depth, 128, 16])
    gF = din("gF", [128, 16])
    cw = din("cw", [depth, 128, 88, 4])
    rotf = din("rotf", [128, 2, T])
    rott = din("rott", [128, 2, T // 128, 256])
    btile = din("btile", [8, 6, 128, 512])
    cb = din("cb", [8, 128, 2])
    lamp = din("lamp", [depth, 128, 256])
    lcst = din("lcst", [depth, 128, 2])
    gains = din("gains", [depth, 128, 16])
    gw = din("gw", [depth, 4, 33, 128])
    rdec = din("rdec", [depth, 4, 128, 2])
    cmask = din("cmask", [128, 4, 128])
    y = nc.dram_tensor("y", [D, T], F32, kind="ExternalOutput").ap()
    Xs = [nc.dram_tensor("Xa", [D, T], F32).ap(), nc.dram_tensor("Xb", [D, T], F32).ap()]
    PTd = nc.dram_tensor("PTd", [NFM, T], BF16).ap()
    PGd = nc.dram_tensor("PGd", [1024, T], F32).ap()
    PVd = nc.dram_tensor("PVd", [T, NTM], BF16).ap()
    Md = nc.dram_tensor("Md", [D, T], BF16).ap()
    with contextlib.ExitStack() as st:
        S = Sched(nc, st)
        P = [S.psum("P%d" % i, [128, 512]) for i in range(8)]
        bP = [Buf() for _ in range(8)]
        for L in range(depth):
            Xin = x0 if L == 0 else Xs[(L - 1) % 2]
            Xout = y if L == depth - 1 else Xs[L % 2]
            for q in range(4):
                qs = slice(q * NT, (q + 1) * NT)
                emit_A(S, P, bP, Xin[:, qs], g1[L], w_ext[L], rotf[:, :, qs], rott[:, :, q * 8:(q + 1) * 8, :],
                       PTd[:, qs], PGd[:, qs], PVd[qs, :])
            emit_B(S, P, bP, PTd, PGd, PVd, Md, btile, cb, lamp[L], lcst[L], gains[L], gw[L], rdec[L], cmask)
            for q in range(4):
                qs = slice(q * NT, (q + 1) * NT)
                emit_C(S, P, bP, Xin, Md, q, wo[L], g2[L], wup[L], cw[L], wdn[L], gF, Xout[:, qs], final and L == depth - 1)
        S.finish()
        build_F.n_inst = S.n_inst
    return nc


def prep_F(p, depth):
    cos, sin = _rot_tables()
    r = np.arange(128)
    sgn_r = np.where(r % 64 < 32, -1.0, 1.0).astype(np.float32)
    rotf = np.empty((128, 2, T), np.float32)
    rotf[:, 0, :] = cos[:, r % 32].T
    rotf[:, 1, :] = (sin[:, r % 32] * sgn_r[None, :]).T
    col = np.arange(256)
    sgn_c = np.where(col % 64 < 32, -1.0, 1.0).astype(np.float32)
    ct = cos[:, col % 32].reshape(T // 128, 128, 256).transpose(1, 0, 2)
    stb = (sin[:, col % 32] * sgn_c[None, :]).reshape(T // 128, 128, 256).transpose(1, 0, 2)
    rott = np.ascontiguousarray(np.stack([ct, stb], axis=1))
    bucket = _t5_bucket(np.arange(-(T - 1), T))
    kl = np.arange(128)[:, None]
    ql = np.arange(512)[None, :]
    btile = np.stack([np.stack([p['rel_bias'][bucket[d + kl - ql + T - 1], h] for d in NEAR_D]) for h in range(8)]).astype(np.float32)
    cb = np.empty((8, 128, 2), np.float32)
    cb[:, :, 0] = p['rel_bias'][15, :][:, None]
    cb[:, :, 1] = p['rel_bias'][31, :][:, None]
    lamp = np.ascontiguousarray(np.broadcast_to(p['diff_lambda'][:depth].reshape(depth, 1, 256), (depth, 128, 256))).astype(np.float32)
    lcst = np.empty((depth, 128, 2), np.float32)
    for L in range(depth):
        lam_init = 0.8 - 0.6 * math.exp(-0.3 * L)
        lcst[L, :, 0] = lam_init
        lcst[L, :, 1] = 1.0 - lam_init
    gains = np.ascontiguousarray(p['head_gain'][:depth].reshape(depth, 16, 128).transpose(0, 2, 1)).astype(np.float32)
    gw = np.zeros((depth, 4, 33, 128), np.float32)
    rdec = np.empty((depth, 4, 128, 2), np.float32)
    for L in range(depth):
        for hg in range(4):
            gw[L, hg, 0:16, 0:64] = p['gla_gate_w'][L][0][:, hg * 64:(hg + 1) * 64]
            gw[L, hg, 16:32, 64:128] = p['gla_gate_w'][L][1][:, hg * 64:(hg + 1) * 64]
            gw[L, hg, 32, 0:64] = p['gla_gate_b'][L][0][hg * 64:(hg + 1) * 64]
            gw[L, hg, 32, 64:128] = p['gla_gate_b'][L][1][hg * 64:(hg + 1) * 64]
            rdec[L, hg, :, 0] = p['ret_decay_logit'][L][0, hg]
            rdec[L, hg, :, 1] = p['ret_decay_logit'][L][1, hg]
    cw = np.empty((depth, 128, 88, 4), np.float32)
    for L in range(depth):
        for i in range(3):
            cw[L, :, :, i] = p['conv_w'][L][i].reshape(88, 128).T
        cw[L, :, :, 3] = p['conv_b'][L].reshape(88, 128).T
    shared = {
        "w_ext": np.ascontiguousarray(np.stack([_w_ext(p['w_in'][L]) for L in range(depth)])),
        "wo": np.ascontiguousarray(p['w_o'][:depth]), "wup": np.ascontiguousarray(p['w_up'][:depth]),
        "wdn": np.ascontiguousarray(p['w_down'][:depth]),
        "g1": np.stack([_vec16(p['ln1_g'][L]) for L in range(depth)]), "g2": np.stack([_vec16(p['ln2_g'][L]) for L in range(depth)]),
        "gF": _vec16(p['final_g']), "cw": cw, "rotf": rotf, "rott": rott, "btile": btile, "cb": cb, "lamp": lamp, "lcst": lcst,
        "gains": gains, "gw": gw, "rdec": rdec, "cmask": _cmask(),
    }
    x = p['x'].astype(np.float32)
    xT = [np.ascontiguousarray(x[b].T) for b in range(NB)]
    return [dict(shared, x0=xT[c // 4]) for c in range(NCORE)]


_PROG = {}


def run_F(p, depth, final):
    key = (depth, final)
    if key not in _PROG:
        _PROG[key] = build_F(depth, final)
    res = run_bass_kernel_spmd(_PROG[key], prep_F(p, depth), core_ids=list(range(NCORE)))
    return np.stack([np.asarray(res.results[4 * b]["y"]).T for b in range(NB)]).astype(np.float32)


def kernel(**p):
    p = {k: np.asarray(v) for k, v in p.items()}
    return np.ascontiguousarray(run_F(p, DEPTH, True))
```

```python
import contextlib
import math
import numpy as np
import ml_dtypes
import concourse.bass as bass
import concourse.mybir as mybir
from concourse.bass_utils import run_bass_kernel_spmd

F32 = mybir.dt.float32
BF16 = mybir.dt.bfloat16
AF = mybir.ActivationFunctionType
ALU = mybir.AluOpType
NPBF = ml_dtypes.bfloat16

D = 2048
T = 4096
NB = 2
NCORE = 8
NT = 1024
NTH = NT + 2
DEPTH = 4
IN_W = 6176
DFF = 5632
EPS = 1e-6
PT_DQ, PT_DK, PT_GQ, PT_GK, PT_RQ, PT_RK, PT_GA = 0, 1024, 2048, 2304, 2560, 2816, 3072
NFM = 3104
PV_DV, PV_GK, PV_GV, PV_RK, PV_RV = 0, 1024, 1280, 1792, 2048
NTM = 2560
W_EXT = IN_W + 512


class Buf:
    __slots__ = ("name", "last_w", "readers", "dma_sem", "dma_cnt")

    def __init__(self, name=""):
        self.name = name
        self.last_w = None
        self.readers = []
        self.dma_sem = None
        self.dma_cnt = 0


class Sched:
    def __init__(self, nc, stack):
        self.nc = nc
        self.stack = stack
        self.eng = {"pe": nc.tensor, "act": nc.scalar, "dve": nc.vector, "pool": nc.gpsimd, "sp": nc.sync}
        self.sem, self.cnt, self.seen = {}, {}, {}
        for k in self.eng:
            self.sem[k] = stack.enter_context(nc.semaphore("s_" + k))
            self.cnt[k] = 0
            self.seen[k] = {}
        self.nsem = len(self.eng)
        self.all_dma = []
        self.sem_pool = []
        self.gstack = stack
        self.n_inst = 0

    def barrier(self):
        deps = [(b.dma_sem, b.dma_cnt, "dma") for b in self.all_dma]
        for k in self.eng:
            if self.cnt[k] > 0:
                deps.append((self.sem[k], self.cnt[k], k))
        for e in self.eng:
            self._wait(e, deps)

    @contextlib.contextmanager
    def phase(self):
        outer = self.stack
        n0 = len(self.all_dma)
        with contextlib.ExitStack() as ps:
            self.stack = ps
            yield
            self.barrier()
            for b in self.all_dma[n0:]:
                self.sem_pool.append((b.dma_sem, b.dma_cnt))
            del self.all_dma[n0:]
        self.stack = outer

    def sbuf(self, name, shape, dt):
        self.n_tiles = getattr(self, "n_tiles", 0) + 1
        return self.stack.enter_context(self.nc.sbuf_tensor("%s_%d" % (name, self.n_tiles), shape, dt))

    def psum(self, name, shape, dt=F32):
        return self.stack.enter_context(self.nc.psum_tensor(name, shape, dt))

    def _wait(self, e, deps):
        best = {}
        for d in deps:
            if d is None:
                continue
            sem, val, en = d
            if e == "pe" and en == "pe":
                continue
            key = id(sem)
            if key not in best or best[key][1] < val:
                best[key] = (sem, val)
        for key, (sem, val) in best.items():
            if self.seen[e].get(key, 0) >= val:
                continue
            self.eng[e].wait_ge(sem, val)
            self.seen[e][key] = val

    @staticmethod
    def _deps(reads, writes):
        deps = []
        for b in reads:
            deps.append(b.last_w)
        for b in writes:
            deps.append(b.last_w)
            deps.extend(b.readers)
        return deps

    @staticmethod
    def _mark(tok, reads, writes):
        for b in writes:
            b.last_w = tok
            b.readers = []
        for b in reads:
            if b not in writes:
                b.readers = [r for r in b.readers if r[2] != tok[2] or tok[2] == "dma"] + [tok]

    def op(self, e, fn, reads=(), writes=()):
        self._wait(e, self._deps(reads, writes))
        ins = fn(self.eng[e])
        self.cnt[e] += 1
        ins.then_inc(self.sem[e], 1)
        self._mark((self.sem[e], self.cnt[e], e), reads, writes)
        self.n_inst += 1
        return ins

    def dma(self, e, out, in_, reads=(), writes=(), owner=None, **kw):
        self._wait(e, self._deps(reads, writes))
        if owner is None:
            owner = (list(writes) + list(reads))[0]
        if owner.dma_sem is None:
            if self.sem_pool:
                owner.dma_sem, owner.dma_cnt = self.sem_pool.pop()
            else:
                owner.dma_sem = self.gstack.enter_context(self.nc.semaphore("d_%d" % self.nsem))
                self.nsem += 1
            self.all_dma.append(owner)
        owner.dma_cnt += 16
        ins = self.eng[e].dma_start(out=out, in_=in_, **kw)
        ins.then_inc(owner.dma_sem, 16)
        self._mark((owner.dma_sem, owner.dma_cnt, "dma"), reads, writes)
        self.n_inst += 1
        return ins

    def finish(self):
        deps = [(b.dma_sem, b.dma_cnt, "dma") for b in self.all_dma]
        for k in self.eng:
            if k != "sp" and self.cnt[k] > 0:
                deps.append((self.sem[k], self.cnt[k], k))
        self._wait("sp", deps)


def _chunked(ap2d):
    return ap2d.rearrange("(c p) n -> p c n", p=128)


def _rms_rstd(S, nc, x_sb, bx, ncols, pieces, ones, bones, sq, bsq, pss, bpss, rstd, brstd, epst, beps, dscale):
    for ch in range(16):
        s = ch % 2
        S.op("act", lambda e: e.activation(sq[s][:, 0:ncols], x_sb[:, ch, 0:ncols], AF.Square),
             reads=[bx[ch]], writes=[bsq[s]])
        for pi, (c0, c1) in enumerate(pieces):
            S.op("pe", lambda e: e.matmul(pss[pi][:, 0:c1 - c0], ones[:], sq[s][:, c0:c1],
                                          start=(ch == 0), stop=(ch == 15)),
                 reads=[bsq[s], bones], writes=[bpss[pi]])
    for pi, (c0, c1) in enumerate(pieces):
        S.op("act", lambda e: e.activation(rstd[:, c0:c1], pss[pi][:, 0:c1 - c0], AF.Ln, bias=epst[:, 0:1], scale=dscale),
             reads=[bpss[pi], beps], writes=[brstd])
    S.op("act", lambda e: e.activation(rstd[:, 0:ncols], rstd[:, 0:ncols], AF.Exp, scale=-0.5),
         reads=[brstd], writes=[brstd])


def emit_A(S, P, bP, xT, g1, w, rotf, rott, PT, PG, PV):
    nc = S.nc
    with S.phase():
        x_sb = S.sbuf("x_sb", [128, 16, NT], F32); bx = [Buf() for _ in range(16)]
        h_sb = S.sbuf("h_sb", [128, 16, NT], BF16); bh = [Buf() for _ in range(16)]
        g_sb = S.sbuf("g_sb", [128, 16], F32); bg = Buf()
        rf_sb = S.sbuf("rf_sb", [128, 2, NT], F32); brf = Buf()
        rt_sb = S.sbuf("rt_sb", [128, 2, 8, 256], F32); brt = Buf()
        ones = S.sbuf("ones", [128, 128], BF16); bones = Buf()
        epst = S.sbuf("epst", [128, 1], F32); beps = Buf()
        sq = [S.sbuf("sq%d" % i, [128, NT], BF16) for i in range(2)]; bsq = [Buf(), Buf()]
        rstd = S.sbuf("rstd", [128, NT], F32); brstd = Buf()
        pan = [S.sbuf("pan%d" % i, [128, 16, 512], BF16) for i in range(2)]; bpan = [Buf(), Buf()]
        stg32 = [S.sbuf("stg32_%d" % i, [128, NT], F32) for i in range(2)]; bstg32 = [Buf(), Buf()]
        stg16 = [S.sbuf("stg16_%d" % i, [128, NT], BF16) for i in range(2)]; bstg16 = [Buf(), Buf()]
        stgT = [S.sbuf("stgT%d" % i, [128, 512], BF16) for i in range(3)]; bstgT = [Buf() for _ in range(3)]
        tmpA = S.sbuf("tmpA", [128, 512], F32); btA = Buf()
        tmpB = S.sbuf("tmpB", [128, 512], F32); btB = Buf()

        S.op("pool", lambda e: e.memset(ones[:], 1.0), writes=[bones])
        S.op("pool", lambda e: e.memset(epst[:], EPS), writes=[beps])
        S.dma("sp", g_sb[:], g1, writes=[bg])
        for ch in range(16):
            S.dma("sp", x_sb[:, ch, :], xT[ch * 128:(ch + 1) * 128, :], writes=[bx[ch]])
        S.dma("sp", rf_sb[:], rotf, writes=[brf])
        S.dma("sp", rt_sb[:], rott, writes=[brt])

        XQ, XK = IN_W, IN_W + 256
        panels = [
            (0, 512, [(j * 128, 128, PT, PT_DQ + j * 128, 0.125, 'bf') for j in range(4)], []),
            (512, 512, [(j * 128, 128, PT, PT_DQ + 512 + j * 128, 0.125, 'bf') for j in range(4)], []),
            (1024, 512, [(j * 128, 128, PT, PT_DK + j * 128, 1.0, 'bf') for j in range(4)], []),
            (1536, 512, [(j * 128, 128, PT, PT_DK + 512 + j * 128, 1.0, 'bf') for j in range(4)], []),
            (2048, 512, [], [(0, 512, PV_DV, 1.0, None)]),
            (2560, 512, [], [(0, 512, PV_DV + 512, 1.0, None)]),
            (3072, 512, [(0, 128, PT, PT_GQ, 0.125, 'bf'), (128, 128, PT, PT_GQ + 128, 0.125, 'bf'),
                         (256, 128, PT, PT_GK, 1.0, 'bf'), (384, 128, PT, PT_GK + 128, 1.0, 'bf')],
             [(256, 256, PV_GK, 1.0, None)]),
            (3584, 512, [], [(0, 512, PV_GV, 1.0, None)]),
            (4096, 512, [(j * 128, 128, PG, j * 128, 1.0, 'f32') for j in range(4)], []),
            (4608, 32, [(0, 32, PT, PT_GA, 1.0, 'bf')], []),
            ((4640, XQ), 512, [(0, 128, PT, PT_RQ, 1.0, ('rot', 256)), (128, 128, PT, PT_RQ + 128, 1.0, ('rot', 384))], []),
            ((4896, XK), 512, [(0, 128, PT, PT_RK, 0.125, ('rot', 256)), (128, 128, PT, PT_RK + 128, 0.125, ('rot', 384))],
             [(0, 256, PV_RK, 0.125, 256)]),
            (5152, 512, [], [(0, 512, PV_RV, 1.0, None)]),
            (5664, 512, [(j * 128, 128, PG, 512 + j * 128, 1.0, 'f32') for j in range(4)], []),
        ]

        def load_panel(pi):
            c0, ncols, _, _ = panels[pi]
            s = pi % 2
            if isinstance(c0, tuple):
                S.dma("pool", pan[s][:, :, 0:256], _chunked(w[:, c0[0]:c0[0] + 256]), writes=[bpan[s]])
                S.dma("pool", pan[s][:, :, 256:512], _chunked(w[:, c0[1]:c0[1] + 256]), writes=[bpan[s]])
            else:
                S.dma("pool", pan[s][:, :, 0:ncols], _chunked(w[:, c0:c0 + ncols]), writes=[bpan[s]])

        load_panel(0)
        _rms_rstd(S, nc, x_sb, bx, NT, [(0, 512), (512, 1024)], ones, bones, sq, bsq, [P[6], P[7]], [bP[6], bP[7]],
                  rstd, brstd, epst, beps, 1.0 / D)
        for ch in range(16):
            S.op("dve", lambda e: e.scalar_tensor_tensor(out=h_sb[:, ch, :], in0=x_sb[:, ch, :], scalar=g_sb[:, ch:ch + 1],
                                                         in1=rstd[:], op0=ALU.mult, op1=ALU.mult),
                 reads=[bx[ch], bg, brstd], writes=[bh[ch]])

        rot = [0]
        evc = [0]

        def next_ps():
            i = rot[0] % 6
            rot[0] += 1
            return P[i], bP[i]

        def evac(dst_ap, src_ap, scale, reads, writes):
            if evc[0] % 2 == 0:
                S.op("act", lambda e: e.activation(dst_ap, src_ap, AF.Copy, scale=float(scale)), reads=reads, writes=writes)
            else:
                S.op("dve", lambda e: e.tensor_scalar(out=dst_ap, in0=src_ap, scalar1=float(scale), scalar2=None, op0=ALU.mult),
                     reads=reads, writes=writes)
            evc[0] += 1

        n32 = n16 = nT = 0
        for pi, (c0, ncols, fm_jobs, tm_jobs) in enumerate(panels):
            s = pi % 2
            if pi + 1 < len(panels):
                load_panel(pi + 1)
            pn = pan[s]
            for (off, M, dst, row0, scale, kind) in fm_jobs:
                if kind == 'f32':
                    stg, bstg = stg32[n32 % 2], bstg32[n32 % 2]; n32 += 1
                else:
                    stg, bstg = stg16[n16 % 2], bstg16[n16 % 2]; n16 += 1
                for th in range(2):
                    ps, bps = next_ps()
                    for ch in range(16):
                        S.op("pe", lambda e: e.matmul(ps[0:M, :], pn[:, ch, off:off + M], h_sb[:, ch, th * 512:(th + 1) * 512],
                                                      start=(ch == 0), stop=(ch == 15)),
                             reads=[bpan[s], bh[ch]], writes=[bps])
                    if isinstance(kind, tuple):
                        soff = kind[1]
                        ps2, bps2 = next_ps()
                        for ch in range(16):
                            S.op("pe", lambda e: e.matmul(ps2[0:M, :], pn[:, ch, soff:soff + M], h_sb[:, ch, th * 512:(th + 1) * 512],
                                                          start=(ch == 0), stop=(ch == 15)),
                                 reads=[bpan[s], bh[ch]], writes=[bps2])
                        S.op("dve", lambda e: e.tensor_tensor(out=tmpA[:], in0=ps[:], in1=rf_sb[:, 0, th * 512:(th + 1) * 512], op=ALU.mult),
                             reads=[bps, brf], writes=[btA])
                        S.op("dve", lambda e: e.tensor_tensor(out=tmpB[:], in0=ps2[:], in1=rf_sb[:, 1, th * 512:(th + 1) * 512], op=ALU.mult),
                             reads=[bps2, brf], writes=[btB])
                        S.op("dve", lambda e: e.tensor_tensor(out=tmpA[:], in0=tmpA[:], in1=tmpB[:], op=ALU.add),
                             reads=[btA, btB], writes=[btA])
                        S.op("dve", lambda e: e.tensor_scalar(out=stg[:, th * 512:(th + 1) * 512], in0=tmpA[:], scalar1=float(scale),
                                                              scalar2=None, op0=ALU.mult),
                             reads=[btA], writes=[bstg])
                    else:
                        evac(stg[0:M, th * 512:(th + 1) * 512], ps[0:M, :], scale, [bps], [bstg])
                S.dma("sp", dst[row0:row0 + M, :], stg[0:M, :], reads=[bstg], owner=bstg)
            for (off, ncl, pv0, scale, soff) in tm_jobs:
                for tt in range(8):
                    ps, bps = next_ps()
                    for ch in range(16):
                        S.op("pe", lambda e: e.matmul(ps[:, 0:ncl], h_sb[:, ch, tt * 128:(tt + 1) * 128], pn[:, ch, off:off + ncl],
                                                      start=(ch == 0), stop=(ch == 15)),
                             reads=[bpan[s], bh[ch]], writes=[bps])
                    stt, bstt = stgT[nT % 3], bstgT[nT % 3]; nT += 1
                    if soff is not None:
                        ps2, bps2 = next_ps()
                        for ch in range(16):
                            S.op("pe", lambda e: e.matmul(ps2[:, 0:ncl], h_sb[:, ch, tt * 128:(tt + 1) * 128], pn[:, ch, soff:soff + ncl],
                                                          start=(ch == 0), stop=(ch == 15)),
                                 reads=[bpan[s], bh[ch]], writes=[bps2])
                        S.op("dve", lambda e: e.tensor_tensor(out=tmpA[:, 0:ncl], in0=ps[:, 0:ncl], in1=rt_sb[:, 0, tt, :], op=ALU.mult),
                             reads=[bps, brt], writes=[btA])
                        S.op("dve", lambda e: e.tensor_tensor(out=tmpB[:, 0:ncl], in0=ps2[:, 0:ncl], in1=rt_sb[:, 1, tt, :], op=ALU.mult),
                             reads=[bps2, brt], writes=[btB])
                        S.op("dve", lambda e: e.tensor_tensor(out=tmpA[:, 0:ncl], in0=tmpA[:, 0:ncl], in1=tmpB[:, 0:ncl], op=ALU.add),
                             reads=[btA, btB], writes=[btA])
                        S.op("dve", lambda e: e.tensor_scalar(out=stt[:, 0:ncl], in0=tmpA[:, 0:ncl], scalar1=float(scale), scalar2=None, op0=ALU.mult),
                             reads=[btA], writes=[bstt])
                    else:
                        evac(stt[:, 0:ncl], ps[:, 0:ncl], scale, [bps], [bstt])
                    S.dma("sp", PV[tt * 128:(tt + 1) * 128, pv0:pv0 + ncl], stt[:, 0:ncl], reads=[bstt], owner=bstt)


def emit_C(S, P, bP, Xin, Md, q, wo, g2, wup, cw, wdn, gF, xo, final):
    nc = S.nc
    PIECES = [(0, 512), (512, 1024), (1024, NTH)]
    with S.phase():
        x_sb = S.sbuf("x_sb", [128, 16, NTH], F32); bx = [Buf() for _ in range(16)]
        m_sb = S.sbuf("m_sb", [128, 16, NTH], BF16); bm = [Buf() for _ in range(16)]
        act_sb = S.sbuf("act_sb", [128, 22, NT], BF16); bact = [Buf() for _ in range(22)]
        g_sb = S.sbuf("g_sb", [128, 16], F32); bg = Buf()
        gF_sb = S.sbuf("gF_sb", [128, 16], F32); bgF = Buf()
        cw_sb = S.sbuf("cw_sb", [128, 88, 4], F32); bcw = Buf()
        ones = S.sbuf("ones", [128, 128], BF16); bones = Buf()
        epst = S.sbuf("epst", [128, 1], F32); beps = Buf()
        sq = [S.sbuf("sq%d" % i, [128, NTH], BF16) for i in range(2)]; bsq = [Buf(), Buf()]
        rstd = S.sbuf("rstd", [128, NTH], F32); brstd = Buf()
        pool_ = [S.sbuf("pool%d" % i, [128, 8192], BF16) for i in range(2)]; bpool = [Buf(), Buf()]
        ua = S.sbuf("ua", [128, NTH], F32); bua = Buf()
        ug = S.sbuf("ug", [128, NTH], F32); bug = Buf()
        ya = S.sbuf("ya", [128, NT], F32); bya = Buf()
        yg = S.sbuf("yg", [128, NT], F32); byg = Buf()

        S.op("pool", lambda e: e.memset(ones[:], 1.0), writes=[bones])
        S.op("pool", lambda e: e.memset(epst[:], EPS), writes=[beps])
        S.dma("sp", g_sb[:], g2, writes=[bg])
        S.dma("sp", gF_sb[:], gF, writes=[bgF])
        S.dma("sp", cw_sb[:], cw, writes=[bcw])
        c0 = q * NT
        for ch in range(16):
            S.dma("sp", m_sb[:, ch, 0:NT], Md[ch * 128:(ch + 1) * 128, c0:c0 + NT], writes=[bm[ch]])
        for ch in range(16):
            S.dma("sp", x_sb[:, ch, 0:NT], Xin[ch * 128:(ch + 1) * 128, c0:c0 + NT], writes=[bx[ch]])
        for (col, src) in ((NT, c0 - 1), (NT + 1, c0 + NT)):
            if 0 <= src < T:
                S.dma("sp", m_sb[:, :, col:col + 1], Md[:, src:src + 1].rearrange("(c p) o -> p c o", p=128), writes=bm, allow_slow_non_contiguous=True)
                S.dma("sp", x_sb[:, :, col:col + 1], Xin[:, src:src + 1].rearrange("(c p) o -> p c o", p=128), writes=bx, allow_slow_non_contiguous=True)
            else:
                S.op("pool", lambda e: e.memset(m_sb[:, :, col:col + 1], 0.0), writes=bm)
                S.op("pool", lambda e: e.memset(x_sb[:, :, col:col + 1], 0.0), writes=bx)

        jobs = []
        rot = [0]

        def next_ps(n=7):
            i = rot[0] % n
            rot[0] += 1
            return P[i], bP[i]

        def mk_wo(pi):
            def load(s):
                v = pool_[s][:].rearrange("p (c n) -> p c n", c=16)
                S.dma("pool", v, _chunked(wo[:, pi * 512:(pi + 1) * 512]), writes=[bpool[s]])

            def comp(s):
                v = pool_[s][:].rearrange("p (c n) -> p c n", c=16)
                for j in range(4):
                    fb = pi * 4 + j
                    for (c0, c1) in PIECES:
                        ps, bps = next_ps()
                        for ch in range(16):
                            S.op("pe", lambda e: e.matmul(ps[:, 0:c1 - c0], v[:, ch, j * 128:(j + 1) * 128], m_sb[:, ch, c0:c1],
                                                          start=(ch == 0), stop=(ch == 15)),
                                 reads=[bpool[s], bm[ch]], writes=[bps])
                        S.op("dve", lambda e: e.tensor_tensor(out=x_sb[:, fb, c0:c1], in0=ps[:, 0:c1 - c0], in1=x_sb[:, fb, c0:c1], op=ALU.add),
                             reads=[bps, bx[fb]], writes=[bx[fb]])
            return load, comp

        for pi in range(4):
            jobs.append(mk_wo(pi))

        def norm2(s_unused=None):
            _rms_rstd(S, nc, x_sb, bx, NTH, PIECES, ones, bones, sq, bsq, [P[5], P[6], P[7]], [bP[5], bP[6], bP[7]],
                      rstd, brstd, epst, beps, 1.0 / D)
            for ch in range(16):
                S.op("dve", lambda e: e.scalar_tensor_tensor(out=m_sb[:, ch, :], in0=x_sb[:, ch, :], scalar=g_sb[:, ch:ch + 1],
                                                             in1=rstd[:], op0=ALU.mult, op1=ALU.mult),
                     reads=[bx[ch], bg, brstd], writes=[bm[ch]])

        def mk_up(hf, pp):
            a0 = (hf * 22 + pp * 2) * 128

            def load(s):
                v = pool_[s][:].rearrange("p (c n) -> p c n", c=16)
                S.dma("pool", v[:, :, 0:256], _chunked(wup[:, a0:a0 + 256]), writes=[bpool[s]])
                S.dma("pool", v[:, :, 256:512], _chunked(wup[:, DFF + a0:DFF + a0 + 256]), writes=[bpool[s]])

            def comp(s):
                v = pool_[s][:].rearrange("p (c n) -> p c n", c=16)
                for j in range(2):
                    li = pp * 2 + j
                    blk = hf * 22 + li
                    for (coff, u, bu, y, by, wi) in ((j * 128, ua, bua, ya, bya, blk), (256 + j * 128, ug, bug, yg, byg, 44 + blk)):
                        pss = []
                        for (c0, c1) in PIECES:
                            if c0 == 1024:
                                ps, bps = P[7], bP[7]
                            else:
                                ps, bps = next_ps(7)
                            for ch in range(16):
                                S.op("pe", lambda e: e.matmul(ps[:, 0:c1 - c0], v[:, ch, coff:coff + 128], m_sb[:, ch, c0:c1],
                                                              start=(ch == 0), stop=(ch == 15)),
                                     reads=[bpool[s], bm[ch]], writes=[bps])
                            pss.append((ps, bps))
                        S.op("act", lambda e: e.activation(u[:, 1:513], pss[0][0][:, 0:512], AF.Copy), reads=[pss[0][1]], writes=[bu])
                        S.op("act", lambda e: e.activation(u[:, 513:1025], pss[1][0][:, 0:512], AF.Copy), reads=[pss[1][1]], writes=[bu])
                        S.op("act", lambda e: e.activation(u[:, 0:1], pss[2][0][:, 0:1], AF.Copy), reads=[pss[2][1]], writes=[bu])
                        S.op("act", lambda e: e.activation(u[:, 1025:1026], pss[2][0][:, 1:2], AF.Copy), reads=[pss[2][1]], writes=[bu])
                        S.op("dve", lambda e: e.tensor_scalar(out=y[:], in0=u[:, 1:1025], scalar1=cw_sb[:, wi, 1:2], scalar2=cw_sb[:, wi, 3:4],
                                                              op0=ALU.mult, op1=ALU.add), reads=[bu, bcw], writes=[by])
                        S.op("dve", lambda e: e.scalar_tensor_tensor(out=y[:], in0=u[:, 0:1024], scalar=cw_sb[:, wi, 0:1], in1=y[:],
                                                                     op0=ALU.mult, op1=ALU.add), reads=[bu, bcw, by], writes=[by])
                        S.op("dve", lambda e: e.scalar_tensor_tensor(out=y[:], in0=u[:, 2:1026], scalar=cw_sb[:, wi, 2:3], in1=y[:],
                                                                     op0=ALU.mult, op1=ALU.add), reads=[bu, bcw, by], writes=[by])
                    S.op("act", lambda e: e.activation(yg[:], yg[:], AF.Silu), reads=[byg], writes=[byg])
                    S.op("dve", lambda e: e.tensor_tensor(out=act_sb[:, li, :], in0=ya[:], in1=yg[:], op=ALU.mult),
                         reads=[bya, byg], writes=[bact[li]])
            return load, comp

        def mk_dn(hf, pj):
            def load(s):
                v = pool_[s][:, 0:22 * 256].rearrange("p (c n) -> p c n", c=22)
                src = wdn[hf * 22 * 128:(hf + 1) * 22 * 128, pj * 256:(pj + 1) * 256].rearrange("(c p) n -> p c n", p=128)
                S.dma("pool", v, src, writes=[bpool[s]])

            def comp(s):
                v = pool_[s][:, 0:22 * 256].rearrange("p (c n) -> p c n", c=22)
                for j in range(2):
                    fb = pj * 2 + j
                    for th in range(2):
                        ps, bps = next_ps()
                        for li in range(22):
                            S.op("pe", lambda e: e.matmul(ps[:], v[:, li, j * 128:(j + 1) * 128], act_sb[:, li, th * 512:(th + 1) * 512],
                                                          start=(li == 0), stop=(li == 21)),
                                 reads=[bpool[s], bact[li]], writes=[bps])
                        S.op("dve", lambda e: e.tensor_tensor(out=x_sb[:, fb, th * 512:(th + 1) * 512], in0=ps[:],
                                                              in1=x_sb[:, fb, th * 512:(th + 1) * 512], op=ALU.add),
                             reads=[bps, bx[fb]], writes=[bx[fb]])
            return load, comp

        jobs.append((None, norm2))
        for hf in range(2):
            for pp in range(11):
                jobs.append(mk_up(hf, pp))
            for pj in range(8):
                jobs.append(mk_dn(hf, pj))

        loads = [(i, j[0]) for i, j in enumerate(jobs) if j[0] is not None]
        slot_of = {}
        for k, (i, _) in enumerate(loads):
            slot_of[i] = k % 2
        nxt = 0
        loads[0][1](slot_of[loads[0][0]]); nxt = 1
        for i, (ld, comp) in enumerate(jobs):
            if ld is not None:
                if nxt < len(loads):
                    loads[nxt][1](slot_of[loads[nxt][0]]); nxt += 1
                comp(slot_of[i])
            else:
                comp()

        if final:
            _rms_rstd(S, nc, x_sb, bx, NT, [(0, 512), (512, 1024)], ones, bones, sq, bsq, [P[5], P[6]], [bP[5], bP[6]],
                      rstd, brstd, epst, beps, 1.0 / D)
            for ch in range(16):
                S.op("dve", lambda e: e.scalar_tensor_tensor(out=x_sb[:, ch, 0:NT], in0=x_sb[:, ch, 0:NT], scalar=gF_sb[:, ch:ch + 1],
                                                             in1=rstd[:, 0:NT], op0=ALU.mult, op1=ALU.mult),
                     reads=[bx[ch], bgF, brstd], writes=[bx[ch]])
        for ch in range(16):
            S.dma("sp", xo[ch * 128:(ch + 1) * 128, :], x_sb[:, ch, 0:NT], reads=[bx[ch]], owner=bx[ch])


def _vec16(v):
    return np.ascontiguousarray(np.asarray(v, np.float32).reshape(16, 128).T)


def _rot_tables():
    half = 32
    inv = (1.0 / (10000.0 ** np.linspace(0.0, 1.0, half, dtype=np.float32))).astype(np.float32)
    ang = np.arange(T, dtype=np.float32)[:, None] * inv[None, :]
    return np.cos(ang).astype(np.float32), np.sin(ang).astype(np.float32)


def _rot_core_tables(cos, sin, c):
    pos = (c % 4) * NT + np.arange(NT)
    r = np.arange(128)
    sgn_r = np.where(r % 64 < 32, -1.0, 1.0).astype(np.float32)
    rotf = np.empty((128, 2, NT), np.float32)
    rotf[:, 0, :] = cos[pos][:, r % 32].T
    rotf[:, 1, :] = (sin[pos][:, r % 32] * sgn_r[None, :]).T
    col = np.arange(256)
    sgn_c = np.where(col % 64 < 32, -1.0, 1.0).astype(np.float32)
    ct = cos[pos][:, col % 32].reshape(8, 128, 256).transpose(1, 0, 2)
    stb = (sin[pos][:, col % 32] * sgn_c[None, :]).reshape(8, 128, 256).transpose(1, 0, 2)
    rott = np.ascontiguousarray(np.stack([ct, stb], axis=1))
    return rotf, rott


def _w_ext(w_in_l):
    def swapped(c0):
        idx = np.arange(256)
        src = c0 + (idx // 64) * 64 + ((idx % 64) + 32) % 64
        return w_in_l[:, src]
    return np.ascontiguousarray(np.concatenate([w_in_l, swapped(4640), swapped(4896)], axis=1))


NEAR_D = (-128, 0, 128, 256, 384, 512)


def emit_B(S, P, bP, PTd, PGd, PVd, Md, btile, cb, lamp, lcst, gains, gw, rdec, cmask):
    nc = S.nc
    NCH = T // 128
    with S.phase():
        ones = S.sbuf("ones", [128, 128], BF16); bones = Buf()
        onesF = S.sbuf("onesF", [128, 512], F32); bonesF = Buf()
        epst = S.sbuf("epst", [128, 1], F32); beps = Buf()
        one1 = S.sbuf("one1", [128, 1], F32); bone1 = Buf()
        cm = S.sbuf("cm", [128, 4, 128], F32); bcm = Buf()
        M4 = S.sbuf("M4", [128, 2, 512], F32); bM4 = Buf()
        cb_sb = S.sbuf("cb_sb", [128, 2], F32); bcb = Buf()
        lamp_sb = S.sbuf("lamp_sb", [128, 256], F32); blamp = Buf()
        lc_sb = S.sbuf("lc_sb", [128, 2], F32); blc = Buf()
        gn_sb = S.sbuf("gn_sb", [128, 16], F32); bgn = Buf()
        rd_sb = S.sbuf("rd_sb", [128, 2], F32); brd = Buf()
        sm = S.sbuf("sm", [128, 16], F32); bsm = Buf()
        tmp64 = S.sbuf("tmp64", [128, 128], F32); bt64 = Buf()
        S.op("pool", lambda e: e.memset(ones[:], 1.0), writes=[bones])
        S.op("pool", lambda e: e.memset(onesF[:], 1.0), writes=[bonesF])
        S.op("pool", lambda e: e.memset(epst[:], EPS), writes=[beps])
        S.op("pool", lambda e: e.memset(one1[:], 1.0), writes=[bone1])
        S.dma("sp", cm[:], cmask, writes=[bcm])
        S.dma("sp", lamp_sb[:], lamp, writes=[blamp])
        S.dma("sp", lc_sb[:], lcst, writes=[blc])
        S.dma("sp", gn_sb[:], gains, writes=[bgn])
        for k in range(4):
            S.op("dve", lambda e: e.tensor_copy(M4[:, 0, k * 128:(k + 1) * 128], cm[:, 0, :]), reads=[bcm], writes=[bM4])
            S.op("dve", lambda e: e.tensor_copy(M4[:, 1, k * 128:(k + 1) * 128], cm[:, 1, :]), reads=[bcm], writes=[bM4])

        o_sb = S.sbuf("o_sb", [128, 512], F32); bo = Buf()
        sq_sb = S.sbuf("sq_sb", [128, 512], BF16); bsq = Buf()
        rs_sb = S.sbuf("rs_sb", [128, 512], F32); brs = Buf()
        stg = [S.sbuf("stg%d" % i, [128, 512], BF16) for i in range(2)]; bstg = [Buf(), Buf()]
        nst = [0]

        def finalize(src_ap, src_bufs, pss, bpss, scal_ap, scal_bufs, gate_ap, gate_bufs, dst_ap):
            S.op("act", lambda e: e.activation(o_sb[:], src_ap, AF.Copy), reads=src_bufs, writes=[bo])
            S.op("act", lambda e: e.activation(sq_sb[:], src_ap, AF.Square), reads=src_bufs, writes=[bsq])
            S.op("pe", lambda e: e.matmul(pss[:], ones[:], sq_sb[:], start=True, stop=True), reads=[bones, bsq], writes=[bpss])
            S.op("act", lambda e: e.activation(rs_sb[:], pss[:], AF.Ln, bias=epst[:, 0:1], scale=1.0 / 128), reads=[bpss, beps], writes=[brs])
            S.op("act", lambda e: e.activation(rs_sb[:], rs_sb[:], AF.Exp, scale=-0.5), reads=[brs], writes=[brs])
            S.op("dve", lambda e: e.tensor_tensor(out=o_sb[:], in0=o_sb[:], in1=rs_sb[:], op=ALU.mult), reads=[bo, brs], writes=[bo])
            sg, bsg = stg[nst[0] % 2], bstg[nst[0] % 2]; nst[0] += 1
            if gate_ap is not None:
                S.op("dve", lambda e: e.scalar_tensor_tensor(out=sg[:], in0=o_sb[:], scalar=scal_ap, in1=gate_ap, op0=ALU.mult, op1=ALU.mult),
                     reads=[bo] + scal_bufs + gate_bufs, writes=[bsg])
            else:
                S.op("dve", lambda e: e.tensor_scalar(out=sg[:], in0=o_sb[:], scalar1=scal_ap, scalar2=None, op0=ALU.mult),
                     reads=[bo] + scal_bufs, writes=[bsg])
            S.dma("sp", dst_ap, sg[:], reads=[bsg], owner=bsg)

        qd = S.sbuf("qd", [128, T], BF16); bqd = Buf()
        kd = S.sbuf("kd", [128, T], BF16); bkd = Buf()
        Qf0 = S.sbuf("Qf0", [128, T], BF16); bQf = Buf()
        Qb0 = S.sbuf("Qb0", [128, T], BF16); bQb = Buf()
        Kp = S.sbuf("Kp", [128, T], BF16); bKp = Buf()
        ktv = S.sbuf("ktv", [128, NCH, 192], BF16); bktv = Buf()
        kend = S.sbuf("kend", [128, NCH, 128], BF16); bkend = Buf()
        KVa = S.sbuf("KVa", [128, NCH, 128], F32); bKVf = Buf(); bKVb = Buf()
        Sbf = S.sbuf("Sbf", [128, NCH, 128], BF16); bSbf = Buf()
        gt_sb = S.sbuf("gt_sb", [128, T], F32); bgt = Buf()
        ga_sb = S.sbuf("ga_sb", [33, T], BF16); bga = Buf()
        gw_sb = S.sbuf("gw_sb", [33, 128], BF16); bgw = Buf()
        la4 = [S.sbuf("la4_%d" % i, [128, 4, 128], F32) for i in range(2)]; bla4 = [Buf(), Buf()]
        la_ret = S.sbuf("la_ret", [128, 4, 128], F32); blar = Buf()
        Epos = S.sbuf("Epos", [128, 512], F32); bEpos = Buf()
        Eneg = S.sbuf("Eneg", [128, 512], F32); bEneg = Buf()
        Eend = S.sbuf("Eend", [128, 4, 128], F32); bEend = Buf()
        Dall = S.sbuf("Dall", [128, NCH], F32); bD = Buf()
        Pf = S.sbuf("Pf", [128, 512], BF16); bPf = Buf()
        Pb = S.sbuf("Pb", [128, 512], BF16); bPb = Buf()

        S.op("pool", lambda e: e.memset(Qf0[:], 0.0), writes=[bQf])
        S.op("pool", lambda e: e.memset(Qb0[:], 0.0), writes=[bQb])
        S.op("pool", lambda e: e.memset(ga_sb[:], 1.0), writes=[bga])
        S.dma("sp", ga_sb[0:32, :], PTd[PT_GA:PT_GA + 32, :], writes=[bga])

        for kind, hg in [(kk_, hh_) for kk_ in range(2) for hh_ in range(4)]:
            qrow = (PT_GQ if kind == 0 else PT_RQ) + hg * 64
            krow = (PT_GK if kind == 0 else PT_RK) + hg * 64
            kcol = (PV_GK if kind == 0 else PV_RK) + hg * 64
            vcol = (PV_GV if kind == 0 else PV_RV) + hg * 128
            grow = (0 if kind == 0 else 512) + hg * 128
            for half in range(2):
                S.dma("sp", qd[half * 64:(half + 1) * 64, :], PTd[qrow:qrow + 64, :], writes=[bqd])
                S.dma("sp", kd[half * 64:(half + 1) * 64, :], PTd[krow:krow + 64, :], writes=[bkd])
            S.dma("sp", ktv[:, :, 0:64], PVd[:, kcol:kcol + 64].rearrange("(n p) c -> p n c", p=128), writes=[bktv])
            S.dma("sp", ktv[:, :, 64:192], PVd[:, vcol:vcol + 128].rearrange("(n p) c -> p n c", p=128), writes=[bktv])
            S.dma("sp", gt_sb[:], PGd[grow:grow + 128, :], writes=[bgt])
            if kind == 0:
                S.dma("pool", gw_sb[:], gw[hg], writes=[bgw])
            else:
                S.dma("sp", rd_sb[:], rdec[hg], writes=[brd])
                S.op("act", lambda e: e.activation(sm[:, 0:2], rd_sb[:], AF.Exp, scale=-1.0), reads=[brd], writes=[bsm])
                S.op("act", lambda e: e.activation(sm[:, 0:2], sm[:, 0:2], AF.Ln, bias=one1[:, 0:1]), reads=[bsm, bone1], writes=[bsm])
                S.op("dve", lambda e: e.tensor_scalar(out=sm[:, 0:2], in0=sm[:, 0:2], scalar1=-1.0, scalar2=None, op0=ALU.mult), reads=[bsm], writes=[bsm])
                for k in range(4):
                    S.op("dve", lambda e: e.tensor_scalar(out=la_ret[:, k, 0:64], in0=onesF[:, 0:64], scalar1=sm[:, 0:1], scalar2=None, op0=ALU.mult),
                         reads=[bonesF, bsm], writes=[blar])
                    S.op("dve", lambda e: e.tensor_scalar(out=la_ret[:, k, 64:128], in0=onesF[:, 0:64], scalar1=sm[:, 1:2], scalar2=None, op0=ALU.mult),
                         reads=[bonesF, bsm], writes=[blar])
            S.op("act", lambda e: e.activation(gt_sb[:], gt_sb[:], AF.Silu), reads=[bgt], writes=[bgt])
            for g in range(NCH // 4):
                cols = slice(g * 512, (g + 1) * 512)
                if kind == 0:
                    la, bla = la4[g % 2], bla4[g % 2]
                    zps, bz = P[0], bP[0]
                    for k in range(4):
                        n = g * 4 + k
                        S.op("pe", lambda e: e.matmul(zps[:, k * 128:(k + 1) * 128], ga_sb[:, n * 128:(n + 1) * 128], gw_sb[:], start=True, stop=True),
                             reads=[bga, bgw], writes=[bz])
                    lav = la[:].rearrange("p k c -> p (k c)")
                    S.op("act", lambda e: e.activation(lav, zps[:], AF.Exp, scale=-1.0), reads=[bz], writes=[bla])
                    S.op("act", lambda e: e.activation(lav, lav, AF.Ln, bias=one1[:, 0:1]), reads=[bla, bone1], writes=[bla])
                    S.op("dve", lambda e: e.tensor_scalar(out=lav, in0=lav, scalar1=-1.0 / 16.0, scalar2=None, op0=ALU.mult), reads=[bla], writes=[bla])
                else:
                    la, bla = la_ret, blar
                o5 = 4 * (g % 2)
                psA, bA, psB, bB, psC, bC, psKV, bKV = P[1 + o5], bP[1 + o5], P[2 + o5], bP[2 + o5], P[3 + o5], bP[3 + o5], P[4], bP[4]
                for k in range(4):
                    S.op("pe", lambda e: e.matmul(psA[:, k * 128:(k + 1) * 128], la[:, k, :], cm[:, 0, :], start=True, stop=True),
                         reads=[bla, bcm], writes=[bA])
                    S.op("pe", lambda e: e.matmul(psB[:, k * 128:(k + 1) * 128], la[:, k, :], cm[:, 1, :], start=True, stop=True),
                         reads=[bla, bcm], writes=[bB])
                    S.op("pe", lambda e: e.matmul(psC[:, k * 128:k * 128 + 64], cm[:, 2, :], la[:, k, 0:64], start=True, stop=True),
                         reads=[bla, bcm], writes=[bC])
                    S.op("pe", lambda e: e.matmul(psC[:, k * 128 + 64:(k + 1) * 128], cm[:, 3, :], la[:, k, 64:128], start=True, stop=True),
                         reads=[bla, bcm], writes=[bC])
                S.op("act", lambda e: e.activation(Epos[0:64, :], psA[0:64, :], AF.Exp), reads=[bA], writes=[bEpos])
                S.op("act", lambda e: e.activation(Epos[64:128, :], psB[64:128, :], AF.Exp), reads=[bB], writes=[bEpos])
                S.op("act", lambda e: e.activation(Eneg[0:64, :], psA[0:64, :], AF.Exp, scale=-1.0), reads=[bA], writes=[bEneg])
                S.op("act", lambda e: e.activation(Eneg[64:128, :], psB[64:128, :], AF.Exp, scale=-1.0), reads=[bB], writes=[bEneg])
                S.op("act", lambda e: e.activation(Eend[:].rearrange("p k c -> p (k c)"), psC[:], AF.Exp), reads=[bC], writes=[bEend])
                S.op("dve", lambda e: e.tensor_tensor(out=Qf0[0:64, cols], in0=qd[0:64, cols], in1=Epos[0:64, :], op=ALU.mult),
                     reads=[bqd, bEpos], writes=[bQf])
                S.op("dve", lambda e: e.tensor_tensor(out=Qb0[64:128, cols], in0=qd[64:128, cols], in1=Epos[64:128, :], op=ALU.mult),
                     reads=[bqd, bEpos], writes=[bQb])
                S.op("dve", lambda e: e.tensor_tensor(out=Kp[:, cols], in0=kd[:, cols], in1=Eneg[:], op=ALU.mult),
                     reads=[bkd, bEneg], writes=[bKp])
                S.op("dve", lambda e: e.tensor_tensor(out=kend[:, g * 4:(g + 1) * 4, 0:64], in0=ktv[:, g * 4:(g + 1) * 4, 0:64], in1=Eend[:, :, 0:64], op=ALU.mult),
                     reads=[bktv, bEend], writes=[bkend])
                S.op("dve", lambda e: e.tensor_tensor(out=kend[:, g * 4:(g + 1) * 4, 64:128], in0=ktv[:, g * 4:(g + 1) * 4, 0:64], in1=Eend[:, :, 64:128], op=ALU.mult),
                     reads=[bktv, bEend], writes=[bkend])
                Ev = Epos[:].rearrange("p (k c) -> p k c", c=128)
                S.op("dve", lambda e: e.tensor_copy(Dall[0:64, g * 4:(g + 1) * 4], Ev[0:64, :, 127]), reads=[bEpos], writes=[bD])
                S.op("dve", lambda e: e.tensor_copy(Dall[64:128, g * 4:(g + 1) * 4], Ev[64:128, :, 0]), reads=[bEpos], writes=[bD])
                for k in range(4):
                    n = g * 4 + k
                    S.op("pe", lambda e: e.matmul(psKV[:, k * 128:(k + 1) * 128], kend[:, n, :], ktv[:, n, 64:192], start=True, stop=True),
                         reads=[bkend, bktv], writes=[bKV])
                S.op("act", lambda e: e.activation(KVa[:, g * 4:(g + 1) * 4, :].rearrange("p k c -> p (k c)"), psKV[:], AF.Copy),
                     reads=[bKV], writes=[bKVf, bKVb])
            for i in range(1, NCH):
                n = i
                S.op("dve", lambda e: e.scalar_tensor_tensor(out=KVa[0:64, n, :], in0=KVa[0:64, n - 1, :], scalar=Dall[0:64, n:n + 1],
                                                             in1=KVa[0:64, n, :], op0=ALU.mult, op1=ALU.add),
                     reads=[bD, bKVf], writes=[bKVf])
                m = NCH - 1 - i
                S.op("dve", lambda e: e.scalar_tensor_tensor(out=KVa[64:128, m, :], in0=KVa[64:128, m + 1, :], scalar=Dall[64:128, m:m + 1],
                                                             in1=KVa[64:128, m, :], op0=ALU.mult, op1=ALU.add),
                     reads=[bD, bKVb], writes=[bKVb])
            S.op("pool", lambda e: e.memset(Sbf[0:64, 0, :], 0.0), writes=[bSbf])
            S.op("pool", lambda e: e.memset(Sbf[64:128, NCH - 1, :], 0.0), writes=[bSbf])
            S.op("dve", lambda e: e.tensor_copy(Sbf[0:64, 1:NCH, :], KVa[0:64, 0:NCH - 1, :]), reads=[bKVf], writes=[bSbf])
            S.op("dve", lambda e: e.tensor_copy(Sbf[64:128, 0:NCH - 1, :], KVa[64:128, 1:NCH, :]), reads=[bKVb], writes=[bSbf])
            for g in range(NCH // 4):
                psSf, bSf, psSb, bSb = P[0], bP[0], P[1], bP[1]
                psO, bO = (P[2], bP[2]) if g % 2 == 0 else (P[3], bP[3])
                for k in range(4):
                    n = g * 4 + k
                    cs = slice(n * 128, (n + 1) * 128)
                    S.op("pe", lambda e: e.matmul(psSf[:, k * 128:(k + 1) * 128], Kp[:, cs], Qf0[:, cs], start=True, stop=True),
                         reads=[bKp, bQf], writes=[bSf])
                    S.op("pe", lambda e: e.matmul(psSb[:, k * 128:(k + 1) * 128], Kp[:, cs], Qb0[:, cs], start=True, stop=True),
                         reads=[bKp, bQb], writes=[bSb])
                S.op("dve", lambda e: e.tensor_tensor(out=Pf[:], in0=psSf[:], in1=M4[:, 0, :], op=ALU.mult), reads=[bSf, bM4], writes=[bPf])
                S.op("dve", lambda e: e.tensor_tensor(out=Pb[:], in0=psSb[:], in1=M4[:, 1, :], op=ALU.mult), reads=[bSb, bM4], writes=[bPb])
                for k in range(4):
                    n = g * 4 + k
                    cs = slice(n * 128, (n + 1) * 128)
                    ko = slice(k * 128, (k + 1) * 128)
                    S.op("pe", lambda e: e.matmul(psO[:, ko], ktv[:, n, 64:192], Pf[:, ko], start=True, stop=False), reads=[bktv, bPf], writes=[bO])
                    S.op("pe", lambda e: e.matmul(psO[:, ko], ktv[:, n, 64:192], Pb[:, ko], start=False, stop=False), reads=[bktv, bPb], writes=[bO])
                    S.op("pe", lambda e: e.matmul(psO[:, ko], Sbf[:, n, :], Qf0[:, cs], start=False, stop=False), reads=[bSbf, bQf], writes=[bO])
                    S.op("pe", lambda e: e.matmul(psO[:, ko], Sbf[:, n, :], Qb0[:, cs], start=False, stop=True), reads=[bSbf, bQb], writes=[bO])
                gcol = 8 + kind * 4 + hg
                mrow = 1024 + kind * 512 + hg * 128
                finalize(psO[:], [bO], P[4], bP[4], gn_sb[:, gcol:gcol + 1], [bgn], gt_sb[:, g * 512:(g + 1) * 512], [bgt],
                         Md[mrow:mrow + 128, g * 512:(g + 1) * 512])

        S.op("dve", lambda e: e.tensor_tensor(out=tmp64[:, 0:64], in0=lamp_sb[:, 0:64], in1=lamp_sb[:, 64:128], op=ALU.mult), reads=[blamp], writes=[bt64])
        S.op("dve", lambda e: e.tensor_tensor(out=tmp64[:, 64:128], in0=lamp_sb[:, 128:192], in1=lamp_sb[:, 192:256], op=ALU.mult), reads=[blamp], writes=[bt64])
        S.op("dve", lambda e: e.reduce_sum(sm[:, 2:3], tmp64[:, 0:64], axis=mybir.AxisListType.X), reads=[bt64], writes=[bsm])
        S.op("dve", lambda e: e.reduce_sum(sm[:, 3:4], tmp64[:, 64:128], axis=mybir.AxisListType.X), reads=[bt64], writes=[bsm])
        S.op("act", lambda e: e.activation(sm[:, 2:4], sm[:, 2:4], AF.Exp), reads=[bsm], writes=[bsm])
        S.op("dve", lambda e: e.tensor_tensor(out=sm[:, 4:5], in0=sm[:, 3:4], in1=sm[:, 2:3], op=ALU.subtract), reads=[bsm], writes=[bsm])
        S.op("dve", lambda e: e.tensor_tensor(out=sm[:, 4:5], in0=sm[:, 4:5], in1=lc_sb[:, 0:1], op=ALU.subtract), reads=[bsm, blc], writes=[bsm])

        EB = S.sbuf("EB", [128, 6, 512], BF16); bEB = Buf()
        bst = S.sbuf("bst", [128, 512], F32); bbst = Buf()
        q1p = S.sbuf("q1p", [128, T], BF16); q2p = S.sbuf("q2p", [128, T], BF16)
        kT = S.sbuf("kT", [128, T], BF16)
        bq = [Buf() for _ in range(4)]
        S.op("pool", lambda e: e.memset(q1p[64:128, :], 0.0), writes=[bq[0]])
        S.op("pool", lambda e: e.memset(q2p[0:64, :], 0.0), writes=[bq[1]])
        vd = S.sbuf("vd", [128, NCH, 128], BF16); bvd = Buf()
        Et = [S.sbuf("Et%d" % i, [128, 512], BF16) for i in range(2)]; bEt = [Buf(), Buf()]
        E = [S.sbuf("E%d" % i, [128, 512], BF16) for i in range(4)]; bE = [Buf() for _ in range(4)]
        r1 = S.sbuf("r1", [128, 512], F32); br1 = Buf()
        r2 = S.sbuf("r2", [128, 512], F32); br2 = Buf()
        od = S.sbuf("od", [128, 512], F32); bod = Buf()
        steps = [(b, qb, kt_) for b in range(8) for qb in range(T // 512) for kt_ in range(NCH)]
        loaded = [-1]
        pO = [(P[4], bP[4]), (P[5], bP[5])]
        pR = [(P[6], bP[6]), (P[7], bP[7])]
        Eof = {}

        def load_batch(b):
            r0 = PT_DQ + b * 128
            S.dma("sp", q1p[0:64, :], PTd[r0:r0 + 64, :], writes=[bq[0]])
            S.dma("sp", q2p[64:128, :], PTd[r0 + 64:r0 + 128, :], writes=[bq[1]])
            S.dma("sp", kT[:], PTd[PT_DK + b * 128:PT_DK + (b + 1) * 128, :], writes=[bq[2]])
            S.dma("sp", vd[:], PVd[:, PV_DV + b * 128:PV_DV + (b + 1) * 128].rearrange("(n p) c -> p n c", p=128), writes=[bvd])
            S.dma("sp", cb_sb[:], cb[b], writes=[bcb])
            for i in range(6):
                S.dma("sp", bst[:], btile[b, i], writes=[bbst])
                S.op("act", lambda e: e.activation(EB[:, i, :], bst[:], AF.Exp), reads=[bbst], writes=[bEB])
            S.op("dve", lambda e: e.tensor_tensor(out=sm[:, 5:6], in0=gn_sb[:, b:b + 1], in1=lc_sb[:, 1:2], op=ALU.mult), reads=[bgn, blc], writes=[bsm])
            loaded[0] = b

        def scores(i):
            b, qb, kt_ = steps[i]
            if loaded[0] != b:
                load_batch(b)
            qs = slice(qb * 512, (qb + 1) * 512)
            ks = slice(kt_ * 128, (kt_ + 1) * 128)
            d = kt_ * 128 - qb * 512
            for si, qq in enumerate((q1p, q2p)):
                ps, bps = P[(i % 2) * 2 + si], bP[(i % 2) * 2 + si]
                S.op("pe", lambda e: e.matmul(ps[:], kT[:, ks], qq[:, qs], start=True, stop=True),
                     reads=[bq[2], bq[si]], writes=[bps])
                ei = (i % 2) * 2 + si
                Ecur, bEcur = E[ei], bE[ei]
                if d in NEAR_D:
                    et, bet = Et[si], bEt[si]
                    S.op("act", lambda e: e.activation(et[:], ps[:], AF.Exp), reads=[bps], writes=[bet])
                    S.op("dve", lambda e: e.tensor_tensor(out=Ecur[:], in0=et[:], in1=EB[:, NEAR_D.index(d), :], op=ALU.mult),
                         reads=[bet, bEB], writes=[bEcur])
                else:
                    col = 0 if d < 0 else 1
                    S.op("act", lambda e: e.activation(Ecur[:], ps[:], AF.Exp, bias=cb_sb[:, col:col + 1]), reads=[bps, bcb], writes=[bEcur])

        def pv(i):
            b, qb, kt_ = steps[i]
            for si in range(2):
                ei = (i % 2) * 2 + si
                Ecur, bEcur = E[ei], bE[ei]
                S.op("pe", lambda e: e.matmul(pO[si][0][:], vd[:, kt_, :], Ecur[:], start=(kt_ == 0), stop=(kt_ == NCH - 1)),
                     reads=[bvd, bEcur], writes=[pO[si][1]])
                S.op("pe", lambda e: e.matmul(pR[si][0][:], ones[:], Ecur[:], start=(kt_ == 0), stop=(kt_ == NCH - 1)),
                     reads=[bones, bEcur], writes=[pR[si][1]])

        def fin(i):
            b, qb, kt_ = steps[i]
            qs = slice(qb * 512, (qb + 1) * 512)
            S.op("dve", lambda e: e.reciprocal(r1[:], pR[0][0][:]), reads=[pR[0][1]], writes=[br1])
            S.op("dve", lambda e: e.reciprocal(r2[:], pR[1][0][:]), reads=[pR[1][1]], writes=[br2])
            S.op("dve", lambda e: e.tensor_tensor(out=r1[:], in0=pO[0][0][:], in1=r1[:], op=ALU.mult), reads=[pO[0][1], br1], writes=[br1])
            S.op("dve", lambda e: e.tensor_tensor(out=r2[:], in0=pO[1][0][:], in1=r2[:], op=ALU.mult), reads=[pO[1][1], br2], writes=[br2])
            S.op("dve", lambda e: e.scalar_tensor_tensor(out=od[:], in0=r2[:], scalar=sm[:, 4:5], in1=r1[:], op0=ALU.mult, op1=ALU.add),
                 reads=[br1, br2, bsm], writes=[bod])
            finalize(od[:], [bod], pR[0][0], pR[0][1], sm[:, 5:6], [bsm], None, [], Md[b * 128:(b + 1) * 128, qs])

        scores(0)
        for i in range(len(steps)):
            nxt_same = (i + 1 < len(steps)) and steps[i + 1][0] == steps[i][0]
            if nxt_same:
                scores(i + 1)
            pv(i)
            if steps[i][2] == NCH - 1:
                fin(i)
            if (i + 1 < len(steps)) and not nxt_same:
                scores(i + 1)


def _t5_bucket(rel):
    half, max_exact = 16, 8
    ret = np.where(rel > 0, half, 0)
    n = np.abs(rel)
    nf = np.maximum(n, 1).astype(np.float32)
    large = max_exact + (np.log(nf / max_exact) / math.log(128 / max_exact) * (half - max_exact)).astype(np.int32)
    large = np.minimum(large, half - 1)
    return ret + np.where(n < max_exact, n, large)


def _cmask():
    j = np.arange(128)[:, None]
    i = np.arange(128)[None, :]
    m = np.stack([(j <= i), (j >= i), (j > i), (j < i)], axis=1).astype(np.float32)
    return np.ascontiguousarray(m)


def build_F(depth, final):
    nc = bass.Bass("TRN2", target_bir_lowering=False)
    def din(name, shape, dt=F32):
        return nc.dram_tensor(name, shape, dt, kind="ExternalInput").ap()
    x0 = din("x0", [D, T])
    w_ext = din("w_ext", [depth, D, W_EXT])
    wo = din("wo", [depth, D, D])
    wup = din("wup", [depth, D, 2 * DFF])
    wdn = din("wdn", [depth, DFF, D])
    g1 = din("g1", [depth, 128, 16])
    g2 = din("g2", [depth, 128, 16])
    gF = din("gF", [128, 16])
    cw = din("cw", [depth, 128, 88, 4])
    rotf = din("rotf", [128, 2, T])
    rott = din("rott", [128, 2, T // 128, 256])
    btile = din("btile", [8, 6, 128, 512])
    cb = din("cb", [8, 128, 2])
    lamp = din("lamp", [depth, 128, 256])
    lcst = din("lcst", [depth, 128, 2])
    gains = din("gains", [depth, 128, 16])
    gw = din("gw", [depth, 4, 33, 128])
    rdec = din("rdec", [depth, 4, 128, 2])
    cmask = din("cmask", [128, 4, 128])
    y = nc.dram_tensor("y", [D, T], F32, kind="ExternalOutput").ap()
    Xs = [nc.dram_tensor("Xa", [D, T], F32).ap(), nc.dram_tensor("Xb", [D, T], F32).ap()]
    PTd = nc.dram_tensor("PTd", [NFM, T], BF16).ap()
    PGd = nc.dram_tensor("PGd", [1024, T], F32).ap()
    PVd = nc.dram_tensor("PVd", [T, NTM], BF16).ap()
    Md = nc.dram_tensor("Md", [D, T], BF16).ap()
    with contextlib.ExitStack() as st:
        S = Sched(nc, st)
        P = [S.psum("P%d" % i, [128, 512]) for i in range(8)]
        bP = [Buf() for _ in range(8)]
        for L in range(depth):
            Xin = x0 if L == 0 else Xs[(L - 1) % 2]
            Xout = y if L == depth - 1 else Xs[L % 2]
            for q in range(4):
                qs = slice(q * NT, (q + 1) * NT)
                emit_A(S, P, bP, Xin[:, qs], g1[L], w_ext[L], rotf[:, :, qs], rott[:, :, q * 8:(q + 1) * 8, :],
                       PTd[:, qs], PGd[:, qs], PVd[qs, :])
            emit_B(S, P, bP, PTd, PGd, PVd, Md, btile, cb, lamp[L], lcst[L], gains[L], gw[L], rdec[L], cmask)
            for q in range(4):
                qs = slice(q * NT, (q + 1) * NT)
                emit_C(S, P, bP, Xin, Md, q, wo[L], g2[L], wup[L], cw[L], wdn[L], gF, Xout[:, qs], final and L == depth - 1)
        S.finish()
        build_F.n_inst = S.n_inst
    return nc


def prep_F(p, depth):
    cos, sin = _rot_tables()
    r = np.arange(128)
    sgn_r = np.where(r % 64 < 32, -1.0, 1.0).astype(np.float32)
    rotf = np.empty((128, 2, T), np.float32)
    rotf[:, 0, :] = cos[:, r % 32].T
    rotf[:, 1, :] = (sin[:, r % 32] * sgn_r[None, :]).T
    col = np.arange(256)
    sgn_c = np.where(col % 64 < 32, -1.0, 1.0).astype(np.float32)
    ct = cos[:, col % 32].reshape(T // 128, 128, 256).transpose(1, 0, 2)
    stb = (sin[:, col % 32] * sgn_c[None, :]).reshape(T // 128, 128, 256).transpose(1, 0, 2)
    rott = np.ascontiguousarray(np.stack([ct, stb], axis=1))
    bucket = _t5_bucket(np.arange(-(T - 1), T))
    kl = np.arange(128)[:, None]
    ql = np.arange(512)[None, :]
    btile = np.stack([np.stack([p['rel_bias'][bucket[d + kl - ql + T - 1], h] for d in NEAR_D]) for h in range(8)]).astype(np.float32)
    cb = np.empty((8, 128, 2), np.float32)
    cb[:, :, 0] = p['rel_bias'][15, :][:, None]
    cb[:, :, 1] = p['rel_bias'][31, :][:, None]
    lamp = np.ascontiguousarray(np.broadcast_to(p['diff_lambda'][:depth].reshape(depth, 1, 256), (depth, 128, 256))).astype(np.float32)
    lcst = np.empty((depth, 128, 2), np.float32)
    for L in range(depth):
        lam_init = 0.8 - 0.6 * math.exp(-0.3 * L)
        lcst[L, :, 0] = lam_init
        lcst[L, :, 1] = 1.0 - lam_init
    gains = np.ascontiguousarray(p['head_gain'][:depth].reshape(depth, 16, 128).transpose(0, 2, 1)).astype(np.float32)
    gw = np.zeros((depth, 4, 33, 128), np.float32)
    rdec = np.empty((depth, 4, 128, 2), np.float32)
    for L in range(depth):
        for hg in range(4):
            gw[L, hg, 0:16, 0:64] = p['gla_gate_w'][L][0][:, hg * 64:(hg + 1) * 64]
            gw[L, hg, 16:32, 64:128] = p['gla_gate_w'][L][1][:, hg * 64:(hg + 1) * 64]
            gw[L, hg, 32, 0:64] = p['gla_gate_b'][L][0][hg * 64:(hg + 1) * 64]
            gw[L, hg, 32, 64:128] = p['gla_gate_b'][L][1][hg * 64:(hg + 1) * 64]
            rdec[L, hg, :, 0] = p['ret_decay_logit'][L][0, hg]
            rdec[L, hg, :, 1] = p['ret_decay_logit'][L][1, hg]
    cw = np.empty((depth, 128, 88, 4), np.float32)
    for L in range(depth):
        for i in range(3):
            cw[L, :, :, i] = p['conv_w'][L][i].reshape(88, 128).T
        cw[L, :, :, 3] = p['conv_b'][L].reshape(88, 128).T
    shared = {
        "w_ext": np.ascontiguousarray(np.stack([_w_ext(p['w_in'][L]) for L in range(depth)])),
        "wo": np.ascontiguousarray(p['w_o'][:depth]), "wup": np.ascontiguousarray(p['w_up'][:depth]),
        "wdn": np.ascontiguousarray(p['w_down'][:depth]),
        "g1": np.stack([_vec16(p['ln1_g'][L]) for L in range(depth)]), "g2": np.stack([_vec16(p['ln2_g'][L]) for L in range(depth)]),
        "gF": _vec16(p['final_g']), "cw": cw, "rotf": rotf, "rott": rott, "btile": btile, "cb": cb, "lamp": lamp, "lcst": lcst,
        "gains": gains, "gw": gw, "rdec": rdec, "cmask": _cmask(),
    }
    x = p['x'].astype(np.float32)
    xT = [np.ascontiguousarray(x[b].T) for b in range(NB)]
    return [dict(shared, x0=xT[c // 4]) for c in range(NCORE)]


_PROG = {}


def run_F(p, depth, final):
    key = (depth, final)
    if key not in _PROG:
        _PROG[key] = build_F(depth, final)
    res = run_bass_kernel_spmd(_PROG[key], prep_F(p, depth), core_ids=list(range(NCORE)))
    return np.stack([np.asarray(res.results[4 * b]["y"]).T for b in range(NB)]).astype(np.float32)


def kernel(**p):
    p = {k: np.asarray(v) for k, v in p.items()}
    return np.ascontiguousarray(run_F(p, DEPTH, True))
```
